# Optimizing a Trainium2 kernel written in Bass

```python
import jax, jax.numpy as jnp
from jax import lax
import numpy as np

D_MODEL = 2048
BATCH = 4
SEQ = 8192
DEPTH = 4
DEC_BATCH = 16
DEC_SEQ = 32
PAST_LEN = 4096

CHUNK = 64
N_MIXERS = 2
N_A = (DEPTH + 1) // 2
N_B = DEPTH // 2
ROPE_THETA = 500000.0
EPS = 1e-6
NEG = -1e30
A_HEADS = 16
Q_LORA = 512
KV_LORA = 512
A_NOPE = 128
A_ROPE = 64
A_V = 128
A_QBLOCK = 128
A_SCALE = (A_NOPE + A_ROPE) ** -0.5
B_HEADS = 32
B_KV = 8
B_HD = 64
B_ROT = B_HD // 4
WINDOW = 128
WIN_CHUNKS = WINDOW // CHUNK
B_SCALE = B_HD ** -0.5
D_FF = 5632
CONV_W = 3

kernel_name = 'streaming_mla_swa_convffn_step'


def rmsnorm(x, g):
    xf = x.astype(jnp.float32)
    y = xf * lax.rsqrt(jnp.mean(xf * xf, axis=-1, keepdims=True) + EPS)
    return (y * g.astype(jnp.float32)).astype(x.dtype)


def rope(x, pos, rot):
    half = rot // 2
    inv = ROPE_THETA ** (-jnp.arange(half, dtype=jnp.float32) / half)
    ang = pos.astype(jnp.float32)[:, None] * inv[None, :]
    cos = jnp.cos(ang)[:, None, :]
    sin = jnp.sin(ang)[:, None, :]
    xf = x.astype(jnp.float32)
    x1 = xf[..., :half]
    x2 = xf[..., half:rot]
    out = jnp.concatenate([x1 * cos - x2 * sin, x1 * sin + x2 * cos, xf[..., rot:]], axis=-1)
    return out.astype(x.dtype)


def mla_project(h, pos, w_in, g_q, w_qb, g_kv):
    b, s, _ = h.shape
    a = h @ w_in
    cq = rmsnorm(a[..., :Q_LORA], g_q)
    ckv = rmsnorm(a[..., Q_LORA:Q_LORA + KV_LORA], g_kv)
    kpe = rope(a[..., Q_LORA + KV_LORA:][:, :, None, :], pos, A_ROPE)[:, :, 0, :]
    q = (cq @ w_qb).reshape(b, s, A_HEADS, A_NOPE + A_ROPE)
    q_nope = q[..., :A_NOPE]
    q_pe = rope(q[..., A_NOPE:], pos, A_ROPE)
    return q_nope, q_pe, ckv, kpe


def mla_attend(q_nope, q_pe, q_pos, ckv, kpe, k_pos, w_uk, w_uv):
    q_lat = jnp.einsum('bqhn,rhn->bqhr', q_nope, w_uk)
    s = jnp.einsum('bqhr,bkr->bhqk', q_lat, ckv) + jnp.einsum('bqhe,bke->bhqk', q_pe, kpe)
    s = s.astype(jnp.float32) * A_SCALE
    visible = (k_pos // CHUNK)[None, :] <= (q_pos // CHUNK)[:, None]
    s = jnp.where(visible[None, None], s, NEG)
    p = jax.nn.softmax(s, axis=-1).astype(ckv.dtype)
    o_lat = jnp.einsum('bhqk,bkr->bqhr', p, ckv)
    o = jnp.einsum('bqhr,rhv->bqhv', o_lat, w_uv)
    return o.reshape(o.shape[0], o.shape[1], A_HEADS * A_V)


def mla_mixer(h, pos, past_ckv, past_kpe, w_in, g_q, w_qb, g_kv, w_uk, w_uv, w_o):
    b, s, _ = h.shape
    q_nope, q_pe, ckv, kpe = mla_project(h, pos, w_in, g_q, w_qb, g_kv)
    if past_ckv is None:
        nb = s // A_QBLOCK
        qn_b = q_nope.reshape(b, nb, A_QBLOCK, A_HEADS, A_NOPE).transpose(1, 0, 2, 3, 4)
        qp_b = q_pe.reshape(b, nb, A_QBLOCK, A_HEADS, A_ROPE).transpose(1, 0, 2, 3, 4)
        pos_b = pos.reshape(nb, A_QBLOCK)

        def block(args):
            qn, qp, qpos = args
            return mla_attend(qn, qp, qpos, ckv, kpe, pos, w_uk, w_uv)

        o = lax.map(block, (qn_b, qp_b, pos_b))
        o = o.transpose(1, 0, 2, 3).reshape(b, s, A_HEADS * A_V)
    else:
        past = past_ckv.shape[1]
        k_ckv = jnp.concatenate([past_ckv, ckv], axis=1)
        k_kpe = jnp.concatenate([past_kpe, kpe], axis=1)
        k_pos = jnp.arange(past + s, dtype=jnp.int32)
        o = mla_attend(q_nope, q_pe, pos, k_ckv, k_kpe, k_pos, w_uk, w_uv)
    return o @ w_o, ckv, kpe


def sink_softmax(s, sinks):
    sk = sinks[:, :, None, None]
    m = jnp.maximum(jnp.max(s, axis=-1, keepdims=True), sk)
    e = jnp.exp(s - m)
    return e / (jnp.sum(e, axis=-1, keepdims=True) + jnp.exp(sk - m))


def swa_mixer(h, pos, past_k, past_v, w_qkv, sinks, w_o):
    b, s, _ = h.shape
    g = B_HEADS // B_KV
    qkv = h @ w_qkv
    q = qkv[..., :B_HEADS * B_HD].reshape(b, s, B_HEADS, B_HD)
    k = qkv[..., B_HEADS * B_HD:(B_HEADS + B_KV) * B_HD].reshape(b, s, B_KV, B_HD)
    v = qkv[..., (B_HEADS + B_KV) * B_HD:].reshape(b, s, B_KV, B_HD)
    q = rope(q, pos, B_ROT)
    k = rope(k, pos, B_ROT)
    snk = sinks.astype(jnp.float32).reshape(B_KV, g)
    if past_k is None:
        nc = s // CHUNK
        pad = WIN_CHUNKS * CHUNK
        kp = jnp.pad(k, ((0, 0), (pad, 0), (0, 0), (0, 0))).reshape(b, nc + WIN_CHUNKS, CHUNK, B_KV, B_HD)
        vp = jnp.pad(v, ((0, 0), (pad, 0), (0, 0), (0, 0))).reshape(b, nc + WIN_CHUNKS, CHUNK, B_KV, B_HD)
        band_k = jnp.concatenate([kp[:, j:j + nc] for j in range(WIN_CHUNKS + 1)], axis=2)
        band_v = jnp.concatenate([vp[:, j:j + nc] for j in range(WIN_CHUNKS + 1)], axis=2)
        key_chunk = (jnp.arange(nc)[:, None]
                     + (jnp.arange((WIN_CHUNKS + 1) * CHUNK) // CHUNK)[None, :] - WIN_CHUNKS)
        visible = key_chunk >= 0
        qc = q.reshape(b, nc, CHUNK, B_KV, g, B_HD)
        sc = jnp.einsum('bcqkgd,bcskd->bckgqs', qc, band_k).astype(jnp.float32) * B_SCALE
        sc = jnp.where(visible[None, :, None, None, None, :], sc, NEG)
        p = sink_softmax(sc, snk).astype(v.dtype)
        o = jnp.einsum('bckgqs,bcskd->bcqkgd', p, band_v).reshape(b, s, B_HEADS * B_HD)
        kw = min(WINDOW, s)
        new_k = k[:, s - kw:]
        new_v = v[:, s - kw:]
    else:
        w = past_k.shape[1]
        k_all = jnp.concatenate([past_k, k], axis=1)
        v_all = jnp.concatenate([past_v, v], axis=1)
        k_pos = pos[0] - w + jnp.arange(w + s, dtype=jnp.int32)
        qch = pos // CHUNK
        kch = k_pos // CHUNK
        visible = (kch[None, :] <= qch[:, None]) & (kch[None, :] >= qch[:, None] - WIN_CHUNKS)
        qg = q.reshape(b, s, B_KV, g, B_HD)
        sc = jnp.einsum('bqkgd,bskd->bkgqs', qg, k_all).astype(jnp.float32) * B_SCALE
        sc = jnp.where(visible[None, None, None], sc, NEG)
        p = sink_softmax(sc, snk).astype(v.dtype)
        o = jnp.einsum('bkgqs,bskd->bqkgd', p, v_all).reshape(b, s, B_HEADS * B_HD)
        new_k = k_all[:, -w:]
        new_v = v_all[:, -w:]
    return o @ w_o, new_k, new_v


def conv_ffn(h, prev, w_in, conv_w, conv_b, w_down):
    b, s, _ = h.shape
    gu = h @ w_in
    gate = gu[..., :D_FF]
    up = gu[..., D_FF:]
    if prev is None:
        prev = jnp.zeros((b, CONV_W - 1, D_FF), gate.dtype)
    gp = jnp.concatenate([prev, gate], axis=1)
    conv = conv_b + conv_w[0] * gp[:, 0:s]
    for j in range(1, CONV_W):
        conv = conv + conv_w[j] * gp[:, j:j + s]
    y = (jax.nn.silu(conv) * up) @ w_down
    return y, gp[:, -(CONV_W - 1):]


def setup_inputs(seed: int = 0) -> dict:
    key = jax.random.key(seed)
    ks = jax.random.split(key, 24)
    f32 = jnp.float32

    def nrm(k, shape, fan_in):
        return jax.random.normal(k, shape, f32) * fan_in ** -0.5

    w_c = min(WINDOW, PAST_LEN)
    return {
        'x_prompt': jax.random.normal(ks[0], (BATCH, SEQ, D_MODEL), f32),
        'x_sample': jax.random.normal(ks[1], (DEC_BATCH, DEC_SEQ, D_MODEL), f32),
        'cache_ckv': jax.random.normal(ks[2], (N_A, DEC_BATCH, PAST_LEN, KV_LORA), f32),
        'cache_kpe': jax.random.normal(ks[3], (N_A, DEC_BATCH, PAST_LEN, A_ROPE), f32),
        'cache_win_k': jax.random.normal(ks[4], (N_B, DEC_BATCH, w_c, B_KV, B_HD), f32),
        'cache_win_v': jax.random.normal(ks[5], (N_B, DEC_BATCH, w_c, B_KV, B_HD), f32),
        'state_ffn_conv': jax.random.normal(ks[6], (DEPTH, DEC_BATCH, CONV_W - 1, D_FF), f32),
        'norm_g': 1.0 + 0.05 * jax.random.normal(ks[7], (DEPTH, 4, D_MODEL), f32),
        'mla_w_in': nrm(ks[8], (N_A, D_MODEL, Q_LORA + KV_LORA + A_ROPE), D_MODEL),
        'mla_g_q': 1.0 + 0.05 * jax.random.normal(ks[9], (N_A, Q_LORA), f32),
        'mla_w_qb': nrm(ks[10], (N_A, Q_LORA, A_HEADS * (A_NOPE + A_ROPE)), Q_LORA),
        'mla_g_kv': 1.0 + 0.05 * jax.random.normal(ks[11], (N_A, KV_LORA), f32),
        'mla_w_uk': nrm(ks[12], (N_A, KV_LORA, A_HEADS, A_NOPE), KV_LORA),
        'mla_w_uv': nrm(ks[13], (N_A, KV_LORA, A_HEADS, A_V), KV_LORA),
        'mla_w_o': nrm(ks[14], (N_A, A_HEADS * A_V, D_MODEL), A_HEADS * A_V),
        'swa_w_qkv': nrm(ks[15], (N_B, D_MODEL, (B_HEADS + 2 * B_KV) * B_HD), D_MODEL),
        'swa_sinks': 0.5 * jax.random.normal(ks[16], (N_B, B_HEADS), f32),
        'swa_w_o': nrm(ks[17], (N_B, B_HEADS * B_HD, D_MODEL), B_HEADS * B_HD),
        'ffn_w_in': nrm(ks[18], (DEPTH, D_MODEL, 2 * D_FF), D_MODEL),
        'ffn_conv_w': nrm(ks[19], (DEPTH, CONV_W, D_FF), CONV_W),
        'ffn_conv_b': 0.02 * jax.random.normal(ks[20], (DEPTH, D_FF), f32),
        'ffn_w_down': nrm(ks[21], (DEPTH, D_FF, D_MODEL), D_FF),
    }


def reference(x_prompt, x_sample, cache_ckv, cache_kpe, cache_win_k, cache_win_v, state_ffn_conv,
              norm_g, mla_w_in, mla_g_q, mla_w_qb, mla_g_kv, mla_w_uk, mla_w_uv, mla_w_o,
              swa_w_qkv, swa_sinks, swa_w_o, ffn_w_in, ffn_conv_w, ffn_conv_b, ffn_w_down):
    xp = x_prompt
    xs = x_sample
    past_len = cache_ckv.shape[2]
    pos_p = jnp.arange(xp.shape[1], dtype=jnp.int32)
    pos_s = past_len + jnp.arange(xs.shape[1], dtype=jnp.int32)
    ckv_p, kpe_p, wk_p, wv_p, fc_p = [], [], [], [], []
    ckv_s, kpe_s, wk_s, wv_s, fc_s = [], [], [], [], []
    for i in range(DEPTH):
        g = norm_g[i]
        if i % N_MIXERS == 0:
            a = i // N_MIXERS
            wts = (mla_w_in[a], mla_g_q[a], mla_w_qb[a], mla_g_kv[a], mla_w_uk[a], mla_w_uv[a], mla_w_o[a])
            mp, c_p, k_p = mla_mixer(rmsnorm(xp, g[0]), pos_p, None, None, *wts)
            ms, c_s, k_s = mla_mixer(rmsnorm(xs, g[0]), pos_s, cache_ckv[a], cache_kpe[a], *wts)
            ckv_p.append(c_p)
            kpe_p.append(k_p)
            ckv_s.append(c_s)
            kpe_s.append(k_s)
        else:
            j = i // N_MIXERS
            wts = (swa_w_qkv[j], swa_sinks[j], swa_w_o[j])
            mp, nk_p, nv_p = swa_mixer(rmsnorm(xp, g[0]), pos_p, None, None, *wts)
            ms, nk_s, nv_s = swa_mixer(rmsnorm(xs, g[0]), pos_s, cache_win_k[j], cache_win_v[j], *wts)
            wk_p.append(nk_p)
            wv_p.append(nv_p)
            wk_s.append(nk_s)
            wv_s.append(nv_s)
        xp = xp + rmsnorm(mp, g[1])
        xs = xs + rmsnorm(ms, g[1])
        fw = (ffn_w_in[i], ffn_conv_w[i], ffn_conv_b[i], ffn_w_down[i])
        fp, cp = conv_ffn(rmsnorm(xp, g[2]), None, *fw)
        fs, cs = conv_ffn(rmsnorm(xs, g[2]), state_ffn_conv[i], *fw)
        fc_p.append(cp)
        fc_s.append(cs)
        xp = xp + rmsnorm(fp, g[3])
        xs = xs + rmsnorm(fs, g[3])
    return (xp, xs,
            jnp.stack(ckv_p), jnp.stack(kpe_p), jnp.stack(wk_p), jnp.stack(wv_p), jnp.stack(fc_p),
            jnp.stack(ckv_s), jnp.stack(kpe_s), jnp.stack(wk_s), jnp.stack(wv_s), jnp.stack(fc_s))
```

```python
import numpy as np
import concourse.bass as bass
import concourse.mybir as mybir
from concourse.bass_utils import run_bass_kernel_spmd

F32, BF16 = mybir.dt.float32, mybir.dt.bfloat16
AF = mybir.ActivationFunctionType
ALU = mybir.AluOpType

D = 2048
DEPTH = 4
CHUNK = 64
THETA = 500000.0
EPS = 1e-6
AH, QL, KVL, NOPE, ROPE, AV = 16, 512, 512, 128, 64, 128
A_SCALE = (NOPE + ROPE) ** -0.5
BH, BKV, BHD, BROT = 32, 8, 64, 16
B_SCALE = BHD ** -0.5
DFF = 5632
NFC = DFF // 128
T = 512
NCORES = 8


class StopBuild(Exception):
    pass


class Prog:
    def __init__(s, nc):
        s.nc = nc
        s.eng = {'pe': nc.tensor, 'act': nc.scalar, 'dve': nc.vector, 'pool': nc.gpsimd, 'sp': nc.sync}
        s.ops = {k: [] for k in s.eng}
        s.cnt = {k: 0 for k in ('pe', 'act', 'dve', 'pool')}
        s.esem = {k: nc.alloc_semaphore('e_' + k) for k in s.cnt}
        s.seen = {k: {} for k in s.eng}
        s.lastw = {}
        s.readers = {}
        s.dsem = {}
        s.off = 16512
        s.nps = 0

    def sb(s, name, shape, dt, at=None):
        nbytes = int(np.prod(shape[1:])) * (4 if dt == F32 else 2)
        nbytes = (nbytes + 63) // 64 * 64
        if at is None:
            at = s.off
            s.off += nbytes
            assert s.off <= (16512 + 212000), (name, s.off)
        return s.nc.alloc_sbuf_tensor_at(name, list(shape), dt, offset=at)

    def ps(s, name, shape, dt):
        return s.nc.alloc_psum_tensor(name, list(shape), dt)

    def _deps(s, e, reads, writes):
        need = {}

        def add(ev):
            if ev is None:
                return
            sem, val, src = ev
            if src == e and e == 'pe':
                return
            if need.get(sem.name, (None, 0))[1] < val:
                need[sem.name] = (sem, val)
        for k in reads:
            add(s.lastw.get(k))
        for k in writes:
            add(s.lastw.get(k))
            for ev in s.readers.get(k, {}).values():
                add(ev)
        waits = []
        for name, (sem, val) in need.items():
            if s.seen[e].get(name, 0) < val:
                s.seen[e][name] = val
                waits.append((sem, val))
        return waits

    def _commit(s, ev, reads, writes):
        for k in reads:
            s.readers.setdefault(k, {})[ev[0].name] = ev
        for k in writes:
            s.lastw[k] = ev
            s.readers[k] = {}

    def op(s, e, fn, reads=(), writes=()):
        waits = s._deps(e, reads, writes)
        s.cnt[e] += 1
        ev = (s.esem[e], s.cnt[e], e)
        s.ops[e].append((waits, fn, (s.esem[e], 1)))
        s._commit(ev, reads, writes)

    def dma(s, q, out, in_, slot, reads=(), writes=(), slow=False, throttle=3):
        waits = s._deps(q, reads, writes)
        if slot not in s.dsem:
            s.dsem[slot] = [s.nc.alloc_semaphore('d_' + slot), 0]
        rec = s.dsem[slot]
        if throttle and rec[1] - 16 * throttle > 0 and s.seen[q].get(rec[0].name, 0) < rec[1] - 16 * throttle:
            s.seen[q][rec[0].name] = rec[1] - 16 * throttle
            waits.append((rec[0], rec[1] - 16 * throttle))
        rec[1] += 16
        ev = (rec[0], rec[1], 'dma')
        if slow:
            fn = lambda en: en.dma_start(out=out, in_=in_, allow_slow_non_contiguous=True)
        else:
            fn = lambda en: en.dma_start(out=out, in_=in_)
        s.ops[q].append((waits, fn, (rec[0], 16)))
        s._commit(ev, reads, writes)

    def fence(s, old_keys, new_keys):
        evs = {}
        for k in old_keys:
            ev = s.lastw.get(k)
            if ev is not None:
                evs[ev[0].name] = max(evs.get(ev[0].name, ev), ev, key=lambda x: x[1])
            for ev in s.readers.get(k, {}).values():
                evs[ev[0].name] = max(evs.get(ev[0].name, ev), ev, key=lambda x: x[1])
        for k in new_keys:
            r = s.readers.setdefault(k, {})
            for n, ev in evs.items():
                if n not in r or r[n][1] < ev[1]:
                    r[n] = ev

    def emit(s):
        nc = s.nc
        fin = []
        for k in s.cnt:
            if s.cnt[k]:
                fin.append((s.esem[k], s.cnt[k]))
        for slot, (sem, val) in s.dsem.items():
            fin.append((sem, val))
        with nc.Block() as block:
            for name, method in (('sp', block.sync), ('act', block.scalar), ('dve', block.vector),
                                 ('pool', block.gpsimd), ('pe', block.tensor)):
                def f(en, name=name):
                    for waits, fn, inc in s.ops[name]:
                        for sem, val in waits:
                            en.wait_ge(sem, val)
                        ins = fn(en)
                        if inc is not None:
                            ins.then_inc(inc[0], inc[1])
                    if name == 'sp':
                        for sem, val in fin:
                            en.wait_ge(sem, val)
                method(f)


def build(SEQ, PAST, STAGE=99):
    NT = SEQ // T
    NTS = NT * 4 + 1
    nc = bass.Bass("TRN2", target_bir_lowering=False)
    P = Prog(nc)

    def din(name, shape, dt=F32):
        return nc.dram_tensor(name, list(shape), dt, kind="ExternalInput").ap()

    def dout(name, shape):
        return nc.dram_tensor(name, list(shape), F32, kind="ExternalOutput").ap()

    def dscr(name, shape, dt=BF16):
        return nc.dram_tensor(name, list(shape), dt, kind="Internal").ap()

    xp = din("xp", [SEQ, D]); xs = din("xs", [128, D])
    cckv = din("cckv", [2, 4, PAST, KVL]); ckpe = din("ckpe", [2, 4, PAST, ROPE])
    cwk = din("cwk", [2, 4, 128, 512]); cwv = din("cwv", [2, 4, 128, 512])
    sfc = din("sfc", [4, 4, 2, DFF])
    norm_g = din("norm_g", [16, D])
    w_in_a = din("w_in_a", [2, D, 1088]); g_q = din("g_q", [2, QL]); w_qb = din("w_qb", [2, QL, 3072])
    g_kv = din("g_kv", [2, KVL]); w_uk = din("w_uk", [2, KVL, 2048]); w_uv = din("w_uv", [2, KVL, 2048])
    w_o_a = din("w_o_a", [2, D, D]); w_qkv = din("w_qkv", [2, D, 3072]); sinks = din("sinks", [2, BH])
    w_o_b = din("w_o_b", [2, D, D]); f_w_in = din("f_w_in", [4, D, 2 * DFF]); f_cw = din("f_cw", [4, 3, DFF])
    f_cb = din("f_cb", [4, DFF]); f_wd = din("f_wd", [4, DFF, D])
    ident_d = din("ident", [128, 128])
    tA_c = din("tA_c", [128, NTS, 32]); tA_s = din("tA_s", [128, NTS, 32])
    tB_c = din("tB_c", [128, NTS, 8]); tB_s = din("tB_s", [128, NTS, 8])
    tF_c = din("tF_c", [64, SEQ + 128]); tF_s = din("tF_s", [64, SEQ + 128])

    yp = dout("yp", [SEQ, D]); ys = dout("ys", [128, D])
    o_ckv_p = dout("o_ckv_p", [2, SEQ, KVL]); o_kpe_p = dout("o_kpe_p", [2, SEQ, ROPE])
    o_wk_p = dout("o_wk_p", [2, 128, 512]); o_wv_p = dout("o_wv_p", [2, 128, 512])
    o_fc_p = dout("o_fc_p", [4, 2, DFF])
    o_ckv_s = dout("o_ckv_s", [2, 128, KVL]); o_kpe_s = dout("o_kpe_s", [2, 128, ROPE])
    o_wk_s = dout("o_wk_s", [2, 4, 128, 512]); o_wv_s = dout("o_wv_s", [2, 4, 128, 512])
    o_fc_s = dout("o_fc_s", [4, 4, 2, DFF])

    s_w_in_a = dscr("s_w_in_a", [2, D, 1088]); s_w_qb = dscr("s_w_qb", [2, QL, AH, 256])
    s_w_uk = dscr("s_w_uk", [2, KVL, 2048]); s_w_uv = dscr("s_w_uv", [2, KVL, 2048])
    s_w_o_a = dscr("s_w_o_a", [2, D, D]); s_w_qkv = dscr("s_w_qkv", [2, D, 3072])
    s_w_o_b = dscr("s_w_o_b", [2, D, D]); s_f_w_in = dscr("s_f_w_in", [4, D, 2 * DFF])
    s_f_wd = dscr("s_f_wd", [4, DFF, D])
    KL = SEQ
    KLS = PAST + 32
    kt_p = dscr("kt_p", [2, AH, 128, KL]); v_p = dscr("v_p", [2, AH, KL, 128]); kpe_p = dscr("kpe_p", [2, 64, KL])
    kt_s = dscr("kt_s", [2, 4, AH, 128, KLS]); v_s = dscr("v_s", [2, 4, AH, KLS, 128]); kpe_s = dscr("kpe_s", [2, 4, 64, KLS])

    ident = P.sb("ident", [128, 128], BF16)
    ones = P.sb("ones", [128, 128], BF16)
    cw_sb = P.sb("cw_sb", [128, 4, 3, NFC], F32)
    cb_sb = P.sb("cb_sb", [128, 4, NFC], F32)
    gq_bc = P.sb("gq_bc", [128, 2, 512], F32)
    gkv_bc = P.sb("gkv_bc", [128, 2, 512], F32)
    esink = P.sb("esink", [64, 2, BH], F32)
    tabAc = P.sb("tabAc", [128, 4, 32], F32); tabAs = P.sb("tabAs", [128, 4, 32], F32)
    tabBc = P.sb("tabBc", [128, 4, 8], F32); tabBs = P.sb("tabBs", [128, 4, 8], F32)
    tabFc = P.sb("tabFc", [64, 512], F32); tabFs = P.sb("tabFs", [64, 512], F32)
    cst_p = P.sb("cst_p", [128, 4, NFC, 2], F32)
    cst_s = P.sb("cst_s", [128, 4, 4, NFC, 2], F32)
    kTprev = P.sb("kTprev", [64, 2, 8, 128], BF16)
    vprev = P.sb("vprev", [128, 2, 512], BF16)
    stat = P.sb("stat", [128, 16], F32)
    epsb = P.sb("epsb", [128, 1], F32)
    x_sb = P.sb("x_sb", [128, 4, D], F32)
    Y_AT = P.off
    y_sb = P.sb("y_sb", [128, 4, D], F32)
    kTpast = P.sb("kTpast", [64, 8, 128], BF16, at=Y_AT + 8192)
    vpast = P.sb("vpast", [128, 512], BF16, at=Y_AT + 8192 + 2048)
    cw_f = P.sb("cw_f", [128, 512], F32, at=Y_AT + 8192 + 2048 + 1024)
    hT = P.sb("hT", [128, 16, T], BF16)
    h_bf = P.sb("h_bf", [128, D], BF16)
    gbc = P.sb("gbc", [128, D], F32)
    NWB = 2
    wring = [P.sb("wring%d" % i, [128, 16 * 512], BF16) for i in range(NWB)]
    R0 = P.off
    RSIZE = (16512 + 212000) - R0
    print("SBUF persistent bytes", R0, "region", RSIZE)

    tp = [P.ps("tp%d" % i, [128, 1024], BF16) for i in range(2)]
    pf = [P.ps("pf%d" % i, [128, 512], F32) for i in range(6)]
    tpi = [0]

    def next_tp():
        tpi[0] ^= 1
        return tp[tpi[0]], ('tp', tpi[0])
    pfi = [0]

    def next_pf(lo=0, hi=6):
        i = lo + (pfi[0] % (hi - lo))
        pfi[0] += 1
        return pf[i], ('pf', i)

    wstate = {'n': 0}

    def wpanel(src3, kc, ncols):
        i = wstate['n'] % NWB
        wstate['n'] += 1
        view = wring[i][:, 0:kc * ncols].rearrange("p (k n) -> p k n", n=ncols)
        P.dma('sp', view, src3, 'w%d' % i, writes=[('w', i)])
        return view, ('w', i)

    def wsrc(mat, r0, kc, c0, ncols):
        return mat[r0:r0 + kc * 128, c0:c0 + ncols].rearrange("(k p) n -> p k n", p=128)

    def cast_rows(dst, src, rows, step):
        for r in range(0, rows, step):
            r1 = min(rows, r + step)
            P.dma('pool', dst[r:r1], src[r:r1], 'cast', writes=['wscr'])

    for l in range(2):
        cast_rows(s_w_in_a[l], w_in_a[l], D, 1024)
        wq = w_qb[l].rearrange("r (h c) -> r h c", c=192)
        for r in range(0, QL, 128):
            P.dma('pool', s_w_qb[l][r:r + 128, :, 0:192], wq[r:r + 128], 'cast', writes=['wscr'])
            P.dma('pool', s_w_qb[l][r:r + 128, :, 192:224], wq[r:r + 128, :, 160:192], 'cast', writes=['wscr'])
            P.dma('pool', s_w_qb[l][r:r + 128, :, 224:256], wq[r:r + 128, :, 128:160], 'cast', writes=['wscr'])
        cast_rows(s_w_uk[l], w_uk[l], KVL, 512)
        cast_rows(s_w_uv[l], w_uv[l], KVL, 512)
        cast_rows(s_w_o_a[l], w_o_a[l], D, 1024)
        cast_rows(s_w_qkv[l], w_qkv[l], D, 512)
        cast_rows(s_w_o_b[l], w_o_b[l], D, 1024)
    for l in range(4):
        cast_rows(s_f_w_in[l], f_w_in[l], D, 128)
        cast_rows(s_f_wd[l], f_wd[l], DFF, 512)
    P.dma('pool', ident[:], ident_d, 'cst', writes=['ident'])
    P.op('pool', lambda en: en.memset(ones[:], 1.0), writes=['ones'])
    P.op('pool', lambda en: en.memset(epsb[:], EPS), writes=['epsb'])
    for l in range(4):
        for j in range(3):
            P.dma('pool', cw_sb[:, l, j, :], f_cw[l, j].rearrange("(c p) -> p c", p=128), 'cst', writes=['cw'], slow=True)
        P.dma('pool', cb_sb[:, l, :], f_cb[l].rearrange("(c p) -> p c", p=128), 'cst', writes=['cw'], slow=True)
        for b in range(4):
            for j in range(2):
                P.dma('pool', cst_s[:, l, b, :, j], sfc[l, b, j].rearrange("(c p) -> p c", p=128), 'cst', writes=['cst_s'], slow=True)
    for l in range(2):
        P.dma('pool', gq_bc[:, l, :], g_q[l:l + 1, :].to_broadcast([128, 512]), 'cst', writes=['gq'])
        P.dma('pool', gkv_bc[:, l, :], g_kv[l:l + 1, :].to_broadcast([128, 512]), 'cst', writes=['gq'])
        P.dma('pool', esink[:, l, :], sinks[l:l + 1, :].to_broadcast([64, BH]), 'cst', writes=['esink'])
    P.op('act', lambda en: en.activation(out=esink[:], in_=esink[:], func=AF.Exp), reads=['esink'], writes=['esink'])
    P.op('pool', lambda en: en.memset(cst_p[:], 0.0), writes=['cst_p'])

    def load_g(idx):
        P.dma('pool', gbc[:], norm_g[idx:idx + 1, :].to_broadcast([128, D]), 'gbc', writes=['gbc'])

    def rstd_from_ss(col, n):
        P.op('act', lambda en: en.activation(out=stat[:, col:col + 1], in_=stat[:, col:col + 1], func=AF.Sqrt,
                                             bias=epsb[:, 0:1], scale=1.0),
             reads=[('stat', col), 'epsb'], writes=[('stat', col)])
        P.op('dve', lambda en: en.reciprocal(out=stat[:, col:col + 1], in_=stat[:, col:col + 1]),
             reads=[('stat', col)], writes=[('stat', col)])

    def sumsq(col, src, srckeys, junk, junkkey, n=D):
        P.op('dve', lambda en: en.memset(stat[:, col:col + 1], 0.0), writes=[('stat', col)])
        P.op('act', lambda en: en.activation(out=junk, in_=src, func=AF.Square, scale=float(n ** -0.5),
                                             accum_out=stat[:, col:col + 1]),
             reads=list(srckeys) + [('stat', col)], writes=[junkkey, ('stat', col)])

    def transpose_to(dst_fn, src_bf, nchunk, srckeys, dstkeys, width=128):
        per = 1024 // 128
        for c0 in range(0, nchunk, per):
            n = min(per, nchunk - c0)
            t, tk = next_tp()

            def f(en, c0=c0, n=n, t=t):
                ins = None
                for i in range(n):
                    ins = en.transpose(out=t[0:width, i * 128:(i + 1) * 128],
                                       in_=src_bf[:, (c0 + i) * width:(c0 + i + 1) * width], identity=ident[:])
                return ins
            P.op('pe', f, reads=list(srckeys) + ['ident'], writes=[tk])
            dst = dst_fn(c0, n)
            P.op('dve', lambda en, t=t, n=n, dst=dst: en.tensor_copy(
                out=dst, in_=t[0:width, 0:n * 128].rearrange("p (c t) -> p c t", t=128)),
                reads=[tk], writes=list(dstkeys))

    def prenorm(gidx, NS):
        load_g(gidx)
        for sub in range(NS):
            sumsq(sub, x_sb[:, sub, :], [('x', sub)], y_sb[:, sub, :].bitcast(BF16)[:, 0:D], ('y', sub))
            rstd_from_ss(sub, D)
            P.op('dve', lambda en, sub=sub: en.scalar_tensor_tensor(
                out=h_bf[:], in0=x_sb[:, sub, :], scalar=stat[:, sub:sub + 1], in1=gbc[:],
                op0=ALU.mult, op1=ALU.mult), reads=[('x', sub), ('stat', sub), 'gbc'], writes=['h_bf'])
            transpose_to(lambda c0, n, sub=sub: hT[:, c0:c0 + n, sub * 128:(sub + 1) * 128], h_bf[:], 16,
                         ['h_bf'], [('hT', sub)])

    def postnorm_residual(gidx, NS):
        load_g(gidx)
        for sub in range(NS):
            sumsq(8 + sub, y_sb[:, sub, :], [('y', sub)], h_bf[:], 'h_bf')
            rstd_from_ss(8 + sub, D)
            P.op('dve', lambda en, sub=sub: en.scalar_tensor_tensor(
                out=y_sb[:, sub, :], in0=y_sb[:, sub, :], scalar=stat[:, 8 + sub:9 + sub], in1=gbc[:],
                op0=ALU.mult, op1=ALU.mult), reads=[('y', sub), ('stat', 8 + sub), 'gbc'], writes=[('y', sub)])
            P.op('dve', lambda en, sub=sub: en.tensor_tensor(
                out=x_sb[:, sub, :], in0=x_sb[:, sub, :], in1=y_sb[:, sub, :], op=ALU.add),
                reads=[('x', sub), ('y', sub)], writes=[('x', sub)])

    def down_proj(panels, lhs_fn, lhs_keys, NS, acc_first=True):
        for n, plist in enumerate(panels):
            banks = [next_pf() for _ in range(NS)]
            tot = sum(kc for _, kc, _ in plist)
            done = 0
            for (src3, kc, kbase) in plist:
                w, wk = wpanel(src3, kc, 512)
                for sub in range(NS):
                    pb, pk = banks[sub]

                    def f(en, w=w, kc=kc, kbase=kbase, sub=sub, pb=pb, done=done):
                        ins = None
                        for k in range(kc):
                            ins = en.matmul(pb[:, :], lhs_fn(kbase + k, sub), w[:, k, :],
                                            start=(done + k == 0), stop=(done + k == tot - 1))
                        return ins
                    P.op('pe', f, reads=[wk] + list(lhs_keys), writes=[pk])
                done += kc
            for sub in range(NS):
                pb, pk = banks[sub]
                dst = y_sb[:, sub, n * 512:(n + 1) * 512]
                if acc_first:
                    P.op('act', lambda en, dst=dst, pb=pb: en.activation(out=dst, in_=pb[:, :], func=AF.Identity),
                         reads=[pk], writes=[('y', sub)])
                else:
                    P.op('dve', lambda en, dst=dst, pb=pb: en.tensor_tensor(out=dst, in0=pb[:, :], in1=dst, op=ALU.add),
                         reads=[pk, ('y', sub)], writes=[('y', sub)])

    HF = NFC // 2
    actT = P.sb("actT", [128, HF, T], BF16, at=R0)
    gp = [P.sb("gp%d" % i, [128, T + 8], F32, at=R0 + HF * T * 2 + i * (T + 8) * 4) for i in range(2)]
    ftmp_off = R0 + HF * T * 2 + 2 * (T + 8) * 4
    ft = [P.sb("ft%d" % i, [128, T], F32, at=ftmp_off + i * T * 4) for i in range(3)]
    FFN_END = ftmp_off + 3 * T * 4
    assert FFN_END <= (16512 + 212000), FFN_END
    FFN_KEYS = ['actT', ('gp', 0), ('gp', 1), ('ft', 0), ('ft', 1), ('ft', 2)]

    def ffn(layer, NS, NB, cst, cst_key, out_fc):
        TT = NS * 128
        L = TT // NB
        prenorm(layer * 4 + 2, NS)
        wi = s_f_w_in[layer]
        wd = s_f_wd[layer]
        for half in range(2):
            for cp in range(0, HF, 4):
                ncnk = min(4, HF - cp)
                c_abs = half * HF + cp
                wg, wgk = wpanel(wsrc(wi, 0, 16, c_abs * 128, ncnk * 128), 16, ncnk * 128)
                wu, wuk = wpanel(wsrc(wi, 0, 16, DFF + c_abs * 128, ncnk * 128), 16, ncnk * 128)
                for m in range(ncnk):
                    ci = cp + m
                    ca = c_abs + m
                    pg, pgk = next_pf()
                    pu, puk = next_pf()

                    def f(en, w=wg, pb=pg, m=m):
                        ins = None
                        for k in range(16):
                            ins = en.matmul(pb[:, 0:TT], w[:, k, m * 128:(m + 1) * 128], hT[:, k, 0:TT],
                                            start=(k == 0), stop=(k == 15))
                        return ins
                    P.op('pe', f, reads=[wgk] + [('hT', sb_) for sb_ in range(NS)], writes=[pgk])

                    def f2(en, w=wu, pb=pu, m=m):
                        ins = None
                        for k in range(16):
                            ins = en.matmul(pb[:, 0:TT], w[:, k, m * 128:(m + 1) * 128], hT[:, k, 0:TT],
                                            start=(k == 0), stop=(k == 15))
                        return ins
                    P.op('pe', f2, reads=[wuk] + [('hT', sb_) for sb_ in range(NS)], writes=[puk])
                    gi = ca % 2
                    g = gp[gi]
                    gk = ('gp', gi)
                    g3 = g[:, 0:NB * (L + 2)].rearrange("p (b l) -> p b l", l=L + 2)
                    P.op('pool', lambda en, g3=g3, ca=ca: en.tensor_copy(out=g3[:, :, 0:2], in_=cst(ca)),
                         reads=[cst_key], writes=[gk])
                    P.op('act', lambda en, g3=g3, pg=pg: en.activation(
                        out=g3[:, :, 2:L + 2], in_=pg[:, 0:TT].rearrange("p (b l) -> p b l", l=L), func=AF.Identity),
                        reads=[pgk], writes=[gk])
                    P.op('pool', lambda en, g3=g3, ca=ca: en.tensor_copy(out=cst(ca), in_=g3[:, :, L:L + 2]),
                         reads=[gk], writes=[cst_key])
                    t0 = ft[0][:, 0:TT].rearrange("p (b l) -> p b l", l=L)
                    t1 = ft[1][:, 0:TT].rearrange("p (b l) -> p b l", l=L)
                    t2 = ft[2][:, 0:TT]
                    P.op('dve', lambda en, g3=g3, ca=ca, t0=t0: en.tensor_scalar(
                        out=t0, in0=g3[:, :, 0:L], scalar1=cw_sb[:, layer, 0, ca:ca + 1], scalar2=cb_sb[:, layer, ca:ca + 1],
                        op0=ALU.mult, op1=ALU.add), reads=[gk, 'cw'], writes=[('ft', 0)])
                    P.op('dve', lambda en, g3=g3, ca=ca, t0=t0, t1=t1: en.scalar_tensor_tensor(
                        out=t1, in0=g3[:, :, 1:L + 1], scalar=cw_sb[:, layer, 1, ca:ca + 1], in1=t0,
                        op0=ALU.mult, op1=ALU.add), reads=[gk, 'cw', ('ft', 0)], writes=[('ft', 1)])
                    P.op('dve', lambda en, g3=g3, ca=ca, t0=t0, t1=t1: en.scalar_tensor_tensor(
                        out=t0, in0=g3[:, :, 2:L + 2], scalar=cw_sb[:, layer, 2, ca:ca + 1], in1=t1,
                        op0=ALU.mult, op1=ALU.add), reads=[gk, 'cw', ('ft', 1), ('ft', 0)], writes=[('ft', 0)])
                    P.op('act', lambda en, t2=t2: en.activation(out=t2, in_=ft[0][:, 0:TT], func=AF.Silu),
                         reads=[('ft', 0)], writes=[('ft', 2)])
                    P.op('dve', lambda en, ci=ci, pu=pu, t2=t2: en.tensor_tensor(
                        out=actT[:, ci, 0:TT], in0=pu[:, 0:TT], in1=t2, op=ALU.mult),
                        reads=[puk, ('ft', 2)], writes=['actT'])
            panels = []
            for n in range(4):
                pl = []
                k0 = 0
                while k0 < HF:
                    kc = min(16, HF - k0)
                    pl.append((wsrc(wd, (half * HF + k0) * 128, kc, n * 512, 512), kc, k0))
                    k0 += kc
                panels.append(pl)
            down_proj(panels, lambda k, sub: actT[:, k, sub * 128:(sub + 1) * 128], ['actT'], NS, acc_first=(half == 0))
        if out_fc is not None:
            out_fc()
        postnorm_residual(layer * 4 + 3, NS)

    o = R0
    def ralloc(name, shape, dt):
        nonlocal o
        nbytes = (int(np.prod(shape[1:])) * (4 if dt == F32 else 2) + 63) // 64 * 64
        t_ = P.sb(name, shape, dt, at=o)
        o += nbytes
        assert o <= (16512 + 212000), (name, o)
        return t_
    a_sb = ralloc("a_sb", [128, 512], F32)
    a_bf = ralloc("a_bf", [128, 512], BF16)
    kpe_f = ralloc("kpe_f", [128, 64], F32)
    kpe_t = ralloc("kpe_t", [128, 64], F32)
    kpe_bf = ralloc("kpe_bf", [128, 64], BF16)
    cqT = ralloc("cqT", [128, 4, T], BF16)
    ckvT = ralloc("ckvT", [128, 4, T], BF16)
    kpeT = ralloc("kpeT", [64, T], BF16)
    qnT = ralloc("qnT", [128, T], BF16)
    qpeT = ralloc("qpeT", [64, T], BF16)
    rt0 = ralloc("rt0", [64, T], F32)
    rt1 = ralloc("rt1", [64, T], F32)
    oT = ralloc("oT", [128, AH, T], BF16)
    NKV = 3
    kvK = [ralloc("kvK%d" % i, [128, 512], BF16) for i in range(NKV)]
    kvP = [ralloc("kvP%d" % i, [64, 512], BF16) for i in range(NKV)]
    kvV = [ralloc("kvV%d" % i, [128, 4, 128], BF16) for i in range(NKV)]
    PT = [ralloc("PT%d" % i, [128, T], BF16) for i in range(3)]
    recip = ralloc("recip", [128, T], F32)
    kn_bf = [ralloc("kn_bf%d" % i, [128, T], BF16) for i in range(2)]
    MLA_END = o
    MLA_KEYS = ['a_sb', 'a_bf', 'kpe_f', 'kpe_t', 'kpe_bf', 'cqT', 'ckvT', 'kpeT', 'qnT', 'qpeT', 'rt0', 'rt1', 'oT',
                'recip', ('kn', 0), ('kn', 1)] + [('kv', i) for i in range(NKV)] + [('PT', i) for i in range(3)]
    print("MLA region end", MLA_END, "FFN end", FFN_END)

    kvstate = {'n': 0}
    ptstate = {'n': 0}

    def make_kv(la, ckv_src_keys, nk, kt_dst, v_dst, col0):
        for hp in range(0, AH, 4):
            w, wk = wpanel(wsrc(s_w_uk[la], 0, 4, hp * 128, 512), 4, 512)
            for m in range(4):
                h = hp + m
                pb, pk = next_pf()

                def f(en, w=w, pb=pb, m=m):
                    ins = None
                    for k in range(4):
                        ins = en.matmul(pb[:, 0:nk], w[:, k, m * 128:(m + 1) * 128], ckvT[:, k, 0:nk],
                                        start=(k == 0), stop=(k == 3))
                    return ins
                P.op('pe', f, reads=[wk] + list(ckv_src_keys), writes=[pk])
                i = h % 2
                P.op('act', lambda en, pb=pb, i=i: en.activation(out=kn_bf[i][:, 0:nk], in_=pb[:, 0:nk], func=AF.Identity),
                     reads=[pk], writes=[('kn', i)])
                P.dma('pool', kt_dst[h, :, col0:col0 + nk], kn_bf[i][:, 0:nk], 'kn%d' % i, reads=[('kn', i)], writes=['kvscr'])
        nsub = (nk + 127) // 128
        for hp in range(0, AH, 4):
            w, wk = wpanel(wsrc(s_w_uv[la], 0, 4, hp * 128, 512), 4, 512)
            for sub in range(nsub):
                rows = min(128, nk - sub * 128)
                pb, pk = next_pf()

                def f(en, w=w, pb=pb, sub=sub, rows=rows):
                    ins = None
                    for k in range(4):
                        ins = en.matmul(pb[0:rows, :], ckvT[:, k, sub * 128:sub * 128 + rows], w[:, k, :],
                                        start=(k == 0), stop=(k == 3))
                    return ins
                P.op('pe', f, reads=[wk] + list(ckv_src_keys), writes=[pk])
                i = (sub + hp // 4) % 2
                P.op('dve', lambda en, pb=pb, i=i, rows=rows: en.tensor_copy(out=kn_bf[i][0:rows, :], in_=pb[0:rows, :]),
                     reads=[pk], writes=[('kn', i)])
                P.dma('pool', v_dst[hp:hp + 4, col0 + sub * 128:col0 + sub * 128 + rows, :].rearrange("h t v -> t h v"),
                      kn_bf[i][0:rows, :].rearrange("t (h v) -> t h v", v=128), 'kn%d' % i,
                      reads=[('kn', i)], writes=['kvscr'])

    def attend(ncols, qn_ap, qpe_ap, qkeys, kblocks, out_fn, outkeys, scale):
        o_ps, o_k = pf[4], ('pf', 4)
        s_ps, s_k = pf[5], ('pf', 5)
        first = True
        nb = len(kblocks)
        for bi, (kt_src, kpe_src, v_src, nk, diag) in enumerate(kblocks):
            i = kvstate['n'] % NKV
            kvstate['n'] += 1
            nch = (nk + 127) // 128
            P.dma('pool', kvK[i][:, 0:nk], kt_src, 'kvK%d' % i, reads=['kvscr'], writes=[('kv', i)])
            P.dma('pool', kvP[i][:, 0:nk], kpe_src, 'kvP%d' % i, reads=['kvscr'], writes=[('kv', i)])
            if nk % 128 == 0:
                P.dma('pool', kvV[i][:, 0:nch, :], v_src.rearrange("(c p) v -> p c v", p=128), 'kvV%d' % i,
                      reads=['kvscr'], writes=[('kv', i)])
            else:
                P.dma('pool', kvV[i][0:nk, 0, :], v_src, 'kvV%d' % i, reads=['kvscr'], writes=[('kv', i)])
            for c in range(nch):
                rows = min(128, nk - c * 128)
                q0 = c * 128 if diag else 0
                sp_, sk = next_pf(0, 4)

                def f(en, i=i, c=c, rows=rows, q0=q0, sp_=sp_):
                    en.matmul(sp_[0:rows, q0:ncols], kvK[i][:, c * 128:c * 128 + rows], qn_ap[:, q0:ncols], start=True, stop=False)
                    return en.matmul(sp_[0:rows, q0:ncols], kvP[i][:, c * 128:c * 128 + rows], qpe_ap[:, q0:ncols], start=False, stop=True)
                P.op('pe', f, reads=[('kv', i)] + list(qkeys), writes=[sk])
                pi = ptstate['n'] % 3
                ptstate['n'] += 1
                pt = PT[pi]
                P.op('act', lambda en, pt=pt, sp_=sp_, rows=rows, q0=q0: en.activation(
                    out=pt[0:rows, q0:ncols], in_=sp_[0:rows, q0:ncols], func=AF.Exp, scale=scale),
                    reads=[sk], writes=[('PT', pi)])
                if diag:
                    P.op('pool', lambda en, pt=pt, q0=q0: en.memset(pt[64:128, q0:q0 + 64], 0.0),
                         reads=[], writes=[('PT', pi)])

                lastmm = (bi == nb - 1 and c == nch - 1)

                def f2(en, i=i, c=c, rows=rows, q0=q0, pt=pt, first=first, lastmm=lastmm):
                    en.matmul(o_ps[:, q0:ncols], kvV[i][0:rows, c, :], pt[0:rows, q0:ncols], start=first, stop=lastmm,
                              skip_group_check=True)
                    return en.matmul(s_ps[:, q0:ncols], ones[0:rows, :], pt[0:rows, q0:ncols], start=first, stop=lastmm,
                                     skip_group_check=True)
                P.op('pe', f2, reads=[('kv', i), ('PT', pi), 'ones'], writes=[o_k, s_k])
                first = False
        P.op('dve', lambda en: en.reciprocal(out=recip[:, 0:ncols], in_=s_ps[:, 0:ncols]), reads=[s_k], writes=['recip'])
        out_fn(o_ps, o_k)

    def mla(layer, la, NS, NB, tile_idx, sample):
        TT = NS * 128
        tcol = tile_idx
        prenorm(layer * 4 + 0, NS)
        tsub = tile_idx * 4 if not sample else NT * 4
        nts = NS
        P.dma('pool', tabAc[:, 0:nts, :], tA_c[:, tsub:tsub + nts, :], 'tab', writes=['tab'])
        P.dma('pool', tabAs[:, 0:nts, :], tA_s[:, tsub:tsub + nts, :], 'tab', writes=['tab'])
        fc0 = tile_idx * T if not sample else SEQ
        P.dma('pool', tabFc[:, 0:TT], tF_c[:, fc0:fc0 + TT], 'tab', writes=['tab'])
        P.dma('pool', tabFs[:, 0:TT], tF_s[:, fc0:fc0 + TT], 'tab', writes=['tab'])
        ckpt(20)
        win = s_w_in_a[la]
        wq_, wqk = None, None
        for blk, (c0, ncol) in enumerate(((0, 512), (512, 512), (1024, 64))):
            w, wk = wpanel(wsrc(win, 0, 16, c0, ncol), 16, ncol)
            for sub in range(NS):
                pb, pk = next_pf(0, 4)

                def f(en, w=w, pb=pb, sub=sub, ncol=ncol):
                    ins = None
                    for k in range(16):
                        ins = en.matmul(pb[:, 0:ncol], hT[:, k, sub * 128:(sub + 1) * 128], w[:, k, :],
                                        start=(k == 0), stop=(k == 15))
                    return ins
                P.op('pe', f, reads=[wk, ('hT', sub)], writes=[pk])
                if blk < 2:
                    col = 4 + sub
                    sumsq(col, pb[:, 0:512], [pk], a_bf[:], 'a_bf', n=512)
                    rstd_from_ss(col, 512)
                    gsrc = gq_bc if blk == 0 else gkv_bc
                    if blk == 0:
                        P.op('dve', lambda en, pb=pb, col=col, gsrc=gsrc: en.scalar_tensor_tensor(
                            out=a_bf[:], in0=pb[:, 0:512], scalar=stat[:, col:col + 1], in1=gsrc[:, la, :],
                            op0=ALU.mult, op1=ALU.mult), reads=[pk, ('stat', col), 'gq'], writes=['a_bf'])
                        transpose_to(lambda cc, n, sub=sub: cqT[:, cc:cc + n, sub * 128:(sub + 1) * 128], a_bf[:], 4,
                                     ['a_bf'], ['cqT'])
                    else:
                        P.op('dve', lambda en, pb=pb, col=col, gsrc=gsrc: en.scalar_tensor_tensor(
                            out=a_sb[:], in0=pb[:, 0:512], scalar=stat[:, col:col + 1], in1=gsrc[:, la, :],
                            op0=ALU.mult, op1=ALU.mult), reads=[pk, ('stat', col), 'gq'], writes=['a_sb'])
                        if not sample:
                            r0 = tile_idx * T + sub * 128
                            P.dma('pool', o_ckv_p[la, r0:r0 + 128, :], a_sb[:], 'o_a', reads=['a_sb'], writes=['o_ckv'])
                        else:
                            P.dma('pool', o_ckv_s[la, :, :], a_sb[:], 'o_a', reads=['a_sb'], writes=['o_ckv'])
                        P.op('pool', lambda en: en.tensor_copy(out=a_bf[:], in_=a_sb[:]), reads=['a_sb'], writes=['a_bf'])
                        transpose_to(lambda cc, n, sub=sub: ckvT[:, cc:cc + n, sub * 128:(sub + 1) * 128], a_bf[:], 4,
                                     ['a_bf'], ['ckvT'])
                else:
                    P.op('dve', lambda en, pb=pb, sub=sub: en.tensor_tensor(out=kpe_f[:, 0:32], in0=pb[:, 0:32], in1=tabAc[:, sub, :], op=ALU.mult),
                         reads=[pk, 'tab'], writes=['kpe_f'])
                    P.op('dve', lambda en, pb=pb, sub=sub: en.tensor_tensor(out=kpe_t[:, 0:32], in0=pb[:, 32:64], in1=tabAs[:, sub, :], op=ALU.mult),
                         reads=[pk, 'tab'], writes=['kpe_t'])
                    P.op('dve', lambda en, pb=pb, sub=sub: en.tensor_tensor(out=kpe_f[:, 32:64], in0=pb[:, 0:32], in1=tabAs[:, sub, :], op=ALU.mult),
                         reads=[pk, 'tab'], writes=['kpe_f'])
                    P.op('dve', lambda en, pb=pb, sub=sub: en.tensor_tensor(out=kpe_t[:, 32:64], in0=pb[:, 32:64], in1=tabAc[:, sub, :], op=ALU.mult),
                         reads=[pk, 'tab'], writes=['kpe_t'])
                    P.op('dve', lambda en: en.tensor_tensor(out=kpe_f[:, 0:32], in0=kpe_f[:, 0:32], in1=kpe_t[:, 0:32], op=ALU.subtract),
                         reads=['kpe_f', 'kpe_t'], writes=['kpe_f'])
                    P.op('dve', lambda en: en.tensor_tensor(out=kpe_f[:, 32:64], in0=kpe_f[:, 32:64], in1=kpe_t[:, 32:64], op=ALU.add),
                         reads=['kpe_f', 'kpe_t'], writes=['kpe_f'])
                    if not sample:
                        r0 = tile_idx * T + sub * 128
                        P.dma('pool', o_kpe_p[la, r0:r0 + 128, :], kpe_f[:], 'o_k', reads=['kpe_f'], writes=['o_kpe'])
                    else:
                        P.dma('pool', o_kpe_s[la, :, :], kpe_f[:], 'o_k', reads=['kpe_f'], writes=['o_kpe'])
                    P.op('pool', lambda en: en.tensor_copy(out=kpe_bf[:], in_=kpe_f[:]), reads=['kpe_f'], writes=['kpe_bf'])
                    transpose_to(lambda cc, n, sub=sub: kpeT[:, sub * 128:(sub + 1) * 128].rearrange("p (c t) -> p c t", c=1),
                                 kpe_bf[:], 1, ['kpe_bf'], ['kpeT'], width=64)
        ckpt(21)
        if not sample:
            c0 = tile_idx * T
            make_kv(la, ['ckvT'], TT, kt_p[la], v_p[la], c0)
            P.dma('pool', kpe_p[la][:, c0:c0 + TT], kpeT[:, 0:TT], 'kpw', reads=['kpeT'], writes=['kvscr'])
        else:
            for b in range(4):
                pass
            make_kv_sample_new(la)
        ckpt(22)
        for h in range(AH):
            w, wk = wpanel(s_w_qb[la][:, h, :].rearrange("(k p) n -> p k n", p=128), 4, 256)
            pn, pnk = next_pf(0, 4)
            pa, pak = next_pf(0, 4)
            pb2, pbk = next_pf(0, 4)

            def f(en, w=w, pn=pn, pa=pa, pb2=pb2):
                ins = None
                for (pp, cc0, mm) in ((pn, 0, 128), (pa, 128, 64), (pb2, 192, 64)):
                    for k in range(4):
                        ins = en.matmul(pp[0:mm, 0:TT], w[:, k, cc0:cc0 + mm], cqT[:, k, 0:TT], start=(k == 0), stop=(k == 3))
                return ins
            P.op('pe', f, reads=[wk, 'cqT'], writes=[pnk, pak, pbk])
            P.op('act', lambda en, pn=pn: en.activation(out=qnT[:, 0:TT], in_=pn[:, 0:TT], func=AF.Identity), reads=[pnk], writes=['qnT'])
            P.op('dve', lambda en, pa=pa: en.tensor_tensor(out=rt0[:, 0:TT], in0=pa[0:64, 0:TT], in1=tabFc[:, 0:TT], op=ALU.mult),
                 reads=[pak, 'tab'], writes=['rt0'])
            P.op('dve', lambda en, pb2=pb2: en.tensor_tensor(out=rt1[:, 0:TT], in0=pb2[0:64, 0:TT], in1=tabFs[:, 0:TT], op=ALU.mult),
                 reads=[pbk, 'tab'], writes=['rt1'])
            P.op('pool', lambda en: en.tensor_tensor(out=qpeT[:, 0:TT], in0=rt0[:, 0:TT], in1=rt1[:, 0:TT], op=ALU.add),
                 reads=['rt0', 'rt1'], writes=['qpeT'])
            ckpt(23)
            if not sample:
                kb = []
                for j in range(tile_idx + 1):
                    kb.append((kt_p[la, h, :, j * T:(j + 1) * T], kpe_p[la, :, j * T:(j + 1) * T],
                               v_p[la, h, j * T:(j + 1) * T, :], T, j == tile_idx))

                def outf(o_ps, o_k, h=h):
                    P.op('dve', lambda en: en.tensor_tensor(out=oT[:, h, 0:TT], in0=o_ps[:, 0:TT], in1=recip[:, 0:TT], op=ALU.mult),
                         reads=[o_k, 'recip'], writes=['oT'])
                attend(TT, qnT, qpeT, ['qnT', 'qpeT'], kb, outf, ['oT'], A_SCALE)
                ckpt(24)
            else:
                for b in range(4):
                    kb = []
                    for j in range(0, KLS, T):
                        nk = min(T, KLS - j)
                        kb.append((kt_s[la, b, h, :, j:j + nk], kpe_s[la, b, :, j:j + nk], v_s[la, b, h, j:j + nk, :], nk, False))

                    def outf(o_ps, o_k, h=h, b=b):
                        P.op('dve', lambda en: en.tensor_tensor(out=oT[:, h, b * 32:(b + 1) * 32], in0=o_ps[:, 0:32], in1=recip[:, 0:32], op=ALU.mult),
                             reads=[o_k, 'recip'], writes=['oT'])
                    attend(32, qnT[:, b * 32:(b + 1) * 32], qpeT[:, b * 32:(b + 1) * 32], ['qnT', 'qpeT'], kb, outf, ['oT'], A_SCALE)
        ckpt(25)
        wo = s_w_o_a[la]
        panels = [[(wsrc(wo, 0, 16, n * 512, 512), 16, 0)] for n in range(4)]
        down_proj(panels, lambda k, sub: oT[:, k, sub * 128:(sub + 1) * 128], ['oT'], NS)
        postnorm_residual(layer * 4 + 1, NS)

    def make_kv_sample_new(la):
        for hp in range(0, AH, 4):
            w, wk = wpanel(wsrc(s_w_uk[la], 0, 4, hp * 128, 512), 4, 512)
            for m in range(4):
                h = hp + m
                pb, pk = next_pf(0, 4)

                def f(en, w=w, pb=pb, m=m):
                    ins = None
                    for k in range(4):
                        ins = en.matmul(pb[:, 0:128], w[:, k, m * 128:(m + 1) * 128], ckvT[:, k, 0:128], start=(k == 0), stop=(k == 3))
                    return ins
                P.op('pe', f, reads=[wk, 'ckvT'], writes=[pk])
                i = h % 2
                P.op('act', lambda en, pb=pb, i=i: en.activation(out=kn_bf[i][:, 0:128], in_=pb[:, 0:128], func=AF.Identity),
                     reads=[pk], writes=[('kn', i)])
                P.dma('pool', kt_s[la, :, h, :, PAST:PAST + 32].rearrange("b p t -> p b t"),
                      kn_bf[i][:, 0:128].rearrange("p (b t) -> p b t", t=32), 'kn%d' % i, reads=[('kn', i)], writes=['kvscr'])
        for hp in range(0, AH, 4):
            w, wk = wpanel(wsrc(s_w_uv[la], 0, 4, hp * 128, 512), 4, 512)
            pb, pk = next_pf(0, 4)

            def f(en, w=w, pb=pb):
                ins = None
                for k in range(4):
                    ins = en.matmul(pb[:, :], ckvT[:, k, 0:128], w[:, k, :], start=(k == 0), stop=(k == 3))
                return ins
            P.op('pe', f, reads=[wk, 'ckvT'], writes=[pk])
            i = (hp // 4) % 2
            P.op('dve', lambda en, pb=pb, i=i: en.tensor_copy(out=kn_bf[i][:, :], in_=pb[:, :]), reads=[pk], writes=[('kn', i)])
            for b in range(4):
                P.dma('pool', v_s[la, b, hp:hp + 4, PAST:PAST + 32, :].rearrange("h t v -> t h v"),
                      kn_bf[i][b * 32:(b + 1) * 32, :].rearrange("t (h v) -> t h v", v=128), 'kn%d' % i,
                      reads=[('kn', i)], writes=['kvscr'])
        P.dma('pool', kpe_s[la, :, :, PAST:PAST + 32].rearrange("b p t -> p b t"),
              kpeT[:, 0:128].rearrange("p (b t) -> p b t", t=32), 'kpw', reads=['kpeT'], writes=['kvscr'])

    def sample_past_kv(la):
        for b in range(4):
            for j in range(0, PAST, T):
                for sub in range(4):
                    r0 = j + sub * 128
                    P.dma('pool', a_sb[:], cckv[la, b, r0:r0 + 128, :], 'ld_a', writes=['a_sb'])
                    P.op('pool', lambda en: en.tensor_copy(out=a_bf[:], in_=a_sb[:]), reads=['a_sb'], writes=['a_bf'])
                    transpose_to(lambda cc, n, sub=sub: ckvT[:, cc:cc + n, sub * 128:(sub + 1) * 128], a_bf[:], 4,
                                 ['a_bf'], ['ckvT'])
                    P.dma('pool', kpe_f[:], ckpe[la, b, r0:r0 + 128, :], 'ld_k', writes=['kpe_f'])
                    P.op('pool', lambda en: en.tensor_copy(out=kpe_bf[:], in_=kpe_f[:]), reads=['kpe_f'], writes=['kpe_bf'])
                    transpose_to(lambda cc, n, sub=sub: kpeT[:, sub * 128:(sub + 1) * 128].rearrange("p (c t) -> p c t", c=1),
                                 kpe_bf[:], 1, ['kpe_bf'], ['kpeT'], width=64)
                make_kv(la, ['ckvT'], T, kt_s[la, b], v_s[la, b], j)
                P.dma('pool', kpe_s[la, b][:, j:j + T], kpeT[:, 0:T], 'kpw', reads=['kpeT'], writes=['kvscr'])

    o = R0
    class _QB:
        def __getitem__(self, idx):
            p, sub, cols = idx
            return y_sb[:, sub, :].bitcast(BF16)[:, cols]
    q_bf = _QB()
    kv_f = ralloc("kv_f", [128, 1024], F32)
    kv_bf = ralloc("kv_bf", [128, 1024], BF16)
    rr = [ralloc("rr%d" % i, [128, 8, 8], F32) for i in range(4)]
    qT = ralloc("qT", [64, 8, 512], BF16)
    kTc = ralloc("kTc", [64, 8, 128], BF16)
    sPT = [ralloc("sPT%d" % i, [128, 512], BF16) for i in range(2)]
    sden = ralloc("sden", [64, 512], F32)
    oTs = ralloc("oTs", [64, BH, T], BF16)
    SWA_END = o
    SWA_KEYS = ['kv_f', 'kv_bf', 'rr', 'qT', 'kTc', ('sPT', 0), ('sPT', 1), 'sden', 'oTs']
    print("SWA region end", SWA_END)

    def rope_tm(pb, pk, nh, sub, dst, dstkey, dcol0):
        src = pb[:, 0:nh * 64].rearrange("p (h d) -> p h d", d=64)
        d3 = dst[:, dcol0:dcol0 + nh * 64].rearrange("p (h d) -> p h d", d=64)
        cb = tabBc[:, sub, :].rearrange("p (o d) -> p o d", o=1).to_broadcast([128, nh, 8])
        sbb = tabBs[:, sub, :].rearrange("p (o d) -> p o d", o=1).to_broadcast([128, nh, 8])
        r = [rr[i][:, 0:nh, :] for i in range(4)]
        P.op('dve', lambda en: en.tensor_tensor(out=r[0], in0=src[:, :, 0:8], in1=cb, op=ALU.mult), reads=[pk, 'tab', dstkey], writes=['rr'])
        P.op('dve', lambda en: en.tensor_tensor(out=r[1], in0=src[:, :, 8:16], in1=sbb, op=ALU.mult), reads=[pk, 'tab', dstkey], writes=['rr'])
        P.op('dve', lambda en: en.tensor_tensor(out=r[2], in0=src[:, :, 0:8], in1=sbb, op=ALU.mult), reads=[pk, 'tab', dstkey], writes=['rr'])
        P.op('dve', lambda en: en.tensor_tensor(out=r[3], in0=src[:, :, 8:16], in1=cb, op=ALU.mult), reads=[pk, 'tab', dstkey], writes=['rr'])
        P.op('dve', lambda en: en.tensor_tensor(out=d3[:, :, 0:8], in0=r[0], in1=r[1], op=ALU.subtract), reads=['rr'], writes=[dstkey])
        P.op('dve', lambda en: en.tensor_tensor(out=d3[:, :, 8:16], in0=r[2], in1=r[3], op=ALU.add), reads=['rr'], writes=[dstkey])

    def swa(layer, lb, NS, tile_idx, sample):
        ckpt(30)
        prenorm(layer * 4 + 0, NS)
        tsub = tile_idx * 4 if not sample else NT * 4
        P.dma('pool', tabBc[:, 0:NS, :], tB_c[:, tsub:tsub + NS, :], 'tab', writes=['tab'])
        P.dma('pool', tabBs[:, 0:NS, :], tB_s[:, tsub:tsub + NS, :], 'tab', writes=['tab'])
        wq = s_w_qkv[lb]
        for n in range(4):
            w, wk = wpanel(wsrc(wq, 0, 16, n * 512, 512), 16, 512)
            for sub in range(NS):
                pb, pk = next_pf()

                def f(en, w=w, pb=pb, sub=sub):
                    ins = None
                    for k in range(16):
                        ins = en.matmul(pb[:, :], hT[:, k, sub * 128:(sub + 1) * 128], w[:, k, :], start=(k == 0), stop=(k == 15))
                    return ins
                P.op('pe', f, reads=[wk, ('hT', sub)], writes=[pk])
                P.op('act', lambda en, pb=pb, sub=sub, n=n: en.activation(out=q_bf[:, sub, n * 512:(n + 1) * 512], in_=pb[:, :], func=AF.Identity),
                     reads=[pk], writes=[('y', sub)])
                rope_tm(pb, pk, 8, sub, q_bf[:, sub, :], ('y', sub), n * 512)
        ckpt(31)
        for sub in range(NS):
            wkp, wkk = wpanel(wsrc(wq, 0, 16, 2048, 512), 16, 512)
            pbk_, pkk = next_pf()

            def f(en, w=wkp, pb=pbk_, sub=sub):
                ins = None
                for k in range(16):
                    ins = en.matmul(pb[:, :], hT[:, k, sub * 128:(sub + 1) * 128], w[:, k, :], start=(k == 0), stop=(k == 15))
                return ins
            P.op('pe', f, reads=[wkk, ('hT', sub)], writes=[pkk])
            wvp, wvk = wpanel(wsrc(wq, 0, 16, 2560, 512), 16, 512)
            pbv, pkv = next_pf()

            def f2(en, w=wvp, pb=pbv, sub=sub):
                ins = None
                for k in range(16):
                    ins = en.matmul(pb[:, :], hT[:, k, sub * 128:(sub + 1) * 128], w[:, k, :], start=(k == 0), stop=(k == 15))
                return ins
            P.op('pe', f2, reads=[wvk, ('hT', sub)], writes=[pkv])
            P.op('act', lambda en, pb=pbk_: en.activation(out=kv_f[:, 0:512], in_=pb[:, :], func=AF.Identity), reads=[pkk], writes=['kv_f'])
            rope_tm(pbk_, pkk, 8, sub, kv_f[:, :], 'kv_f', 0)
            P.op('act', lambda en, pb=pbv: en.activation(out=kv_f[:, 512:1024], in_=pb[:, :], func=AF.Identity), reads=[pkv], writes=['kv_f'])
            P.op('pool', lambda en: en.tensor_copy(out=kv_bf[:], in_=kv_f[:]), reads=['kv_f'], writes=['kv_bf'])
            transpose_to(lambda cc, n: kTc[:, cc:cc + n, :], kv_bf[:, 0:512], 8, ['kv_bf'], ['kTc'], width=64)
            transpose_to(lambda cc, n: qT[:, cc // 4:(cc + n) // 4, :].rearrange("p g (h t) -> p (g h) t", t=128),
                         q_bf[:, sub, :], 32, [('y', sub)], ['qT'], width=64)
            ckpt(32)
            if not sample:
                last = (tile_idx == NT - 1 and sub == NS - 1)
                if last:
                    P.dma('pool', o_wk_p[lb], kv_f[:, 0:512], 'o_w', reads=['kv_f'], writes=['o_wk'])
                    P.dma('pool', o_wv_p[lb], kv_f[:, 512:1024], 'o_w', reads=['kv_f'], writes=['o_wk'])
                has_prev = not (tile_idx == 0 and sub == 0)
                for g in range(BKV):
                    o_ps, o_k = next_pf()
                    d_ps, d_k = next_pf()
                    blocks = []
                    if has_prev:
                        blocks.append((kTprev[:, lb, g, :], vprev[:, lb, g * 64:(g + 1) * 64], ['kprev'], True))
                    blocks.append((kTc[:, g, :], kv_bf[:, 512 + g * 64:512 + (g + 1) * 64], ['kTc', 'kv_bf'], False))
                    for bi, (kt_ap, v_ap, kkeys, isprev) in enumerate(blocks):
                        sp_, sk = next_pf()
                        P.op('pe', lambda en, sp_=sp_, kt_ap=kt_ap, g=g: en.matmul(sp_[:, :], kt_ap, qT[:, g, :], start=True, stop=True),
                             reads=kkeys + ['qT'], writes=[sk])
                        pi = ptstate['n'] % 2
                        ptstate['n'] += 1
                        pt = sPT[pi]
                        P.op('act', lambda en, pt=pt, sp_=sp_: en.activation(out=pt[:, :], in_=sp_[:, :], func=AF.Exp, scale=B_SCALE),
                             reads=[sk], writes=[('sPT', pi)])
                        pt3 = pt[:, :].rearrange("p (h t) -> p h t", t=128)
                        if isprev:
                            P.op('pool', lambda en, pt3=pt3: en.memset(pt3[0:64, :, 64:128], 0.0), writes=[('sPT', pi)])
                        else:
                            P.op('pool', lambda en, pt3=pt3: en.memset(pt3[64:128, :, 0:64], 0.0), writes=[('sPT', pi)])

                        def f3(en, pt=pt, v_ap=v_ap, bi=bi, nb=len(blocks), o_ps=o_ps, d_ps=d_ps):
                            en.matmul(o_ps[0:64, :], v_ap, pt[:, :], start=(bi == 0), stop=(bi == nb - 1), skip_group_check=True)
                            return en.matmul(d_ps[0:64, :], ones[:, 0:64], pt[:, :], start=(bi == 0), stop=(bi == nb - 1), skip_group_check=True)
                        P.op('pe', f3, reads=kkeys + [('sPT', pi), 'ones'], writes=[o_k, d_k])
                    swa_finish(lb, g, o_ps, o_k, d_ps, d_k, 128, lambda hh, sub=sub: oTs[:, hh, sub * 128:(sub + 1) * 128])
                P.op('pool', lambda en: en.tensor_copy(out=kTprev[:, lb], in_=kTc[:]), reads=['kTc'], writes=['kprev'])
                P.op('pool', lambda en: en.tensor_copy(out=vprev[:, lb, :], in_=kv_bf[:, 512:1024]), reads=['kv_bf'], writes=['kprev'])
            else:
                for b in range(4):
                    P.dma('pool', cw_f[:], cwk[lb, b], 'ld_c', writes=['cw_f'])
                    P.dma('pool', o_wk_s[lb, b, 0:96, :], cwk[lb, b, 32:128, :], 'o_w', writes=['o_wk'])
                    P.dma('pool', o_wv_s[lb, b, 0:96, :], cwv[lb, b, 32:128, :], 'o_w', writes=['o_wk'])
                    P.dma('pool', o_wk_s[lb, b, 96:128, :], kv_f[b * 32:(b + 1) * 32, 0:512], 'o_w', reads=['kv_f'], writes=['o_wk'])
                    P.dma('pool', o_wv_s[lb, b, 96:128, :], kv_f[b * 32:(b + 1) * 32, 512:1024], 'o_w', reads=['kv_f'], writes=['o_wk'])
                    P.op('pool', lambda en: en.tensor_copy(out=vpast[:], in_=cw_f[:]), reads=['cw_f'], writes=['vpast'])
                    transpose_to(lambda cc, n: kTpast[:, cc:cc + n, :], vpast[:], 8, ['vpast'], ['kTpast'], width=64)
                    P.dma('pool', cw_f[:], cwv[lb, b], 'ld_c', reads=['vpast'], writes=['cw_f'])
                    P.op('pool', lambda en: en.tensor_copy(out=vpast[:], in_=cw_f[:]), reads=['cw_f', 'kTpast'], writes=['vpast'])
                    for g in range(BKV):
                        o_ps, o_k = next_pf()
                        d_ps, d_k = next_pf()
                        qv = qT[:, g, :].rearrange("p (h t) -> p h t", t=128)[:, :, b * 32:(b + 1) * 32]
                        blocks = [(kTpast[:, g, :], vpast[:, g * 64:(g + 1) * 64], 128, ['kTpast', 'vpast']),
                                  (kTc[:, g, :], kv_bf[:, 512 + g * 64:512 + (g + 1) * 64], 128, ['kTc', 'kv_bf'])]
                        for bi, (kt_ap, v_ap, nk, kkeys) in enumerate(blocks):
                            sp_, sk = next_pf()
                            s3 = sp_[0:nk, 0:128].rearrange("p (h t) -> p h t", t=32)
                            P.op('pe', lambda en, s3=s3, kt_ap=kt_ap, qv=qv: en.matmul(s3, kt_ap, qv, start=True, stop=True),
                                 reads=kkeys + ['qT'], writes=[sk])
                            pi = ptstate['n'] % 2
                            ptstate['n'] += 1
                            pt = sPT[pi]
                            P.op('act', lambda en, pt=pt, sp_=sp_, nk=nk: en.activation(out=pt[0:nk, 0:128], in_=sp_[0:nk, 0:128], func=AF.Exp, scale=B_SCALE),
                                 reads=[sk], writes=[('sPT', pi)])
                            if bi == 1:
                                for ob in range(4):
                                    if ob != b:
                                        P.op('pool', lambda en, pt=pt, ob=ob: en.memset(pt[ob * 32:(ob + 1) * 32, 0:128], 0.0), writes=[('sPT', pi)])

                            def f3(en, pt=pt, v_ap=v_ap, bi=bi, nk=nk, o_ps=o_ps, d_ps=d_ps):
                                en.matmul(o_ps[0:64, 0:128], v_ap, pt[0:nk, 0:128], start=(bi == 0), stop=(bi == 1), skip_group_check=True)
                                return en.matmul(d_ps[0:64, 0:128], ones[0:nk, 0:64], pt[0:nk, 0:128], start=(bi == 0), stop=(bi == 1), skip_group_check=True)
                            P.op('pe', f3, reads=kkeys + [('sPT', pi), 'ones'], writes=[o_k, d_k])
                        swa_finish(lb, g, o_ps, o_k, d_ps, d_k, 32, lambda hh, b=b: oTs[:, hh, b * 32:(b + 1) * 32])
        ckpt(33)
        wo = s_w_o_b[lb]
        for n in range(4):
            banks = [next_pf() for _ in range(NS)]
            for hq in range(0, BH, 8):
                src3 = wo[hq * 64:(hq + 8) * 64, n * 512:(n + 1) * 512].rearrange("(k p) n -> p k n", p=64)
                i = wstate['n'] % NWB
                wstate['n'] += 1
                view = wring[i][0:64, 0:8 * 512].rearrange("p (k n) -> p k n", n=512)
                P.dma('sp', view, src3, 'w%d' % i, writes=[('w', i)])
                for sub in range(NS):
                    pb, pk = banks[sub]

                    def f(en, view=view, hq=hq, sub=sub, pb=pb):
                        ins = None
                        for k in range(8):
                            ins = en.matmul(pb[:, :], oTs[:, hq + k, sub * 128:(sub + 1) * 128], view[:, k, :],
                                            start=(hq + k == 0), stop=(hq + k == BH - 1))
                        return ins
                    P.op('pe', f, reads=[('w', i), 'oTs'], writes=[pk])
            for sub in range(NS):
                pb, pk = banks[sub]
                P.op('act', lambda en, pb=pb, sub=sub, n=n: en.activation(out=y_sb[:, sub, n * 512:(n + 1) * 512], in_=pb[:, :], func=AF.Identity),
                     reads=[pk], writes=[('y', sub)])
        postnorm_residual(layer * 4 + 1, NS)

    def swa_finish(lb, g, o_ps, o_k, d_ps, d_k, nq, dst_fn):
        ncol = 4 * nq
        d3 = sden[:, 0:ncol].rearrange("p (h t) -> p h t", t=nq)
        es = esink[:, lb, g * 4:(g + 1) * 4].rearrange("p (h o) -> p h o", o=1).to_broadcast([64, 4, nq])
        P.op('dve', lambda en: en.tensor_tensor(out=d3, in0=d_ps[0:64, 0:ncol].rearrange("p (h t) -> p h t", t=nq), in1=es, op=ALU.add),
             reads=[d_k, 'esink'], writes=['sden'])
        P.op('dve', lambda en: en.reciprocal(out=sden[:, 0:ncol], in_=sden[:, 0:ncol]), reads=['sden'], writes=['sden'])
        for hh in range(4):
            P.op('dve', lambda en, hh=hh: en.tensor_tensor(out=dst_fn(g * 4 + hh), in0=o_ps[0:64, hh * nq:(hh + 1) * nq],
                                                            in1=sden[:, hh * nq:(hh + 1) * nq], op=ALU.mult),
                 reads=[o_k, 'sden'], writes=['oTs'])

    def ckpt(n):
        if STAGE == n:
            raise StopBuild()
    try:
        cur = {'keys': []}

        def phase(newkeys):
            P.fence(cur['keys'], newkeys)
            cur['keys'] = newkeys

        def fc_out(layer, sample):
            def f():
                if not sample:
                    for j in range(2):
                        P.dma('pool', o_fc_p[layer, j].rearrange("(c p) -> p c", p=128), cst_p[:, layer, :, j], 'o_fc',
                              reads=['cst_p'], writes=['o_fc'], slow=True)
                else:
                    for b in range(4):
                        for j in range(2):
                            P.dma('pool', o_fc_s[layer, b, j].rearrange("(c p) -> p c", p=128), cst_s[:, layer, b, :, j], 'o_fc',
                                  reads=['cst_s'], writes=['o_fc'], slow=True)
            return f

        def wscr_guard():
            pass
        P.fence(['wscr'], [('w', i) for i in range(NWB)])

        if STAGE == 0:
            P.emit(); return nc
        phase(MLA_KEYS)
        for la in range(2):
            sample_past_kv(la)
        if STAGE == 1:
            P.emit(); return nc

        for t in range(NT):
            for sub in range(4):
                P.dma('pool', x_sb[:, sub, :], xp[t * T + sub * 128:t * T + (sub + 1) * 128, :], 'ldx', writes=[('x', sub)])
            for layer in range(DEPTH):
                if layer % 2 == 0:
                    phase(MLA_KEYS)
                    mla(layer, layer // 2, 4, 1, t, False)
                else:
                    phase(SWA_KEYS)
                    swa(layer, layer // 2, 4, t, False)
                if STAGE == 2 + layer * 2 and t == 0:
                    P.emit(); return nc
                phase(FFN_KEYS)
                ffn(layer, 4, 1, lambda ca, layer=layer: cst_p[:, layer, ca, :].rearrange("p (b j) -> p b j", b=1), 'cst_p',
                    fc_out(layer, False) if t == NT - 1 else None)
            for sub in range(4):
                P.dma('pool', yp[t * T + sub * 128:t * T + (sub + 1) * 128, :], x_sb[:, sub, :], 'stx', reads=[('x', sub)], writes=['yp'])
        P.fence([('y', 1), ('y', 2), ('y', 3)], ['kTpast', 'vpast', 'cw_f'])
        P.dma('pool', x_sb[:, 0, :], xs[:, :], 'ldx', writes=[('x', 0)])
        for layer in range(DEPTH):
            if layer % 2 == 0:
                phase(MLA_KEYS)
                mla(layer, layer // 2, 1, 4, 0, True)
            else:
                phase(SWA_KEYS)
                swa(layer, layer // 2, 1, 0, True)
            phase(FFN_KEYS)
            ffn(layer, 1, 4, lambda ca, layer=layer: cst_s[:, layer, :, ca, :], 'cst_s', fc_out(layer, True))
        P.dma('pool', ys[:, :], x_sb[:, 0, :], 'stx', reads=[('x', 0)], writes=['ys'])
    except StopBuild:
        pass
    P.emit()
    return nc


def rope_tables(SEQ, PAST):
    NT = SEQ // T
    NTS = NT * 4 + 1
    pos_tm = np.zeros((128, NTS), np.float32)
    p = np.arange(128)
    for j in range(NT * 4):
        pos_tm[:, j] = j * 128 + p
    pos_tm[:, NT * 4] = PAST + (p % 32)

    def tabs(half):
        inv = (np.float32(THETA) ** (-np.arange(half, dtype=np.float32) / np.float32(half))).astype(np.float32)
        ang = pos_tm[:, :, None].astype(np.float32) * inv[None, None, :]
        return np.cos(ang).astype(np.float32), np.sin(ang).astype(np.float32)
    tA_c, tA_s = tabs(32)
    tB_c, tB_s = tabs(8)
    pos_f = np.concatenate([np.arange(SEQ), PAST + (np.arange(128) % 32)]).astype(np.float32)
    inv = (np.float32(THETA) ** (-np.arange(32, dtype=np.float32) / np.float32(32))).astype(np.float32)
    ang = pos_f[None, :] * inv[:, None]
    c = np.cos(ang).astype(np.float32)
    s_ = np.sin(ang).astype(np.float32)
    tF_c = np.concatenate([c, c], axis=0)
    tF_s = np.concatenate([-s_, s_], axis=0)
    return dict(tA_c=tA_c, tA_s=tA_s, tB_c=tB_c, tB_s=tB_s, tF_c=np.ascontiguousarray(tF_c), tF_s=np.ascontiguousarray(tF_s))


def run(inputs, SEQ, PAST, STAGE=99, ncores=NCORES):
    f = lambda a: np.ascontiguousarray(np.asarray(a, dtype=np.float32))
    I = {k: np.asarray(v) for k, v in inputs.items()}
    nc = build(SEQ, PAST, STAGE)
    tabs = rope_tables(SEQ, PAST)
    shared = dict(
        norm_g=f(I['norm_g'].reshape(16, D)), w_in_a=f(I['mla_w_in']), g_q=f(I['mla_g_q']), w_qb=f(I['mla_w_qb']),
        g_kv=f(I['mla_g_kv']), w_uk=f(I['mla_w_uk'].reshape(2, KVL, 2048)), w_uv=f(I['mla_w_uv'].reshape(2, KVL, 2048)),
        w_o_a=f(I['mla_w_o']), w_qkv=f(I['swa_w_qkv']), sinks=f(I['swa_sinks']), w_o_b=f(I['swa_w_o']),
        f_w_in=f(I['ffn_w_in']), f_cw=f(I['ffn_conv_w']), f_cb=f(I['ffn_conv_b']), f_wd=f(I['ffn_w_down']),
        ident=np.eye(128, dtype=np.float32), **tabs)
    in_maps = []
    for c in range(ncores):
        b = c % 4
        sb_ = slice(4 * b, 4 * b + 4)
        m = dict(shared)
        m.update(
            xp=f(I['x_prompt'][b]), xs=f(I['x_sample'][sb_].reshape(128, D)),
            cckv=f(I['cache_ckv'][:, sb_]), ckpe=f(I['cache_kpe'][:, sb_]),
            cwk=f(I['cache_win_k'][:, sb_].reshape(2, 4, 128, 512)), cwv=f(I['cache_win_v'][:, sb_].reshape(2, 4, 128, 512)),
            sfc=f(I['state_ffn_conv'][:, sb_]))
        in_maps.append(m)
    res = run_bass_kernel_spmd(nc, in_maps, core_ids=list(range(ncores)))
    R = res.results
    if ncores < 4:
        R = [R[0]] * 4
    st = lambda name, ax=0: np.stack([R[c][name] for c in range(4)], axis=ax)
    y_prompt = st('yp')
    y_sample = st('ys').reshape(16, 32, D)
    ckv_p = st('o_ckv_p', 1)
    kpe_p = st('o_kpe_p', 1)
    wk_p = st('o_wk_p', 1).reshape(2, 4, 128, BKV, BHD)
    wv_p = st('o_wv_p', 1).reshape(2, 4, 128, BKV, BHD)
    fc_p = st('o_fc_p', 1)
    ckv_s = st('o_ckv_s', 1).reshape(2, 16, 32, KVL)
    kpe_s = st('o_kpe_s', 1).reshape(2, 16, 32, ROPE)
    wk_s = np.concatenate([R[c]['o_wk_s'] for c in range(4)], axis=1).reshape(2, 16, 128, BKV, BHD)
    wv_s = np.concatenate([R[c]['o_wv_s'] for c in range(4)], axis=1).reshape(2, 16, 128, BKV, BHD)
    fc_s = np.concatenate([R[c]['o_fc_s'] for c in range(4)], axis=1)
    return (y_prompt, y_sample, ckv_p, kpe_p, wk_p, wv_p, fc_p, ckv_s, kpe_s, wk_s, wv_s, fc_s)


def kernel(**inputs):
    return run(inputs, 8192, 4096)
```

```python
import numpy as np
import concourse.bass as bass
import concourse.mybir as mybir
from concourse.bass_utils import run_bass_kernel_spmd

F32, BF16 = mybir.dt.float32, mybir.dt.bfloat16
AF = mybir.ActivationFunctionType
ALU = mybir.AluOpType

D = 2048
DEPTH = 4
CHUNK = 64
THETA = 500000.0
EPS = 1e-6
AH, QL, KVL, NOPE, ROPE, AV = 16, 512, 512, 128, 64, 128
A_SCALE = (NOPE + ROPE) ** -0.5
BH, BKV, BHD, BROT = 32, 8, 64, 16
B_SCALE = BHD ** -0.5
DFF = 5632
NFC = DFF // 128
T = 512
NCORES = 8


class StopBuild(Exception):
    pass


class Prog:
    def __init__(s, nc):
        s.nc = nc
        s.eng = {'pe': nc.tensor, 'act': nc.scalar, 'dve': nc.vector, 'pool': nc.gpsimd, 'sp': nc.sync}
        s.ops = {k: [] for k in s.eng}
        s.cnt = {k: 0 for k in ('pe', 'act', 'dve', 'pool')}
        s.esem = {k: nc.alloc_semaphore('e_' + k) for k in s.cnt}
        s.seen = {k: {} for k in s.eng}
        s.lastw = {}
        s.readers = {}
        s.dsem = {}
        s.off = 16512
        s.nps = 0

    def sb(s, name, shape, dt, at=None):
        nbytes = int(np.prod(shape[1:])) * (4 if dt == F32 else 2)
        nbytes = (nbytes + 63) // 64 * 64
        if at is None:
            at = s.off
            s.off += nbytes
            assert s.off <= (16512 + 212000), (name, s.off)
        return s.nc.alloc_sbuf_tensor_at(name, list(shape), dt, offset=at)

    def ps(s, name, shape, dt):
        return s.nc.alloc_psum_tensor(name, list(shape), dt)

    def _deps(s, e, reads, writes):
        need = {}

        def add(ev):
            if ev is None:
                return
            sem, val, src = ev
            if src == e and e == 'pe':
                return
            if need.get(sem.name, (None, 0))[1] < val:
                need[sem.name] = (sem, val)
        for k in reads:
            add(s.lastw.get(k))
        for k in writes:
            add(s.lastw.get(k))
            for ev in s.readers.get(k, {}).values():
                add(ev)
        waits = []
        for name, (sem, val) in need.items():
            if s.seen[e].get(name, 0) < val:
                s.seen[e][name] = val
                waits.append((sem, val))
        return waits

    def _commit(s, ev, reads, writes):
        for k in reads:
            s.readers.setdefault(k, {})[ev[0].name] = ev
        for k in writes:
            s.lastw[k] = ev
            s.readers[k] = {}

    def op(s, e, fn, reads=(), writes=()):
        waits = s._deps(e, reads, writes)
        s.cnt[e] += 1
        ev = (s.esem[e], s.cnt[e], e)
        s.ops[e].append((waits, fn, (s.esem[e], 1)))
        s._commit(ev, reads, writes)

    def dma(s, q, out, in_, slot, reads=(), writes=(), slow=False, throttle=3):
        waits = s._deps(q, reads, writes)
        if slot not in s.dsem:
            s.dsem[slot] = [s.nc.alloc_semaphore('d_' + slot), 0]
        rec = s.dsem[slot]
        if throttle and rec[1] - 16 * throttle > 0 and s.seen[q].get(rec[0].name, 0) < rec[1] - 16 * throttle:
            s.seen[q][rec[0].name] = rec[1] - 16 * throttle
            waits.append((rec[0], rec[1] - 16 * throttle))
        rec[1] += 16
        ev = (rec[0], rec[1], 'dma')
        if slow:
            fn = lambda en: en.dma_start(out=out, in_=in_, allow_slow_non_contiguous=True)
        else:
            fn = lambda en: en.dma_start(out=out, in_=in_)
        s.ops[q].append((waits, fn, (rec[0], 16)))
        s._commit(ev, reads, writes)

    def fence(s, old_keys, new_keys):
        evs = {}
        for k in old_keys:
            ev = s.lastw.get(k)
            if ev is not None:
                evs[ev[0].name] = max(evs.get(ev[0].name, ev), ev, key=lambda x: x[1])
            for ev in s.readers.get(k, {}).values():
                evs[ev[0].name] = max(evs.get(ev[0].name, ev), ev, key=lambda x: x[1])
        for k in new_keys:
            r = s.readers.setdefault(k, {})
            for n, ev in evs.items():
                if n not in r or r[n][1] < ev[1]:
                    r[n] = ev

    def emit(s):
        nc = s.nc
        fin = []
        for k in s.cnt:
            if s.cnt[k]:
                fin.append((s.esem[k], s.cnt[k]))
        for slot, (sem, val) in s.dsem.items():
            fin.append((sem, val))
        with nc.Block() as block:
            for name, method in (('sp', block.sync), ('act', block.scalar), ('dve', block.vector),
                                 ('pool', block.gpsimd), ('pe', block.tensor)):
                def f(en, name=name):
                    for waits, fn, inc in s.ops[name]:
                        for sem, val in waits:
                            en.wait_ge(sem, val)
                        ins = fn(en)
                        if inc is not None:
                            ins.then_inc(inc[0], inc[1])
                    if name == 'sp':
                        for sem, val in fin:
                            en.wait_ge(sem, val)
                method(f)


def build(SEQ, PAST, STAGE=99):
    NT = SEQ // T
    NTS = NT * 4 + 1
    nc = bass.Bass("TRN2", target_bir_lowering=False)
    P = Prog(nc)

    def din(name, shape, dt=F32):
        return nc.dram_tensor(name, list(shape), dt, kind="ExternalInput").ap()

    def dout(name, shape):
        return nc.dram_tensor(name, list(shape), F32, kind="ExternalOutput").ap()

    def dscr(name, shape, dt=BF16):
        return nc.dram_tensor(name, list(shape), dt, kind="Internal").ap()

    xp = din("xp", [SEQ, D]); xs = din("xs", [128, D])
    cckv = din("cckv", [2, 4, PAST, KVL]); ckpe = din("ckpe", [2, 4, PAST, ROPE])
    cwk = din("cwk", [2, 4, 128, 512]); cwv = din("cwv", [2, 4, 128, 512])
    sfc = din("sfc", [4, 4, 2, DFF])
    norm_g = din("norm_g", [16, D])
    w_in_a = din("w_in_a", [2, D, 1088]); g_q = din("g_q", [2, QL]); w_qb = din("w_qb", [2, QL, 3072])
    g_kv = din("g_kv", [2, KVL]); w_uk = din("w_uk", [2, KVL, 2048]); w_uv = din("w_uv", [2, KVL, 2048])
    w_o_a = din("w_o_a", [2, D, D]); w_qkv = din("w_qkv", [2, D, 3072]); sinks = din("sinks", [2, BH])
    w_o_b = din("w_o_b", [2, D, D]); f_w_in = din("f_w_in", [4, D, 2 * DFF]); f_cw = din("f_cw", [4, 3, DFF])
    f_cb = din("f_cb", [4, DFF]); f_wd = din("f_wd", [4, DFF, D])
    ident_d = din("ident", [128, 128])
    tA_c = din("tA_c", [128, NTS, 32]); tA_s = din("tA_s", [128, NTS, 32])
    tB_c = din("tB_c", [128, NTS, 8]); tB_s = din("tB_s", [128, NTS, 8])
    tF_c = din("tF_c", [64, SEQ + 128]); tF_s = din("tF_s", [64, SEQ + 128])

    yp = dout("yp", [SEQ, D]); ys = dout("ys", [128, D])
    o_ckv_p = dout("o_ckv_p", [2, SEQ, KVL]); o_kpe_p = dout("o_kpe_p", [2, SEQ, ROPE])
    o_wk_p = dout("o_wk_p", [2, 128, 512]); o_wv_p = dout("o_wv_p", [2, 128, 512])
    o_fc_p = dout("o_fc_p", [4, 2, DFF])
    o_ckv_s = dout("o_ckv_s", [2, 128, KVL]); o_kpe_s = dout("o_kpe_s", [2, 128, ROPE])
    o_wk_s = dout("o_wk_s", [2, 4, 128, 512]); o_wv_s = dout("o_wv_s", [2, 4, 128, 512])
    o_fc_s = dout("o_fc_s", [4, 4, 2, DFF])

    s_w_in_a = dscr("s_w_in_a", [2, D, 1088]); s_w_qb = dscr("s_w_qb", [2, QL, AH, 256])
    s_w_uk = dscr("s_w_uk", [2, KVL, 2048]); s_w_uv = dscr("s_w_uv", [2, KVL, 2048])
    s_w_o_a = dscr("s_w_o_a", [2, D, D]); s_w_qkv = dscr("s_w_qkv", [2, D, 3072])
    s_w_o_b = dscr("s_w_o_b", [2, D, D]); s_f_w_in = dscr("s_f_w_in", [4, D, 2 * DFF])
    s_f_wd = dscr("s_f_wd", [4, DFF, D])
    KL = SEQ
    KLS = PAST + 32
    kt_p = dscr("kt_p", [2, AH, 128, KL]); v_p = dscr("v_p", [2, AH, KL, 128]); kpe_p = dscr("kpe_p", [2, 64, KL])
    kt_s = dscr("kt_s", [2, 4, AH, 128, KLS]); v_s = dscr("v_s", [2, 4, AH, KLS, 128]); kpe_s = dscr("kpe_s", [2, 4, 64, KLS])

    ident = P.sb("ident", [128, 128], BF16)
    ones = P.sb("ones", [128, 128], BF16)
    cw_sb = P.sb("cw_sb", [128, 4, 3, NFC], F32)
    cb_sb = P.sb("cb_sb", [128, 4, NFC], F32)
    gq_bc = P.sb("gq_bc", [128, 2, 512], F32)
    gkv_bc = P.sb("gkv_bc", [128, 2, 512], F32)
    esink = P.sb("esink", [64, 2, BH], F32)
    tabAc = P.sb("tabAc", [128, 4, 32], F32); tabAs = P.sb("tabAs", [128, 4, 32], F32)
    tabBc = P.sb("tabBc", [128, 4, 8], F32); tabBs = P.sb("tabBs", [128, 4, 8], F32)
    tabFc = P.sb("tabFc", [64, 512], F32); tabFs = P.sb("tabFs", [64, 512], F32)
    cst_p = P.sb("cst_p", [128, 4, NFC, 2], F32)
    cst_s = P.sb("cst_s", [128, 4, 4, NFC, 2], F32)
    kTprev = P.sb("kTprev", [64, 2, 8, 128], BF16)
    vprev = P.sb("vprev", [128, 2, 512], BF16)
    stat = P.sb("stat", [128, 16], F32)
    epsb = P.sb("epsb", [128, 1], F32)
    x_sb = P.sb("x_sb", [128, 4, D], F32)
    Y_AT = P.off
    y_sb = P.sb("y_sb", [128, 4, D], F32)
    kTpast = P.sb("kTpast", [64, 8, 128], BF16, at=Y_AT + 8192)
    vpast = P.sb("vpast", [128, 512], BF16, at=Y_AT + 8192 + 2048)
    cw_f = P.sb("cw_f", [128, 512], F32, at=Y_AT + 8192 + 2048 + 1024)
    hT = P.sb("hT", [128, 16, T], BF16)
    h_bf = P.sb("h_bf", [128, D], BF16)
    gbc = P.sb("gbc", [128, D], F32)
    NWB = 2
    wring = [P.sb("wring%d" % i, [128, 16 * 512], BF16) for i in range(NWB)]
    R0 = P.off
    RSIZE = (16512 + 212000) - R0
    print("SBUF persistent bytes", R0, "region", RSIZE)

    tp = [P.ps("tp%d" % i, [128, 1024], BF16) for i in range(2)]
    pf = [P.ps("pf%d" % i, [128, 512], F32) for i in range(6)]
    tpi = [0]

    def next_tp():
        tpi[0] ^= 1
        return tp[tpi[0]], ('tp', tpi[0])
    pfi = [0]

    def next_pf(lo=0, hi=6):
        i = lo + (pfi[0] % (hi - lo))
        pfi[0] += 1
        return pf[i], ('pf', i)

    wstate = {'n': 0}

    def wpanel(src3, kc, ncols):
        i = wstate['n'] % NWB
        wstate['n'] += 1
        view = wring[i][:, 0:kc * ncols].rearrange("p (k n) -> p k n", n=ncols)
        P.dma('sp', view, src3, 'w%d' % i, writes=[('w', i)])
        return view, ('w', i)

    def wsrc(mat, r0, kc, c0, ncols):
        return mat[r0:r0 + kc * 128, c0:c0 + ncols].rearrange("(k p) n -> p k n", p=128)

    def cast_rows(dst, src, rows, step):
        for r in range(0, rows, step):
            r1 = min(rows, r + step)
            P.dma('pool', dst[r:r1], src[r:r1], 'cast', writes=['wscr'])

    for l in range(2):
        cast_rows(s_w_in_a[l], w_in_a[l], D, 1024)
        wq = w_qb[l].rearrange("r (h c) -> r h c", c=192)
        for r in range(0, QL, 128):
            P.dma('pool', s_w_qb[l][r:r + 128, :, 0:192], wq[r:r + 128], 'cast', writes=['wscr'])
            P.dma('pool', s_w_qb[l][r:r + 128, :, 192:224], wq[r:r + 128, :, 160:192], 'cast', writes=['wscr'])
            P.dma('pool', s_w_qb[l][r:r + 128, :, 224:256], wq[r:r + 128, :, 128:160], 'cast', writes=['wscr'])
        cast_rows(s_w_uk[l], w_uk[l], KVL, 512)
        cast_rows(s_w_uv[l], w_uv[l], KVL, 512)
        cast_rows(s_w_o_a[l], w_o_a[l], D, 1024)
        cast_rows(s_w_qkv[l], w_qkv[l], D, 512)
        cast_rows(s_w_o_b[l], w_o_b[l], D, 1024)
    for l in range(4):
        cast_rows(s_f_w_in[l], f_w_in[l], D, 128)
        cast_rows(s_f_wd[l], f_wd[l], DFF, 512)
    P.dma('pool', ident[:], ident_d, 'cst', writes=['ident'])
    P.op('pool', lambda en: en.memset(ones[:], 1.0), writes=['ones'])
    P.op('pool', lambda en: en.memset(epsb[:], EPS), writes=['epsb'])
    for l in range(4):
        for j in range(3):
            P.dma('pool', cw_sb[:, l, j, :], f_cw[l, j].rearrange("(c p) -> p c", p=128), 'cst', writes=['cw'], slow=True)
        P.dma('pool', cb_sb[:, l, :], f_cb[l].rearrange("(c p) -> p c", p=128), 'cst', writes=['cw'], slow=True)
        for b in range(4):
            for j in range(2):
                P.dma('pool', cst_s[:, l, b, :, j], sfc[l, b, j].rearrange("(c p) -> p c", p=128), 'cst', writes=['cst_s'], slow=True)
    for l in range(2):
        P.dma('pool', gq_bc[:, l, :], g_q[l:l + 1, :].to_broadcast([128, 512]), 'cst', writes=['gq'])
        P.dma('pool', gkv_bc[:, l, :], g_kv[l:l + 1, :].to_broadcast([128, 512]), 'cst', writes=['gq'])
        P.dma('pool', esink[:, l, :], sinks[l:l + 1, :].to_broadcast([64, BH]), 'cst', writes=['esink'])
    P.op('act', lambda en: en.activation(out=esink[:], in_=esink[:], func=AF.Exp), reads=['esink'], writes=['esink'])
    P.op('pool', lambda en: en.memset(cst_p[:], 0.0), writes=['cst_p'])

    def load_g(idx):
        P.dma('pool', gbc[:], norm_g[idx:idx + 1, :].to_broadcast([128, D]), 'gbc', writes=['gbc'])

    def rstd_from_ss(col, n):
        P.op('act', lambda en: en.activation(out=stat[:, col:col + 1], in_=stat[:, col:col + 1], func=AF.Sqrt,
                                             bias=epsb[:, 0:1], scale=1.0),
             reads=[('stat', col), 'epsb'], writes=[('stat', col)])
        P.op('dve', lambda en: en.reciprocal(out=stat[:, col:col + 1], in_=stat[:, col:col + 1]),
             reads=[('stat', col)], writes=[('stat', col)])

    def sumsq(col, src, srckeys, junk, junkkey, n=D):
        P.op('dve', lambda en: en.memset(stat[:, col:col + 1], 0.0), writes=[('stat', col)])
        P.op('act', lambda en: en.activation(out=junk, in_=src, func=AF.Square, scale=float(n ** -0.5),
                                             accum_out=stat[:, col:col + 1]),
             reads=list(srckeys) + [('stat', col)], writes=[junkkey, ('stat', col)])

    def transpose_to(dst_fn, src_bf, nchunk, srckeys, dstkeys, width=128):
        per = 1024 // 128
        for c0 in range(0, nchunk, per):
            n = min(per, nchunk - c0)
            t, tk = next_tp()

            def f(en, c0=c0, n=n, t=t):
                ins = None
                for i in range(n):
                    ins = en.transpose(out=t[0:width, i * 128:(i + 1) * 128],
                                       in_=src_bf[:, (c0 + i) * width:(c0 + i + 1) * width], identity=ident[:])
                return ins
            P.op('pe', f, reads=list(srckeys) + ['ident'], writes=[tk])
            dst = dst_fn(c0, n)
            P.op('dve', lambda en, t=t, n=n, dst=dst: en.tensor_copy(
                out=dst, in_=t[0:width, 0:n * 128].rearrange("p (c t) -> p c t", t=128)),
                reads=[tk], writes=list(dstkeys))

    def prenorm(gidx, NS):
        load_g(gidx)
        for sub in range(NS):
            sumsq(sub, x_sb[:, sub, :], [('x', sub)], y_sb[:, sub, :].bitcast(BF16)[:, 0:D], ('y', sub))
            rstd_from_ss(sub, D)
            P.op('dve', lambda en, sub=sub: en.scalar_tensor_tensor(
                out=h_bf[:], in0=x_sb[:, sub, :], scalar=stat[:, sub:sub + 1], in1=gbc[:],
                op0=ALU.mult, op1=ALU.mult), reads=[('x', sub), ('stat', sub), 'gbc'], writes=['h_bf'])
            transpose_to(lambda c0, n, sub=sub: hT[:, c0:c0 + n, sub * 128:(sub + 1) * 128], h_bf[:], 16,
                         ['h_bf'], [('hT', sub)])

    def postnorm_residual(gidx, NS):
        load_g(gidx)
        for sub in range(NS):
            sumsq(8 + sub, y_sb[:, sub, :], [('y', sub)], h_bf[:], 'h_bf')
            rstd_from_ss(8 + sub, D)
            P.op('dve', lambda en, sub=sub: en.scalar_tensor_tensor(
                out=y_sb[:, sub, :], in0=y_sb[:, sub, :], scalar=stat[:, 8 + sub:9 + sub], in1=gbc[:],
                op0=ALU.mult, op1=ALU.mult), reads=[('y', sub), ('stat', 8 + sub), 'gbc'], writes=[('y', sub)])
            P.op('dve', lambda en, sub=sub: en.tensor_tensor(
                out=x_sb[:, sub, :], in0=x_sb[:, sub, :], in1=y_sb[:, sub, :], op=ALU.add),
                reads=[('x', sub), ('y', sub)], writes=[('x', sub)])

    def down_proj(panels, lhs_fn, lhs_keys, NS, acc_first=True):
        for n, plist in enumerate(panels):
            banks = [next_pf() for _ in range(NS)]
            tot = sum(kc for _, kc, _ in plist)
            done = 0
            for (src3, kc, kbase) in plist:
                w, wk = wpanel(src3, kc, 512)
                for sub in range(NS):
                    pb, pk = banks[sub]

                    def f(en, w=w, kc=kc, kbase=kbase, sub=sub, pb=pb, done=done):
                        ins = None
                        for k in range(kc):
                            ins = en.matmul(pb[:, :], lhs_fn(kbase + k, sub), w[:, k, :],
                                            start=(done + k == 0), stop=(done + k == tot - 1))
                        return ins
                    P.op('pe', f, reads=[wk] + list(lhs_keys), writes=[pk])
                done += kc
            for sub in range(NS):
                pb, pk = banks[sub]
                dst = y_sb[:, sub, n * 512:(n + 1) * 512]
                if acc_first:
                    P.op('act', lambda en, dst=dst, pb=pb: en.activation(out=dst, in_=pb[:, :], func=AF.Identity),
                         reads=[pk], writes=[('y', sub)])
                else:
                    P.op('dve', lambda en, dst=dst, pb=pb: en.tensor_tensor(out=dst, in0=pb[:, :], in1=dst, op=ALU.add),
                         reads=[pk, ('y', sub)], writes=[('y', sub)])

    HF = NFC // 2
    actT = P.sb("actT", [128, HF, T], BF16, at=R0)
    gp = [P.sb("gp%d" % i, [128, T + 8], F32, at=R0 + HF * T * 2 + i * (T + 8) * 4) for i in range(2)]
    ftmp_off = R0 + HF * T * 2 + 2 * (T + 8) * 4
    ft = [P.sb("ft%d" % i, [128, T], F32, at=ftmp_off + i * T * 4) for i in range(3)]
    FFN_END = ftmp_off + 3 * T * 4
    assert FFN_END <= (16512 + 212000), FFN_END
    FFN_KEYS = ['actT', ('gp', 0), ('gp', 1), ('ft', 0), ('ft', 1), ('ft', 2)]

    def ffn(layer, NS, NB, cst, cst_key, out_fc):
        TT = NS * 128
        L = TT // NB
        prenorm(layer * 4 + 2, NS)
        wi = s_f_w_in[layer]
        wd = s_f_wd[layer]
        for half in range(2):
            for cp in range(0, HF, 4):
                ncnk = min(4, HF - cp)
                c_abs = half * HF + cp
                wg, wgk = wpanel(wsrc(wi, 0, 16, c_abs * 128, ncnk * 128), 16, ncnk * 128)
                wu, wuk = wpanel(wsrc(wi, 0, 16, DFF + c_abs * 128, ncnk * 128), 16, ncnk * 128)
                for m in range(ncnk):
                    ci = cp + m
                    ca = c_abs + m
                    pg, pgk = next_pf()
                    pu, puk = next_pf()

                    def f(en, w=wg, pb=pg, m=m):
                        ins = None
                        for k in range(16):
                            ins = en.matmul(pb[:, 0:TT], w[:, k, m * 128:(m + 1) * 128], hT[:, k, 0:TT],
                                            start=(k == 0), stop=(k == 15))
                        return ins
                    P.op('pe', f, reads=[wgk] + [('hT', sb_) for sb_ in range(NS)], writes=[pgk])

                    def f2(en, w=wu, pb=pu, m=m):
                        ins = None
                        for k in range(16):
                            ins = en.matmul(pb[:, 0:TT], w[:, k, m * 128:(m + 1) * 128], hT[:, k, 0:TT],
                                            start=(k == 0), stop=(k == 15))
                        return ins
                    P.op('pe', f2, reads=[wuk] + [('hT', sb_) for sb_ in range(NS)], writes=[puk])
                    gi = ca % 2
                    g = gp[gi]
                    gk = ('gp', gi)
                    g3 = g[:, 0:NB * (L + 2)].rearrange("p (b l) -> p b l", l=L + 2)
                    P.op('pool', lambda en, g3=g3, ca=ca: en.tensor_copy(out=g3[:, :, 0:2], in_=cst(ca)),
                         reads=[cst_key], writes=[gk])
                    P.op('act', lambda en, g3=g3, pg=pg: en.activation(
                        out=g3[:, :, 2:L + 2], in_=pg[:, 0:TT].rearrange("p (b l) -> p b l", l=L), func=AF.Identity),
                        reads=[pgk], writes=[gk])
                    P.op('pool', lambda en, g3=g3, ca=ca: en.tensor_copy(out=cst(ca), in_=g3[:, :, L:L + 2]),
                         reads=[gk], writes=[cst_key])
                    t0 = ft[0][:, 0:TT].rearrange("p (b l) -> p b l", l=L)
                    t1 = ft[1][:, 0:TT].rearrange("p (b l) -> p b l", l=L)
                    t2 = ft[2][:, 0:TT]
                    P.op('dve', lambda en, g3=g3, ca=ca, t0=t0: en.tensor_scalar(
                        out=t0, in0=g3[:, :, 0:L], scalar1=cw_sb[:, layer, 0, ca:ca + 1], scalar2=cb_sb[:, layer, ca:ca + 1],
                        op0=ALU.mult, op1=ALU.add), reads=[gk, 'cw'], writes=[('ft', 0)])
                    P.op('dve', lambda en, g3=g3, ca=ca, t0=t0, t1=t1: en.scalar_tensor_tensor(
                        out=t1, in0=g3[:, :, 1:L + 1], scalar=cw_sb[:, layer, 1, ca:ca + 1], in1=t0,
                        op0=ALU.mult, op1=ALU.add), reads=[gk, 'cw', ('ft', 0)], writes=[('ft', 1)])
                    P.op('dve', lambda en, g3=g3, ca=ca, t0=t0, t1=t1: en.scalar_tensor_tensor(
                        out=t0, in0=g3[:, :, 2:L + 2], scalar=cw_sb[:, layer, 2, ca:ca + 1], in1=t1,
                        op0=ALU.mult, op1=ALU.add), reads=[gk, 'cw', ('ft', 1), ('ft', 0)], writes=[('ft', 0)])
                    P.op('act', lambda en, t2=t2: en.activation(out=t2, in_=ft[0][:, 0:TT], func=AF.Silu),
                         reads=[('ft', 0)], writes=[('ft', 2)])
                    P.op('dve', lambda en, ci=ci, pu=pu, t2=t2: en.tensor_tensor(
                        out=actT[:, ci, 0:TT], in0=pu[:, 0:TT], in1=t2, op=ALU.mult),
                        reads=[puk, ('ft', 2)], writes=['actT'])
            panels = []
            for n in range(4):
                pl = []
                k0 = 0
                while k0 < HF:
                    kc = min(16, HF - k0)
                    pl.append((wsrc(wd, (half * HF + k0) * 128, kc, n * 512, 512), kc, k0))
                    k0 += kc
                panels.append(pl)
            down_proj(panels, lambda k, sub: actT[:, k, sub * 128:(sub + 1) * 128], ['actT'], NS, acc_first=(half == 0))
        if out_fc is not None:
            out_fc()
        postnorm_residual(layer * 4 + 3, NS)

    o = R0
    def ralloc(name, shape, dt):
        nonlocal o
        nbytes = (int(np.prod(shape[1:])) * (4 if dt == F32 else 2) + 63) // 64 * 64
        t_ = P.sb(name, shape, dt, at=o)
        o += nbytes
        assert o <= (16512 + 212000), (name, o)
        return t_
    a_sb = ralloc("a_sb", [128, 512], F32)
    a_bf = ralloc("a_bf", [128, 512], BF16)
    kpe_f = ralloc("kpe_f", [128, 64], F32)
    kpe_t = ralloc("kpe_t", [128, 64], F32)
    kpe_bf = ralloc("kpe_bf", [128, 64], BF16)
    cqT = ralloc("cqT", [128, 4, T], BF16)
    ckvT = ralloc("ckvT", [128, 4, T], BF16)
    kpeT = ralloc("kpeT", [64, T], BF16)
    qnT = ralloc("qnT", [128, T], BF16)
    qpeT = ralloc("qpeT", [64, T], BF16)
    rt0 = ralloc("rt0", [64, T], F32)
    rt1 = ralloc("rt1", [64, T], F32)
    oT = ralloc("oT", [128, AH, T], BF16)
    NKV = 3
    kvK = [ralloc("kvK%d" % i, [128, 512], BF16) for i in range(NKV)]
    kvP = [ralloc("kvP%d" % i, [64, 512], BF16) for i in range(NKV)]
    kvV = [ralloc("kvV%d" % i, [128, 4, 128], BF16) for i in range(NKV)]
    PT = [ralloc("PT%d" % i, [128, T], BF16) for i in range(3)]
    recip = ralloc("recip", [128, T], F32)
    kn_bf = [ralloc("kn_bf%d" % i, [128, T], BF16) for i in range(2)]
    MLA_END = o
    MLA_KEYS = ['a_sb', 'a_bf', 'kpe_f', 'kpe_t', 'kpe_bf', 'cqT', 'ckvT', 'kpeT', 'qnT', 'qpeT', 'rt0', 'rt1', 'oT',
                'recip', ('kn', 0), ('kn', 1)] + [('kv', i) for i in range(NKV)] + [('PT', i) for i in range(3)]
    print("MLA region end", MLA_END, "FFN end", FFN_END)

    kvstate = {'n': 0}
    ptstate = {'n': 0}

    def make_kv(la, ckv_src_keys, nk, kt_dst, v_dst, col0):
        for hp in range(0, AH, 4):
            w, wk = wpanel(wsrc(s_w_uk[la], 0, 4, hp * 128, 512), 4, 512)
            for m in range(4):
                h = hp + m
                pb, pk = next_pf()

                def f(en, w=w, pb=pb, m=m):
                    ins = None
                    for k in range(4):
                        ins = en.matmul(pb[:, 0:nk], w[:, k, m * 128:(m + 1) * 128], ckvT[:, k, 0:nk],
                                        start=(k == 0), stop=(k == 3))
                    return ins
                P.op('pe', f, reads=[wk] + list(ckv_src_keys), writes=[pk])
                i = h % 2
                P.op('act', lambda en, pb=pb, i=i: en.activation(out=kn_bf[i][:, 0:nk], in_=pb[:, 0:nk], func=AF.Identity),
                     reads=[pk], writes=[('kn', i)])
                P.dma('pool', kt_dst[h, :, col0:col0 + nk], kn_bf[i][:, 0:nk], 'kn%d' % i, reads=[('kn', i)], writes=['kvscr'])
        nsub = (nk + 127) // 128
        for hp in range(0, AH, 4):
            w, wk = wpanel(wsrc(s_w_uv[la], 0, 4, hp * 128, 512), 4, 512)
            for sub in range(nsub):
                rows = min(128, nk - sub * 128)
                pb, pk = next_pf()

                def f(en, w=w, pb=pb, sub=sub, rows=rows):
                    ins = None
                    for k in range(4):
                        ins = en.matmul(pb[0:rows, :], ckvT[:, k, sub * 128:sub * 128 + rows], w[:, k, :],
                                        start=(k == 0), stop=(k == 3))
                    return ins
                P.op('pe', f, reads=[wk] + list(ckv_src_keys), writes=[pk])
                i = (sub + hp // 4) % 2
                P.op('dve', lambda en, pb=pb, i=i, rows=rows: en.tensor_copy(out=kn_bf[i][0:rows, :], in_=pb[0:rows, :]),
                     reads=[pk], writes=[('kn', i)])
                P.dma('pool', v_dst[hp:hp + 4, col0 + sub * 128:col0 + sub * 128 + rows, :].rearrange("h t v -> t h v"),
                      kn_bf[i][0:rows, :].rearrange("t (h v) -> t h v", v=128), 'kn%d' % i,
                      reads=[('kn', i)], writes=['kvscr'])

    def attend(ncols, qn_ap, qpe_ap, qkeys, kblocks, out_fn, outkeys, scale):
        o_ps, o_k = pf[4], ('pf', 4)
        s_ps, s_k = pf[5], ('pf', 5)
        first = True
        nb = len(kblocks)
        for bi, (kt_src, kpe_src, v_src, nk, diag) in enumerate(kblocks):
            i = kvstate['n'] % NKV
            kvstate['n'] += 1
            nch = (nk + 127) // 128
            P.dma('sp', kvK[i][:, 0:nk], kt_src, 'kvK%d' % i, reads=['kvscr'], writes=[('kv', i)])
            P.dma('sp', kvP[i][:, 0:nk], kpe_src, 'kvP%d' % i, reads=['kvscr'], writes=[('kv', i)])
            if nk % 128 == 0:
                P.dma('sp', kvV[i][:, 0:nch, :], v_src.rearrange("(c p) v -> p c v", p=128), 'kvV%d' % i,
                      reads=['kvscr'], writes=[('kv', i)])
            else:
                P.dma('sp', kvV[i][0:nk, 0, :], v_src, 'kvV%d' % i, reads=['kvscr'], writes=[('kv', i)])
            for c in range(nch):
                rows = min(128, nk - c * 128)
                q0 = c * 128 if diag else 0
                sp_, sk = next_pf(0, 4)

                def f(en, i=i, c=c, rows=rows, q0=q0, sp_=sp_):
                    en.matmul(sp_[0:rows, q0:ncols], kvK[i][:, c * 128:c * 128 + rows], qn_ap[:, q0:ncols], start=True, stop=False)
                    return en.matmul(sp_[0:rows, q0:ncols], kvP[i][:, c * 128:c * 128 + rows], qpe_ap[:, q0:ncols], start=False, stop=True)
                P.op('pe', f, reads=[('kv', i)] + list(qkeys), writes=[sk])
                pi = ptstate['n'] % 3
                ptstate['n'] += 1
                pt = PT[pi]
                P.op('act', lambda en, pt=pt, sp_=sp_, rows=rows, q0=q0: en.activation(
                    out=pt[0:rows, q0:ncols], in_=sp_[0:rows, q0:ncols], func=AF.Exp, scale=scale),
                    reads=[sk], writes=[('PT', pi)])
                if diag:
                    P.op('pool', lambda en, pt=pt, q0=q0: en.memset(pt[64:128, q0:q0 + 64], 0.0),
                         reads=[], writes=[('PT', pi)])

                lastmm = (bi == nb - 1 and c == nch - 1)

                def f2(en, i=i, c=c, rows=rows, q0=q0, pt=pt, first=first, lastmm=lastmm):
                    en.matmul(o_ps[:, q0:ncols], kvV[i][0:rows, c, :], pt[0:rows, q0:ncols], start=first, stop=lastmm,
                              skip_group_check=True)
                    return en.matmul(s_ps[:, q0:ncols], ones[0:rows, :], pt[0:rows, q0:ncols], start=first, stop=lastmm,
                                     skip_group_check=True)
                P.op('pe', f2, reads=[('kv', i), ('PT', pi), 'ones'], writes=[o_k, s_k])
                first = False
        P.op('dve', lambda en: en.reciprocal(out=recip[:, 0:ncols], in_=s_ps[:, 0:ncols]), reads=[s_k], writes=['recip'])
        out_fn(o_ps, o_k)

    def mla(layer, la, NS, NB, tile_idx, sample):
        TT = NS * 128
        tcol = tile_idx
        prenorm(layer * 4 + 0, NS)
        tsub = tile_idx * 4 if not sample else NT * 4
        nts = NS
        P.dma('pool', tabAc[:, 0:nts, :], tA_c[:, tsub:tsub + nts, :], 'tab', writes=['tab'])
        P.dma('pool', tabAs[:, 0:nts, :], tA_s[:, tsub:tsub + nts, :], 'tab', writes=['tab'])
        fc0 = tile_idx * T if not sample else SEQ
        P.dma('pool', tabFc[:, 0:TT], tF_c[:, fc0:fc0 + TT], 'tab', writes=['tab'])
        P.dma('pool', tabFs[:, 0:TT], tF_s[:, fc0:fc0 + TT], 'tab', writes=['tab'])
        ckpt(20)
        win = s_w_in_a[la]
        wq_, wqk = None, None
        for blk, (c0, ncol) in enumerate(((0, 512), (512, 512), (1024, 64))):
            w, wk = wpanel(wsrc(win, 0, 16, c0, ncol), 16, ncol)
            for sub in range(NS):
                pb, pk = next_pf(0, 4)

                def f(en, w=w, pb=pb, sub=sub, ncol=ncol):
                    ins = None
                    for k in range(16):
                        ins = en.matmul(pb[:, 0:ncol], hT[:, k, sub * 128:(sub + 1) * 128], w[:, k, :],
                                        start=(k == 0), stop=(k == 15))
                    return ins
                P.op('pe', f, reads=[wk, ('hT', sub)], writes=[pk])
                if blk < 2:
                    col = 4 + sub
                    sumsq(col, pb[:, 0:512], [pk], a_bf[:], 'a_bf', n=512)
                    rstd_from_ss(col, 512)
                    gsrc = gq_bc if blk == 0 else gkv_bc
                    if blk == 0:
                        P.op('dve', lambda en, pb=pb, col=col, gsrc=gsrc: en.scalar_tensor_tensor(
                            out=a_bf[:], in0=pb[:, 0:512], scalar=stat[:, col:col + 1], in1=gsrc[:, la, :],
                            op0=ALU.mult, op1=ALU.mult), reads=[pk, ('stat', col), 'gq'], writes=['a_bf'])
                        transpose_to(lambda cc, n, sub=sub: cqT[:, cc:cc + n, sub * 128:(sub + 1) * 128], a_bf[:], 4,
                                     ['a_bf'], ['cqT'])
                    else:
                        P.op('dve', lambda en, pb=pb, col=col, gsrc=gsrc: en.scalar_tensor_tensor(
                            out=a_sb[:], in0=pb[:, 0:512], scalar=stat[:, col:col + 1], in1=gsrc[:, la, :],
                            op0=ALU.mult, op1=ALU.mult), reads=[pk, ('stat', col), 'gq'], writes=['a_sb'])
                        if not sample:
                            r0 = tile_idx * T + sub * 128
                            P.dma('pool', o_ckv_p[la, r0:r0 + 128, :], a_sb[:], 'o_a', reads=['a_sb'], writes=['o_ckv'])
                        else:
                            P.dma('pool', o_ckv_s[la, :, :], a_sb[:], 'o_a', reads=['a_sb'], writes=['o_ckv'])
                        P.op('pool', lambda en: en.tensor_copy(out=a_bf[:], in_=a_sb[:]), reads=['a_sb'], writes=['a_bf'])
                        transpose_to(lambda cc, n, sub=sub: ckvT[:, cc:cc + n, sub * 128:(sub + 1) * 128], a_bf[:], 4,
                                     ['a_bf'], ['ckvT'])
                else:
                    P.op('dve', lambda en, pb=pb, sub=sub: en.tensor_tensor(out=kpe_f[:, 0:32], in0=pb[:, 0:32], in1=tabAc[:, sub, :], op=ALU.mult),
                         reads=[pk, 'tab'], writes=['kpe_f'])
                    P.op('dve', lambda en, pb=pb, sub=sub: en.tensor_tensor(out=kpe_t[:, 0:32], in0=pb[:, 32:64], in1=tabAs[:, sub, :], op=ALU.mult),
                         reads=[pk, 'tab'], writes=['kpe_t'])
                    P.op('dve', lambda en, pb=pb, sub=sub: en.tensor_tensor(out=kpe_f[:, 32:64], in0=pb[:, 0:32], in1=tabAs[:, sub, :], op=ALU.mult),
                         reads=[pk, 'tab'], writes=['kpe_f'])
                    P.op('dve', lambda en, pb=pb, sub=sub: en.tensor_tensor(out=kpe_t[:, 32:64], in0=pb[:, 32:64], in1=tabAc[:, sub, :], op=ALU.mult),
                         reads=[pk, 'tab'], writes=['kpe_t'])
                    P.op('dve', lambda en: en.tensor_tensor(out=kpe_f[:, 0:32], in0=kpe_f[:, 0:32], in1=kpe_t[:, 0:32], op=ALU.subtract),
                         reads=['kpe_f', 'kpe_t'], writes=['kpe_f'])
                    P.op('dve', lambda en: en.tensor_tensor(out=kpe_f[:, 32:64], in0=kpe_f[:, 32:64], in1=kpe_t[:, 32:64], op=ALU.add),
                         reads=['kpe_f', 'kpe_t'], writes=['kpe_f'])
                    if not sample:
                        r0 = tile_idx * T + sub * 128
                        P.dma('pool', o_kpe_p[la, r0:r0 + 128, :], kpe_f[:], 'o_k', reads=['kpe_f'], writes=['o_kpe'])
                    else:
                        P.dma('pool', o_kpe_s[la, :, :], kpe_f[:], 'o_k', reads=['kpe_f'], writes=['o_kpe'])
                    P.op('pool', lambda en: en.tensor_copy(out=kpe_bf[:], in_=kpe_f[:]), reads=['kpe_f'], writes=['kpe_bf'])
                    transpose_to(lambda cc, n, sub=sub: kpeT[:, sub * 128:(sub + 1) * 128].rearrange("p (c t) -> p c t", c=1),
                                 kpe_bf[:], 1, ['kpe_bf'], ['kpeT'], width=64)
        ckpt(21)
        if not sample:
            c0 = tile_idx * T
            make_kv(la, ['ckvT'], TT, kt_p[la], v_p[la], c0)
            P.dma('pool', kpe_p[la][:, c0:c0 + TT], kpeT[:, 0:TT], 'kpw', reads=['kpeT'], writes=['kvscr'])
        else:
            for b in range(4):
                pass
            make_kv_sample_new(la)
        ckpt(22)
        for h in range(AH):
            w, wk = wpanel(s_w_qb[la][:, h, :].rearrange("(k p) n -> p k n", p=128), 4, 256)
            pn, pnk = next_pf(0, 4)
            pa, pak = next_pf(0, 4)
            pb2, pbk = next_pf(0, 4)

            def f(en, w=w, pn=pn, pa=pa, pb2=pb2):
                ins = None
                for (pp, cc0, mm) in ((pn, 0, 128), (pa, 128, 64), (pb2, 192, 64)):
                    for k in range(4):
                        ins = en.matmul(pp[0:mm, 0:TT], w[:, k, cc0:cc0 + mm], cqT[:, k, 0:TT], start=(k == 0), stop=(k == 3))
                return ins
            P.op('pe', f, reads=[wk, 'cqT'], writes=[pnk, pak, pbk])
            P.op('act', lambda en, pn=pn: en.activation(out=qnT[:, 0:TT], in_=pn[:, 0:TT], func=AF.Identity), reads=[pnk], writes=['qnT'])
            P.op('dve', lambda en, pa=pa: en.tensor_tensor(out=rt0[:, 0:TT], in0=pa[0:64, 0:TT], in1=tabFc[:, 0:TT], op=ALU.mult),
                 reads=[pak, 'tab'], writes=['rt0'])
            P.op('dve', lambda en, pb2=pb2: en.tensor_tensor(out=rt1[:, 0:TT], in0=pb2[0:64, 0:TT], in1=tabFs[:, 0:TT], op=ALU.mult),
                 reads=[pbk, 'tab'], writes=['rt1'])
            P.op('pool', lambda en: en.tensor_tensor(out=qpeT[:, 0:TT], in0=rt0[:, 0:TT], in1=rt1[:, 0:TT], op=ALU.add),
                 reads=['rt0', 'rt1'], writes=['qpeT'])
            ckpt(23)
            if not sample:
                kb = []
                for j in range(tile_idx + 1):
                    kb.append((kt_p[la, h, :, j * T:(j + 1) * T], kpe_p[la, :, j * T:(j + 1) * T],
                               v_p[la, h, j * T:(j + 1) * T, :], T, j == tile_idx))

                def outf(o_ps, o_k, h=h):
                    P.op('dve', lambda en: en.tensor_tensor(out=oT[:, h, 0:TT], in0=o_ps[:, 0:TT], in1=recip[:, 0:TT], op=ALU.mult),
                         reads=[o_k, 'recip'], writes=['oT'])
                attend(TT, qnT, qpeT, ['qnT', 'qpeT'], kb, outf, ['oT'], A_SCALE)
                ckpt(24)
            else:
                for b in range(4):
                    kb = []
                    for j in range(0, KLS, T):
                        nk = min(T, KLS - j)
                        kb.append((kt_s[la, b, h, :, j:j + nk], kpe_s[la, b, :, j:j + nk], v_s[la, b, h, j:j + nk, :], nk, False))

                    def outf(o_ps, o_k, h=h, b=b):
                        P.op('dve', lambda en: en.tensor_tensor(out=oT[:, h, b * 32:(b + 1) * 32], in0=o_ps[:, 0:32], in1=recip[:, 0:32], op=ALU.mult),
                             reads=[o_k, 'recip'], writes=['oT'])
                    attend(32, qnT[:, b * 32:(b + 1) * 32], qpeT[:, b * 32:(b + 1) * 32], ['qnT', 'qpeT'], kb, outf, ['oT'], A_SCALE)
        ckpt(25)
        wo = s_w_o_a[la]
        panels = [[(wsrc(wo, 0, 16, n * 512, 512), 16, 0)] for n in range(4)]
        down_proj(panels, lambda k, sub: oT[:, k, sub * 128:(sub + 1) * 128], ['oT'], NS)
        postnorm_residual(layer * 4 + 1, NS)

    def make_kv_sample_new(la):
        for hp in range(0, AH, 4):
            w, wk = wpanel(wsrc(s_w_uk[la], 0, 4, hp * 128, 512), 4, 512)
            for m in range(4):
                h = hp + m
                pb, pk = next_pf(0, 4)

                def f(en, w=w, pb=pb, m=m):
                    ins = None
                    for k in range(4):
                        ins = en.matmul(pb[:, 0:128], w[:, k, m * 128:(m + 1) * 128], ckvT[:, k, 0:128], start=(k == 0), stop=(k == 3))
                    return ins
                P.op('pe', f, reads=[wk, 'ckvT'], writes=[pk])
                i = h % 2
                P.op('act', lambda en, pb=pb, i=i: en.activation(out=kn_bf[i][:, 0:128], in_=pb[:, 0:128], func=AF.Identity),
                     reads=[pk], writes=[('kn', i)])
                P.dma('pool', kt_s[la, :, h, :, PAST:PAST + 32].rearrange("b p t -> p b t"),
                      kn_bf[i][:, 0:128].rearrange("p (b t) -> p b t", t=32), 'kn%d' % i, reads=[('kn', i)], writes=['kvscr'])
        for hp in range(0, AH, 4):
            w, wk = wpanel(wsrc(s_w_uv[la], 0, 4, hp * 128, 512), 4, 512)
            pb, pk = next_pf(0, 4)

            def f(en, w=w, pb=pb):
                ins = None
                for k in range(4):
                    ins = en.matmul(pb[:, :], ckvT[:, k, 0:128], w[:, k, :], start=(k == 0), stop=(k == 3))
                return ins
            P.op('pe', f, reads=[wk, 'ckvT'], writes=[pk])
            i = (hp // 4) % 2
            P.op('dve', lambda en, pb=pb, i=i: en.tensor_copy(out=kn_bf[i][:, :], in_=pb[:, :]), reads=[pk], writes=[('kn', i)])
            for b in range(4):
                P.dma('pool', v_s[la, b, hp:hp + 4, PAST:PAST + 32, :].rearrange("h t v -> t h v"),
                      kn_bf[i][b * 32:(b + 1) * 32, :].rearrange("t (h v) -> t h v", v=128), 'kn%d' % i,
                      reads=[('kn', i)], writes=['kvscr'])
        P.dma('pool', kpe_s[la, :, :, PAST:PAST + 32].rearrange("b p t -> p b t"),
              kpeT[:, 0:128].rearrange("p (b t) -> p b t", t=32), 'kpw', reads=['kpeT'], writes=['kvscr'])

    def sample_past_kv(la):
        for b in range(4):
            for j in range(0, PAST, T):
                for sub in range(4):
                    r0 = j + sub * 128
                    P.dma('pool', a_sb[:], cckv[la, b, r0:r0 + 128, :], 'ld_a', writes=['a_sb'])
                    P.op('pool', lambda en: en.tensor_copy(out=a_bf[:], in_=a_sb[:]), reads=['a_sb'], writes=['a_bf'])
                    transpose_to(lambda cc, n, sub=sub: ckvT[:, cc:cc + n, sub * 128:(sub + 1) * 128], a_bf[:], 4,
                                 ['a_bf'], ['ckvT'])
                    P.dma('pool', kpe_f[:], ckpe[la, b, r0:r0 + 128, :], 'ld_k', writes=['kpe_f'])
                    P.op('pool', lambda en: en.tensor_copy(out=kpe_bf[:], in_=kpe_f[:]), reads=['kpe_f'], writes=['kpe_bf'])
                    transpose_to(lambda cc, n, sub=sub: kpeT[:, sub * 128:(sub + 1) * 128].rearrange("p (c t) -> p c t", c=1),
                                 kpe_bf[:], 1, ['kpe_bf'], ['kpeT'], width=64)
                make_kv(la, ['ckvT'], T, kt_s[la, b], v_s[la, b], j)
                P.dma('pool', kpe_s[la, b][:, j:j + T], kpeT[:, 0:T], 'kpw', reads=['kpeT'], writes=['kvscr'])

    o = R0
    class _QB:
        def __getitem__(self, idx):
            p, sub, cols = idx
            return y_sb[:, sub, :].bitcast(BF16)[:, cols]
    q_bf = _QB()
    kv_f = ralloc("kv_f", [128, 1024], F32)
    kv_bf = ralloc("kv_bf", [128, 1024], BF16)
    rr = [ralloc("rr%d" % i, [128, 8, 8], F32) for i in range(4)]
    qT = ralloc("qT", [64, 8, 512], BF16)
    kTc = ralloc("kTc", [64, 8, 128], BF16)
    sPT = [ralloc("sPT%d" % i, [128, 512], BF16) for i in range(2)]
    sden = ralloc("sden", [64, 512], F32)
    oTs = ralloc("oTs", [64, BH, T], BF16)
    SWA_END = o
    SWA_KEYS = ['kv_f', 'kv_bf', 'rr', 'qT', 'kTc', ('sPT', 0), ('sPT', 1), 'sden', 'oTs']
    print("SWA region end", SWA_END)

    def rope_tm(pb, pk, nh, sub, dst, dstkey, dcol0):
        src = pb[:, 0:nh * 64].rearrange("p (h d) -> p h d", d=64)
        d3 = dst[:, dcol0:dcol0 + nh * 64].rearrange("p (h d) -> p h d", d=64)
        cb = tabBc[:, sub, :].rearrange("p (o d) -> p o d", o=1).to_broadcast([128, nh, 8])
        sbb = tabBs[:, sub, :].rearrange("p (o d) -> p o d", o=1).to_broadcast([128, nh, 8])
        r = [rr[i][:, 0:nh, :] for i in range(4)]
        P.op('dve', lambda en: en.tensor_tensor(out=r[0], in0=src[:, :, 0:8], in1=cb, op=ALU.mult), reads=[pk, 'tab', dstkey], writes=['rr'])
        P.op('dve', lambda en: en.tensor_tensor(out=r[1], in0=src[:, :, 8:16], in1=sbb, op=ALU.mult), reads=[pk, 'tab', dstkey], writes=['rr'])
        P.op('dve', lambda en: en.tensor_tensor(out=r[2], in0=src[:, :, 0:8], in1=sbb, op=ALU.mult), reads=[pk, 'tab', dstkey], writes=['rr'])
        P.op('dve', lambda en: en.tensor_tensor(out=r[3], in0=src[:, :, 8:16], in1=cb, op=ALU.mult), reads=[pk, 'tab', dstkey], writes=['rr'])
        P.op('dve', lambda en: en.tensor_tensor(out=d3[:, :, 0:8], in0=r[0], in1=r[1], op=ALU.subtract), reads=['rr'], writes=[dstkey])
        P.op('dve', lambda en: en.tensor_tensor(out=d3[:, :, 8:16], in0=r[2], in1=r[3], op=ALU.add), reads=['rr'], writes=[dstkey])

    def swa(layer, lb, NS, tile_idx, sample):
        ckpt(30)
        prenorm(layer * 4 + 0, NS)
        tsub = tile_idx * 4 if not sample else NT * 4
        P.dma('pool', tabBc[:, 0:NS, :], tB_c[:, tsub:tsub + NS, :], 'tab', writes=['tab'])
        P.dma('pool', tabBs[:, 0:NS, :], tB_s[:, tsub:tsub + NS, :], 'tab', writes=['tab'])
        wq = s_w_qkv[lb]
        for n in range(4):
            w, wk = wpanel(wsrc(wq, 0, 16, n * 512, 512), 16, 512)
            for sub in range(NS):
                pb, pk = next_pf()

                def f(en, w=w, pb=pb, sub=sub):
                    ins = None
                    for k in range(16):
                        ins = en.matmul(pb[:, :], hT[:, k, sub * 128:(sub + 1) * 128], w[:, k, :], start=(k == 0), stop=(k == 15))
                    return ins
                P.op('pe', f, reads=[wk, ('hT', sub)], writes=[pk])
                P.op('act', lambda en, pb=pb, sub=sub, n=n: en.activation(out=q_bf[:, sub, n * 512:(n + 1) * 512], in_=pb[:, :], func=AF.Identity),
                     reads=[pk], writes=[('y', sub)])
                rope_tm(pb, pk, 8, sub, q_bf[:, sub, :], ('y', sub), n * 512)
        ckpt(31)
        for sub in range(NS):
            wkp, wkk = wpanel(wsrc(wq, 0, 16, 2048, 512), 16, 512)
            pbk_, pkk = next_pf()

            def f(en, w=wkp, pb=pbk_, sub=sub):
                ins = None
                for k in range(16):
                    ins = en.matmul(pb[:, :], hT[:, k, sub * 128:(sub + 1) * 128], w[:, k, :], start=(k == 0), stop=(k == 15))
                return ins
            P.op('pe', f, reads=[wkk, ('hT', sub)], writes=[pkk])
            wvp, wvk = wpanel(wsrc(wq, 0, 16, 2560, 512), 16, 512)
            pbv, pkv = next_pf()

            def f2(en, w=wvp, pb=pbv, sub=sub):
                ins = None
                for k in range(16):
                    ins = en.matmul(pb[:, :], hT[:, k, sub * 128:(sub + 1) * 128], w[:, k, :], start=(k == 0), stop=(k == 15))
                return ins
            P.op('pe', f2, reads=[wvk, ('hT', sub)], writes=[pkv])
            P.op('act', lambda en, pb=pbk_: en.activation(out=kv_f[:, 0:512], in_=pb[:, :], func=AF.Identity), reads=[pkk], writes=['kv_f'])
            rope_tm(pbk_, pkk, 8, sub, kv_f[:, :], 'kv_f', 0)
            P.op('act', lambda en, pb=pbv: en.activation(out=kv_f[:, 512:1024], in_=pb[:, :], func=AF.Identity), reads=[pkv], writes=['kv_f'])
            P.op('pool', lambda en: en.tensor_copy(out=kv_bf[:], in_=kv_f[:]), reads=['kv_f'], writes=['kv_bf'])
            transpose_to(lambda cc, n: kTc[:, cc:cc + n, :], kv_bf[:, 0:512], 8, ['kv_bf'], ['kTc'], width=64)
            transpose_to(lambda cc, n: qT[:, cc // 4:(cc + n) // 4, :].rearrange("p g (h t) -> p (g h) t", t=128),
                         q_bf[:, sub, :], 32, [('y', sub)], ['qT'], width=64)
            ckpt(32)
            if not sample:
                last = (tile_idx == NT - 1 and sub == NS - 1)
                if last:
                    P.dma('pool', o_wk_p[lb], kv_f[:, 0:512], 'o_w', reads=['kv_f'], writes=['o_wk'])
                    P.dma('pool', o_wv_p[lb], kv_f[:, 512:1024], 'o_w', reads=['kv_f'], writes=['o_wk'])
                has_prev = not (tile_idx == 0 and sub == 0)
                for g in range(BKV):
                    o_ps, o_k = next_pf()
                    d_ps, d_k = next_pf()
                    blocks = []
                    if has_prev:
                        blocks.append((kTprev[:, lb, g, :], vprev[:, lb, g * 64:(g + 1) * 64], ['kprev'], True))
                    blocks.append((kTc[:, g, :], kv_bf[:, 512 + g * 64:512 + (g + 1) * 64], ['kTc', 'kv_bf'], False))
                    for bi, (kt_ap, v_ap, kkeys, isprev) in enumerate(blocks):
                        sp_, sk = next_pf()
                        P.op('pe', lambda en, sp_=sp_, kt_ap=kt_ap, g=g: en.matmul(sp_[:, :], kt_ap, qT[:, g, :], start=True, stop=True),
                             reads=kkeys + ['qT'], writes=[sk])
                        pi = ptstate['n'] % 2
                        ptstate['n'] += 1
                        pt = sPT[pi]
                        P.op('act', lambda en, pt=pt, sp_=sp_: en.activation(out=pt[:, :], in_=sp_[:, :], func=AF.Exp, scale=B_SCALE),
                             reads=[sk], writes=[('sPT', pi)])
                        pt3 = pt[:, :].rearrange("p (h t) -> p h t", t=128)
                        if isprev:
                            P.op('pool', lambda en, pt3=pt3: en.memset(pt3[0:64, :, 64:128], 0.0), writes=[('sPT', pi)])
                        else:
                            P.op('pool', lambda en, pt3=pt3: en.memset(pt3[64:128, :, 0:64], 0.0), writes=[('sPT', pi)])

                        def f3(en, pt=pt, v_ap=v_ap, bi=bi, nb=len(blocks), o_ps=o_ps, d_ps=d_ps):
                            en.matmul(o_ps[0:64, :], v_ap, pt[:, :], start=(bi == 0), stop=(bi == nb - 1), skip_group_check=True)
                            return en.matmul(d_ps[0:64, :], ones[:, 0:64], pt[:, :], start=(bi == 0), stop=(bi == nb - 1), skip_group_check=True)
                        P.op('pe', f3, reads=kkeys + [('sPT', pi), 'ones'], writes=[o_k, d_k])
                    swa_finish(lb, g, o_ps, o_k, d_ps, d_k, 128, lambda hh, sub=sub: oTs[:, hh, sub * 128:(sub + 1) * 128])
                P.op('pool', lambda en: en.tensor_copy(out=kTprev[:, lb], in_=kTc[:]), reads=['kTc'], writes=['kprev'])
                P.op('pool', lambda en: en.tensor_copy(out=vprev[:, lb, :], in_=kv_bf[:, 512:1024]), reads=['kv_bf'], writes=['kprev'])
            else:
                for b in range(4):
                    P.dma('pool', cw_f[:], cwk[lb, b], 'ld_c', writes=['cw_f'])
                    P.dma('pool', o_wk_s[lb, b, 0:96, :], cwk[lb, b, 32:128, :], 'o_w', writes=['o_wk'])
                    P.dma('pool', o_wv_s[lb, b, 0:96, :], cwv[lb, b, 32:128, :], 'o_w', writes=['o_wk'])
                    P.dma('pool', o_wk_s[lb, b, 96:128, :], kv_f[b * 32:(b + 1) * 32, 0:512], 'o_w', reads=['kv_f'], writes=['o_wk'])
                    P.dma('pool', o_wv_s[lb, b, 96:128, :], kv_f[b * 32:(b + 1) * 32, 512:1024], 'o_w', reads=['kv_f'], writes=['o_wk'])
                    P.op('pool', lambda en: en.tensor_copy(out=vpast[:], in_=cw_f[:]), reads=['cw_f'], writes=['vpast'])
                    transpose_to(lambda cc, n: kTpast[:, cc:cc + n, :], vpast[:], 8, ['vpast'], ['kTpast'], width=64)
                    P.dma('pool', cw_f[:], cwv[lb, b], 'ld_c', reads=['vpast'], writes=['cw_f'])
                    P.op('pool', lambda en: en.tensor_copy(out=vpast[:], in_=cw_f[:]), reads=['cw_f', 'kTpast'], writes=['vpast'])
                    for g in range(BKV):
                        o_ps, o_k = next_pf()
                        d_ps, d_k = next_pf()
                        qv = qT[:, g, :].rearrange("p (h t) -> p h t", t=128)[:, :, b * 32:(b + 1) * 32]
                        blocks = [(kTpast[:, g, :], vpast[:, g * 64:(g + 1) * 64], 128, ['kTpast', 'vpast']),
                                  (kTc[:, g, :], kv_bf[:, 512 + g * 64:512 + (g + 1) * 64], 128, ['kTc', 'kv_bf'])]
                        for bi, (kt_ap, v_ap, nk, kkeys) in enumerate(blocks):
                            sp_, sk = next_pf()
                            s3 = sp_[0:nk, 0:128].rearrange("p (h t) -> p h t", t=32)
                            P.op('pe', lambda en, s3=s3, kt_ap=kt_ap, qv=qv: en.matmul(s3, kt_ap, qv, start=True, stop=True),
                                 reads=kkeys + ['qT'], writes=[sk])
                            pi = ptstate['n'] % 2
                            ptstate['n'] += 1
                            pt = sPT[pi]
                            P.op('act', lambda en, pt=pt, sp_=sp_, nk=nk: en.activation(out=pt[0:nk, 0:128], in_=sp_[0:nk, 0:128], func=AF.Exp, scale=B_SCALE),
                                 reads=[sk], writes=[('sPT', pi)])
                            if bi == 1:
                                for ob in range(4):
                                    if ob != b:
                                        P.op('pool', lambda en, pt=pt, ob=ob: en.memset(pt[ob * 32:(ob + 1) * 32, 0:128], 0.0), writes=[('sPT', pi)])

                            def f3(en, pt=pt, v_ap=v_ap, bi=bi, nk=nk, o_ps=o_ps, d_ps=d_ps):
                                en.matmul(o_ps[0:64, 0:128], v_ap, pt[0:nk, 0:128], start=(bi == 0), stop=(bi == 1), skip_group_check=True)
                                return en.matmul(d_ps[0:64, 0:128], ones[0:nk, 0:64], pt[0:nk, 0:128], start=(bi == 0), stop=(bi == 1), skip_group_check=True)
                            P.op('pe', f3, reads=kkeys + [('sPT', pi), 'ones'], writes=[o_k, d_k])
                        swa_finish(lb, g, o_ps, o_k, d_ps, d_k, 32, lambda hh, b=b: oTs[:, hh, b * 32:(b + 1) * 32])
        ckpt(33)
        wo = s_w_o_b[lb]
        for n in range(4):
            banks = [next_pf() for _ in range(NS)]
            for hq in range(0, BH, 8):
                src3 = wo[hq * 64:(hq + 8) * 64, n * 512:(n + 1) * 512].rearrange("(k p) n -> p k n", p=64)
                i = wstate['n'] % NWB
                wstate['n'] += 1
                view = wring[i][0:64, 0:8 * 512].rearrange("p (k n) -> p k n", n=512)
                P.dma('sp', view, src3, 'w%d' % i, writes=[('w', i)])
                for sub in range(NS):
                    pb, pk = banks[sub]

                    def f(en, view=view, hq=hq, sub=sub, pb=pb):
                        ins = None
                        for k in range(8):
                            ins = en.matmul(pb[:, :], oTs[:, hq + k, sub * 128:(sub + 1) * 128], view[:, k, :],
                                            start=(hq + k == 0), stop=(hq + k == BH - 1))
                        return ins
                    P.op('pe', f, reads=[('w', i), 'oTs'], writes=[pk])
            for sub in range(NS):
                pb, pk = banks[sub]
                P.op('act', lambda en, pb=pb, sub=sub, n=n: en.activation(out=y_sb[:, sub, n * 512:(n + 1) * 512], in_=pb[:, :], func=AF.Identity),
                     reads=[pk], writes=[('y', sub)])
        postnorm_residual(layer * 4 + 1, NS)

    def swa_finish(lb, g, o_ps, o_k, d_ps, d_k, nq, dst_fn):
        ncol = 4 * nq
        d3 = sden[:, 0:ncol].rearrange("p (h t) -> p h t", t=nq)
        es = esink[:, lb, g * 4:(g + 1) * 4].rearrange("p (h o) -> p h o", o=1).to_broadcast([64, 4, nq])
        P.op('dve', lambda en: en.tensor_tensor(out=d3, in0=d_ps[0:64, 0:ncol].rearrange("p (h t) -> p h t", t=nq), in1=es, op=ALU.add),
             reads=[d_k, 'esink'], writes=['sden'])
        P.op('dve', lambda en: en.reciprocal(out=sden[:, 0:ncol], in_=sden[:, 0:ncol]), reads=['sden'], writes=['sden'])
        for hh in range(4):
            P.op('dve', lambda en, hh=hh: en.tensor_tensor(out=dst_fn(g * 4 + hh), in0=o_ps[0:64, hh * nq:(hh + 1) * nq],
                                                            in1=sden[:, hh * nq:(hh + 1) * nq], op=ALU.mult),
                 reads=[o_k, 'sden'], writes=['oTs'])

    def ckpt(n):
        if STAGE == n:
            raise StopBuild()
    try:
        cur = {'keys': []}

        def phase(newkeys):
            P.fence(cur['keys'], newkeys)
            cur['keys'] = newkeys

        def fc_out(layer, sample):
            def f():
                if not sample:
                    for j in range(2):
                        P.dma('pool', o_fc_p[layer, j].rearrange("(c p) -> p c", p=128), cst_p[:, layer, :, j], 'o_fc',
                              reads=['cst_p'], writes=['o_fc'], slow=True)
                else:
                    for b in range(4):
                        for j in range(2):
                            P.dma('pool', o_fc_s[layer, b, j].rearrange("(c p) -> p c", p=128), cst_s[:, layer, b, :, j], 'o_fc',
                                  reads=['cst_s'], writes=['o_fc'], slow=True)
            return f

        def wscr_guard():
            pass
        P.fence(['wscr'], [('w', i) for i in range(NWB)])

        if STAGE == 0:
            P.emit(); return nc
        phase(MLA_KEYS)
        for la in range(2):
            sample_past_kv(la)
        if STAGE == 1:
            P.emit(); return nc

        for t in range(NT):
            for sub in range(4):
                P.dma('pool', x_sb[:, sub, :], xp[t * T + sub * 128:t * T + (sub + 1) * 128, :], 'ldx', writes=[('x', sub)])
            for layer in range(DEPTH):
                if layer % 2 == 0:
                    phase(MLA_KEYS)
                    mla(layer, layer // 2, 4, 1, t, False)
                else:
                    phase(SWA_KEYS)
                    swa(layer, layer // 2, 4, t, False)
                if STAGE == 2 + layer * 2 and t == 0:
                    P.emit(); return nc
                phase(FFN_KEYS)
                ffn(layer, 4, 1, lambda ca, layer=layer: cst_p[:, layer, ca, :].rearrange("p (b j) -> p b j", b=1), 'cst_p',
                    fc_out(layer, False) if t == NT - 1 else None)
            for sub in range(4):
                P.dma('pool', yp[t * T + sub * 128:t * T + (sub + 1) * 128, :], x_sb[:, sub, :], 'stx', reads=[('x', sub)], writes=['yp'])
        P.fence([('y', 1), ('y', 2), ('y', 3)], ['kTpast', 'vpast', 'cw_f'])
        P.dma('pool', x_sb[:, 0, :], xs[:, :], 'ldx', writes=[('x', 0)])
        for layer in range(DEPTH):
            if layer % 2 == 0:
                phase(MLA_KEYS)
                mla(layer, layer // 2, 1, 4, 0, True)
            else:
                phase(SWA_KEYS)
                swa(layer, layer // 2, 1, 0, True)
            phase(FFN_KEYS)
            ffn(layer, 1, 4, lambda ca, layer=layer: cst_s[:, layer, :, ca, :], 'cst_s', fc_out(layer, True))
        P.dma('pool', ys[:, :], x_sb[:, 0, :], 'stx', reads=[('x', 0)], writes=['ys'])
    except StopBuild:
        pass
    P.emit()
    return nc


def rope_tables(SEQ, PAST):
    NT = SEQ // T
    NTS = NT * 4 + 1
    pos_tm = np.zeros((128, NTS), np.float32)
    p = np.arange(128)
    for j in range(NT * 4):
        pos_tm[:, j] = j * 128 + p
    pos_tm[:, NT * 4] = PAST + (p % 32)

    def tabs(half):
        inv = (np.float32(THETA) ** (-np.arange(half, dtype=np.float32) / np.float32(half))).astype(np.float32)
        ang = pos_tm[:, :, None].astype(np.float32) * inv[None, None, :]
        return np.cos(ang).astype(np.float32), np.sin(ang).astype(np.float32)
    tA_c, tA_s = tabs(32)
    tB_c, tB_s = tabs(8)
    pos_f = np.concatenate([np.arange(SEQ), PAST + (np.arange(128) % 32)]).astype(np.float32)
    inv = (np.float32(THETA) ** (-np.arange(32, dtype=np.float32) / np.float32(32))).astype(np.float32)
    ang = pos_f[None, :] * inv[:, None]
    c = np.cos(ang).astype(np.float32)
    s_ = np.sin(ang).astype(np.float32)
    tF_c = np.concatenate([c, c], axis=0)
    tF_s = np.concatenate([-s_, s_], axis=0)
    return dict(tA_c=tA_c, tA_s=tA_s, tB_c=tB_c, tB_s=tB_s, tF_c=np.ascontiguousarray(tF_c), tF_s=np.ascontiguousarray(tF_s))


def run(inputs, SEQ, PAST, STAGE=99, ncores=NCORES):
    f = lambda a: np.ascontiguousarray(np.asarray(a, dtype=np.float32))
    I = {k: np.asarray(v) for k, v in inputs.items()}
    nc = build(SEQ, PAST, STAGE)
    tabs = rope_tables(SEQ, PAST)
    shared = dict(
        norm_g=f(I['norm_g'].reshape(16, D)), w_in_a=f(I['mla_w_in']), g_q=f(I['mla_g_q']), w_qb=f(I['mla_w_qb']),
        g_kv=f(I['mla_g_kv']), w_uk=f(I['mla_w_uk'].reshape(2, KVL, 2048)), w_uv=f(I['mla_w_uv'].reshape(2, KVL, 2048)),
        w_o_a=f(I['mla_w_o']), w_qkv=f(I['swa_w_qkv']), sinks=f(I['swa_sinks']), w_o_b=f(I['swa_w_o']),
        f_w_in=f(I['ffn_w_in']), f_cw=f(I['ffn_conv_w']), f_cb=f(I['ffn_conv_b']), f_wd=f(I['ffn_w_down']),
        ident=np.eye(128, dtype=np.float32), **tabs)
    in_maps = []
    for c in range(ncores):
        b = c % 4
        sb_ = slice(4 * b, 4 * b + 4)
        m = dict(shared)
        m.update(
            xp=f(I['x_prompt'][b]), xs=f(I['x_sample'][sb_].reshape(128, D)),
            cckv=f(I['cache_ckv'][:, sb_]), ckpe=f(I['cache_kpe'][:, sb_]),
            cwk=f(I['cache_win_k'][:, sb_].reshape(2, 4, 128, 512)), cwv=f(I['cache_win_v'][:, sb_].reshape(2, 4, 128, 512)),
            sfc=f(I['state_ffn_conv'][:, sb_]))
        in_maps.append(m)
    res = run_bass_kernel_spmd(nc, in_maps, core_ids=list(range(ncores)))
    R = res.results
    if ncores < 4:
        R = [R[0]] * 4
    st = lambda name, ax=0: np.stack([R[c][name] for c in range(4)], axis=ax)
    y_prompt = st('yp')
    y_sample = st('ys').reshape(16, 32, D)
    ckv_p = st('o_ckv_p', 1)
    kpe_p = st('o_kpe_p', 1)
    wk_p = st('o_wk_p', 1).reshape(2, 4, 128, BKV, BHD)
    wv_p = st('o_wv_p', 1).reshape(2, 4, 128, BKV, BHD)
    fc_p = st('o_fc_p', 1)
    ckv_s = st('o_ckv_s', 1).reshape(2, 16, 32, KVL)
    kpe_s = st('o_kpe_s', 1).reshape(2, 16, 32, ROPE)
    wk_s = np.concatenate([R[c]['o_wk_s'] for c in range(4)], axis=1).reshape(2, 16, 128, BKV, BHD)
    wv_s = np.concatenate([R[c]['o_wv_s'] for c in range(4)], axis=1).reshape(2, 16, 128, BKV, BHD)
    fc_s = np.concatenate([R[c]['o_fc_s'] for c in range(4)], axis=1)
    return (y_prompt, y_sample, ckv_p, kpe_p, wk_p, wv_p, fc_p, ckv_s, kpe_s, wk_s, wv_s, fc_s)


def kernel(**inputs):
    return run(inputs, 8192, 4096)
```

```python
import numpy as np
import concourse.bass as bass
import concourse.mybir as mybir
from concourse.bass_utils import run_bass_kernel_spmd

F32, BF16 = mybir.dt.float32, mybir.dt.bfloat16
AF = mybir.ActivationFunctionType
ALU = mybir.AluOpType

D = 2048
DEPTH = 4
CHUNK = 64
THETA = 500000.0
EPS = 1e-6
AH, QL, KVL, NOPE, ROPE, AV = 16, 512, 512, 128, 64, 128
A_SCALE = (NOPE + ROPE) ** -0.5
BH, BKV, BHD, BROT = 32, 8, 64, 16
B_SCALE = BHD ** -0.5
DFF = 5632
NFC = DFF // 128
T = 512
NCORES = 8


class StopBuild(Exception):
    pass


class Prog:
    def __init__(s, nc):
        s.nc = nc
        s.eng = {'pe': nc.tensor, 'act': nc.scalar, 'dve': nc.vector, 'pool': nc.gpsimd, 'sp': nc.sync}
        s.ops = {k: [] for k in s.eng}
        s.cnt = {k: 0 for k in ('pe', 'act', 'dve', 'pool')}
        s.esem = {k: nc.alloc_semaphore('e_' + k) for k in s.cnt}
        s.seen = {k: {} for k in s.eng}
        s.lastw = {}
        s.readers = {}
        s.dsem = {}
        s.off = 16512
        s.nps = 0

    def sb(s, name, shape, dt, at=None):
        nbytes = int(np.prod(shape[1:])) * (4 if dt == F32 else 2)
        nbytes = (nbytes + 63) // 64 * 64
        if at is None:
            at = s.off
            s.off += nbytes
            assert s.off <= (16512 + 212000), (name, s.off)
        return s.nc.alloc_sbuf_tensor_at(name, list(shape), dt, offset=at)

    def ps(s, name, shape, dt):
        return s.nc.alloc_psum_tensor(name, list(shape), dt)

    def _deps(s, e, reads, writes):
        need = {}

        def add(ev):
            if ev is None:
                return
            sem, val, src = ev
            if src == e and e == 'pe':
                return
            if need.get(sem.name, (None, 0))[1] < val:
                need[sem.name] = (sem, val)
        for k in reads:
            add(s.lastw.get(k))
        for k in writes:
            add(s.lastw.get(k))
            for ev in s.readers.get(k, {}).values():
                add(ev)
        waits = []
        for name, (sem, val) in need.items():
            if s.seen[e].get(name, 0) < val:
                s.seen[e][name] = val
                waits.append((sem, val))
        return waits

    def _commit(s, ev, reads, writes):
        for k in reads:
            s.readers.setdefault(k, {})[ev[0].name] = ev
        for k in writes:
            s.lastw[k] = ev
            s.readers[k] = {}

    def op(s, e, fn, reads=(), writes=()):
        waits = s._deps(e, reads, writes)
        s.cnt[e] += 1
        ev = (s.esem[e], s.cnt[e], e)
        s.ops[e].append((waits, fn, (s.esem[e], 1)))
        s._commit(ev, reads, writes)

    def dma(s, q, out, in_, slot, reads=(), writes=(), slow=False, throttle=3):
        waits = s._deps(q, reads, writes)
        if slot not in s.dsem:
            s.dsem[slot] = [s.nc.alloc_semaphore('d_' + slot), 0]
        rec = s.dsem[slot]
        if throttle and rec[1] - 16 * throttle > 0 and s.seen[q].get(rec[0].name, 0) < rec[1] - 16 * throttle:
            s.seen[q][rec[0].name] = rec[1] - 16 * throttle
            waits.append((rec[0], rec[1] - 16 * throttle))
        rec[1] += 16
        ev = (rec[0], rec[1], 'dma')
        if slow:
            fn = lambda en: en.dma_start(out=out, in_=in_, allow_slow_non_contiguous=True)
        else:
            fn = lambda en: en.dma_start(out=out, in_=in_)
        s.ops[q].append((waits, fn, (rec[0], 16)))
        s._commit(ev, reads, writes)

    def fence(s, old_keys, new_keys):
        evs = {}
        for k in old_keys:
            ev = s.lastw.get(k)
            if ev is not None:
                evs[ev[0].name] = max(evs.get(ev[0].name, ev), ev, key=lambda x: x[1])
            for ev in s.readers.get(k, {}).values():
                evs[ev[0].name] = max(evs.get(ev[0].name, ev), ev, key=lambda x: x[1])
        for k in new_keys:
            r = s.readers.setdefault(k, {})
            for n, ev in evs.items():
                if n not in r or r[n][1] < ev[1]:
                    r[n] = ev

    def emit(s):
        nc = s.nc
        fin = []
        for k in s.cnt:
            if s.cnt[k]:
                fin.append((s.esem[k], s.cnt[k]))
        for slot, (sem, val) in s.dsem.items():
            fin.append((sem, val))
        with nc.Block() as block:
            for name, method in (('sp', block.sync), ('act', block.scalar), ('dve', block.vector),
                                 ('pool', block.gpsimd), ('pe', block.tensor)):
                def f(en, name=name):
                    for waits, fn, inc in s.ops[name]:
                        for sem, val in waits:
                            en.wait_ge(sem, val)
                        ins = fn(en)
                        if inc is not None:
                            ins.then_inc(inc[0], inc[1])
                    if name == 'sp':
                        for sem, val in fin:
                            en.wait_ge(sem, val)
                method(f)


def build(SEQ, PAST, STAGE=99):
    NT = SEQ // T
    NTS = NT * 4 + 1
    nc = bass.Bass("TRN2", target_bir_lowering=False)
    P = Prog(nc)

    def din(name, shape, dt=F32):
        return nc.dram_tensor(name, list(shape), dt, kind="ExternalInput").ap()

    def dout(name, shape):
        return nc.dram_tensor(name, list(shape), F32, kind="ExternalOutput").ap()

    def dscr(name, shape, dt=BF16):
        return nc.dram_tensor(name, list(shape), dt, kind="Internal").ap()

    xp = din("xp", [SEQ, D]); xs = din("xs", [128, D])
    cckv = din("cckv", [2, 4, PAST, KVL]); ckpe = din("ckpe", [2, 4, PAST, ROPE])
    cwk = din("cwk", [2, 4, 128, 512]); cwv = din("cwv", [2, 4, 128, 512])
    sfc = din("sfc", [4, 4, 2, DFF])
    norm_g = din("norm_g", [16, D])
    w_in_a = din("w_in_a", [2, D, 1088]); g_q = din("g_q", [2, QL]); w_qb = din("w_qb", [2, QL, 3072])
    g_kv = din("g_kv", [2, KVL]); w_uk = din("w_uk", [2, KVL, 2048]); w_uv = din("w_uv", [2, KVL, 2048])
    w_o_a = din("w_o_a", [2, D, D]); w_qkv = din("w_qkv", [2, D, 3072]); sinks = din("sinks", [2, BH])
    w_o_b = din("w_o_b", [2, D, D]); f_w_in = din("f_w_in", [4, D, 2 * DFF]); f_cw = din("f_cw", [4, 3, DFF])
    f_cb = din("f_cb", [4, DFF]); f_wd = din("f_wd", [4, DFF, D])
    ident_d = din("ident", [128, 128])
    tA_c = din("tA_c", [128, NTS, 32]); tA_s = din("tA_s", [128, NTS, 32])
    tB_c = din("tB_c", [128, NTS, 8]); tB_s = din("tB_s", [128, NTS, 8])
    tF_c = din("tF_c", [64, SEQ + 128]); tF_s = din("tF_s", [64, SEQ + 128])

    yp = dout("yp", [SEQ, D]); ys = dout("ys", [128, D])
    o_ckv_p = dout("o_ckv_p", [2, SEQ, KVL]); o_kpe_p = dout("o_kpe_p", [2, SEQ, ROPE])
    o_wk_p = dout("o_wk_p", [2, 128, 512]); o_wv_p = dout("o_wv_p", [2, 128, 512])
    o_fc_p = dout("o_fc_p", [4, 2, DFF])
    o_ckv_s = dout("o_ckv_s", [2, 128, KVL]); o_kpe_s = dout("o_kpe_s", [2, 128, ROPE])
    o_wk_s = dout("o_wk_s", [2, 4, 128, 512]); o_wv_s = dout("o_wv_s", [2, 4, 128, 512])
    o_fc_s = dout("o_fc_s", [4, 4, 2, DFF])

    s_w_in_a = dscr("s_w_in_a", [2, D, 1088]); s_w_qb = dscr("s_w_qb", [2, QL, AH, 256])
    s_w_uk = dscr("s_w_uk", [2, KVL, 2048]); s_w_uv = dscr("s_w_uv", [2, KVL, 2048])
    s_w_o_a = dscr("s_w_o_a", [2, D, D]); s_w_qkv = dscr("s_w_qkv", [2, D, 3072])
    s_w_o_b = dscr("s_w_o_b", [2, D, D]); s_f_w_in = dscr("s_f_w_in", [4, D, 2 * DFF])
    s_f_wd = dscr("s_f_wd", [4, DFF, D])
    KL = SEQ
    KLS = PAST + 32
    kt_p = dscr("kt_p", [2, AH, 128, KL]); v_p = dscr("v_p", [2, AH, KL, 128]); kpe_p = dscr("kpe_p", [2, 64, KL])
    kt_s = dscr("kt_s", [2, 4, AH, 128, KLS]); v_s = dscr("v_s", [2, 4, AH, KLS, 128]); kpe_s = dscr("kpe_s", [2, 4, 64, KLS])

    ident = P.sb("ident", [128, 128], BF16)
    ones = P.sb("ones", [128, 128], BF16)
    cw_sb = P.sb("cw_sb", [128, 4, 3, NFC], F32)
    cb_sb = P.sb("cb_sb", [128, 4, NFC], F32)
    gq_bc = P.sb("gq_bc", [128, 2, 512], F32)
    gkv_bc = P.sb("gkv_bc", [128, 2, 512], F32)
    esink = P.sb("esink", [64, 2, BH], F32)
    tabAc = P.sb("tabAc", [128, 4, 32], F32); tabAs = P.sb("tabAs", [128, 4, 32], F32)
    tabBc = P.sb("tabBc", [128, 4, 8], F32); tabBs = P.sb("tabBs", [128, 4, 8], F32)
    tabFc = P.sb("tabFc", [64, 512], F32); tabFs = P.sb("tabFs", [64, 512], F32)
    cst_p = P.sb("cst_p", [128, 4, NFC, 2], F32)
    cst_s = P.sb("cst_s", [128, 4, 4, NFC, 2], F32)
    kTprev = P.sb("kTprev", [64, 2, 8, 128], BF16)
    vprev = P.sb("vprev", [128, 2, 512], BF16)
    stat = P.sb("stat", [128, 16], F32)
    epsb = P.sb("epsb", [128, 1], F32)
    x_sb = P.sb("x_sb", [128, 4, D], F32)
    Y_AT = P.off
    y_sb = P.sb("y_sb", [128, 4, D], F32)
    kTpast = P.sb("kTpast", [64, 8, 128], BF16, at=Y_AT + 8192)
    vpast = P.sb("vpast", [128, 512], BF16, at=Y_AT + 8192 + 2048)
    cw_f = P.sb("cw_f", [128, 512], F32, at=Y_AT + 8192 + 2048 + 1024)
    hT = P.sb("hT", [128, 16, T], BF16)
    h_bf = P.sb("h_bf", [128, D], BF16)
    gbc = P.sb("gbc", [128, D], F32)
    NWB = 2
    wring = [P.sb("wring%d" % i, [128, 16 * 512], BF16) for i in range(NWB)]
    R0 = P.off
    RSIZE = (16512 + 212000) - R0
    print("SBUF persistent bytes", R0, "region", RSIZE)

    tp = [P.ps("tp%d" % i, [128, 1024], BF16) for i in range(2)]
    pf = [P.ps("pf%d" % i, [128, 512], F32) for i in range(6)]
    tpi = [0]

    def next_tp():
        tpi[0] ^= 1
        return tp[tpi[0]], ('tp', tpi[0])
    pfi = [0]

    def next_pf(lo=0, hi=6):
        i = lo + (pfi[0] % (hi - lo))
        pfi[0] += 1
        return pf[i], ('pf', i)

    wstate = {'n': 0}

    def wpanel(src3, kc, ncols):
        i = wstate['n'] % NWB
        wstate['n'] += 1
        view = wring[i][:, 0:kc * ncols].rearrange("p (k n) -> p k n", n=ncols)
        P.dma('sp', view, src3, 'w%d' % i, writes=[('w', i)])
        return view, ('w', i)

    def wsrc(mat, r0, kc, c0, ncols):
        return mat[r0:r0 + kc * 128, c0:c0 + ncols].rearrange("(k p) n -> p k n", p=128)

    def cast_rows(dst, src, rows, step):
        for r in range(0, rows, step):
            r1 = min(rows, r + step)
            P.dma('pool', dst[r:r1], src[r:r1], 'cast', writes=['wscr'])

    for l in range(2):
        cast_rows(s_w_in_a[l], w_in_a[l], D, 1024)
        wq = w_qb[l].rearrange("r (h c) -> r h c", c=192)
        for r in range(0, QL, 128):
            P.dma('pool', s_w_qb[l][r:r + 128, :, 0:192], wq[r:r + 128], 'cast', writes=['wscr'])
            P.dma('pool', s_w_qb[l][r:r + 128, :, 192:224], wq[r:r + 128, :, 160:192], 'cast', writes=['wscr'])
            P.dma('pool', s_w_qb[l][r:r + 128, :, 224:256], wq[r:r + 128, :, 128:160], 'cast', writes=['wscr'])
        cast_rows(s_w_uk[l], w_uk[l], KVL, 512)
        cast_rows(s_w_uv[l], w_uv[l], KVL, 512)
        cast_rows(s_w_o_a[l], w_o_a[l], D, 1024)
        cast_rows(s_w_qkv[l], w_qkv[l], D, 512)
        cast_rows(s_w_o_b[l], w_o_b[l], D, 1024)
    for l in range(4):
        cast_rows(s_f_w_in[l], f_w_in[l], D, 128)
        cast_rows(s_f_wd[l], f_wd[l], DFF, 512)
    P.dma('pool', ident[:], ident_d, 'cst', writes=['ident'])
    P.op('pool', lambda en: en.memset(ones[:], 1.0), writes=['ones'])
    P.op('pool', lambda en: en.memset(epsb[:], EPS), writes=['epsb'])
    for l in range(4):
        for j in range(3):
            P.dma('pool', cw_sb[:, l, j, :], f_cw[l, j].rearrange("(c p) -> p c", p=128), 'cst', writes=['cw'], slow=True)
        P.dma('pool', cb_sb[:, l, :], f_cb[l].rearrange("(c p) -> p c", p=128), 'cst', writes=['cw'], slow=True)
        for b in range(4):
            for j in range(2):
                P.dma('pool', cst_s[:, l, b, :, j], sfc[l, b, j].rearrange("(c p) -> p c", p=128), 'cst', writes=['cst_s'], slow=True)
    for l in range(2):
        P.dma('pool', gq_bc[:, l, :], g_q[l:l + 1, :].to_broadcast([128, 512]), 'cst', writes=['gq'])
        P.dma('pool', gkv_bc[:, l, :], g_kv[l:l + 1, :].to_broadcast([128, 512]), 'cst', writes=['gq'])
        P.dma('pool', esink[:, l, :], sinks[l:l + 1, :].to_broadcast([64, BH]), 'cst', writes=['esink'])
    P.op('act', lambda en: en.activation(out=esink[:], in_=esink[:], func=AF.Exp), reads=['esink'], writes=['esink'])
    P.op('pool', lambda en: en.memset(cst_p[:], 0.0), writes=['cst_p'])

    def load_g(idx):
        P.dma('pool', gbc[:], norm_g[idx:idx + 1, :].to_broadcast([128, D]), 'gbc', writes=['gbc'])

    def rstd_from_ss(col, n):
        P.op('act', lambda en: en.activation(out=stat[:, col:col + 1], in_=stat[:, col:col + 1], func=AF.Sqrt,
                                             bias=epsb[:, 0:1], scale=1.0),
             reads=[('stat', col), 'epsb'], writes=[('stat', col)])
        P.op('dve', lambda en: en.reciprocal(out=stat[:, col:col + 1], in_=stat[:, col:col + 1]),
             reads=[('stat', col)], writes=[('stat', col)])

    def sumsq(col, src, srckeys, junk, junkkey, n=D):
        P.op('dve', lambda en: en.memset(stat[:, col:col + 1], 0.0), writes=[('stat', col)])
        P.op('act', lambda en: en.activation(out=junk, in_=src, func=AF.Square, scale=float(n ** -0.5),
                                             accum_out=stat[:, col:col + 1]),
             reads=list(srckeys) + [('stat', col)], writes=[junkkey, ('stat', col)])

    def transpose_to(dst_fn, src_bf, nchunk, srckeys, dstkeys, width=128):
        per = 1024 // 128
        for c0 in range(0, nchunk, per):
            n = min(per, nchunk - c0)
            t, tk = next_tp()

            def f(en, c0=c0, n=n, t=t):
                ins = None
                for i in range(n):
                    ins = en.transpose(out=t[0:width, i * 128:(i + 1) * 128],
                                       in_=src_bf[:, (c0 + i) * width:(c0 + i + 1) * width], identity=ident[:])
                return ins
            P.op('pe', f, reads=list(srckeys) + ['ident'], writes=[tk])
            dst = dst_fn(c0, n)
            P.op('dve', lambda en, t=t, n=n, dst=dst: en.tensor_copy(
                out=dst, in_=t[0:width, 0:n * 128].rearrange("p (c t) -> p c t", t=128)),
                reads=[tk], writes=list(dstkeys))

    def prenorm(gidx, NS):
        load_g(gidx)
        for sub in range(NS):
            sumsq(sub, x_sb[:, sub, :], [('x', sub)], y_sb[:, sub, :].bitcast(BF16)[:, 0:D], ('y', sub))
            rstd_from_ss(sub, D)
            P.op('dve', lambda en, sub=sub: en.scalar_tensor_tensor(
                out=h_bf[:], in0=x_sb[:, sub, :], scalar=stat[:, sub:sub + 1], in1=gbc[:],
                op0=ALU.mult, op1=ALU.mult), reads=[('x', sub), ('stat', sub), 'gbc'], writes=['h_bf'])
            transpose_to(lambda c0, n, sub=sub: hT[:, c0:c0 + n, sub * 128:(sub + 1) * 128], h_bf[:], 16,
                         ['h_bf'], [('hT', sub)])

    def postnorm_residual(gidx, NS):
        load_g(gidx)
        for sub in range(NS):
            sumsq(8 + sub, y_sb[:, sub, :], [('y', sub)], h_bf[:], 'h_bf')
            rstd_from_ss(8 + sub, D)
            P.op('dve', lambda en, sub=sub: en.scalar_tensor_tensor(
                out=y_sb[:, sub, :], in0=y_sb[:, sub, :], scalar=stat[:, 8 + sub:9 + sub], in1=gbc[:],
                op0=ALU.mult, op1=ALU.mult), reads=[('y', sub), ('stat', 8 + sub), 'gbc'], writes=[('y', sub)])
            P.op('dve', lambda en, sub=sub: en.tensor_tensor(
                out=x_sb[:, sub, :], in0=x_sb[:, sub, :], in1=y_sb[:, sub, :], op=ALU.add),
                reads=[('x', sub), ('y', sub)], writes=[('x', sub)])

    def down_proj(panels, lhs_fn, lhs_keys, NS, acc_first=True):
        for n, plist in enumerate(panels):
            banks = [next_pf() for _ in range(NS)]
            tot = sum(kc for _, kc, _ in plist)
            done = 0
            for (src3, kc, kbase) in plist:
                w, wk = wpanel(src3, kc, 512)
                for sub in range(NS):
                    pb, pk = banks[sub]

                    def f(en, w=w, kc=kc, kbase=kbase, sub=sub, pb=pb, done=done):
                        ins = None
                        for k in range(kc):
                            ins = en.matmul(pb[:, :], lhs_fn(kbase + k, sub), w[:, k, :],
                                            start=(done + k == 0), stop=(done + k == tot - 1))
                        return ins
                    P.op('pe', f, reads=[wk] + list(lhs_keys), writes=[pk])
                done += kc
            for sub in range(NS):
                pb, pk = banks[sub]
                dst = y_sb[:, sub, n * 512:(n + 1) * 512]
                if acc_first:
                    P.op('act', lambda en, dst=dst, pb=pb: en.activation(out=dst, in_=pb[:, :], func=AF.Identity),
                         reads=[pk], writes=[('y', sub)])
                else:
                    P.op('dve', lambda en, dst=dst, pb=pb: en.tensor_tensor(out=dst, in0=pb[:, :], in1=dst, op=ALU.add),
                         reads=[pk, ('y', sub)], writes=[('y', sub)])

    HF = NFC // 2
    actT = P.sb("actT", [128, HF, T], BF16, at=R0)
    gp = [P.sb("gp%d" % i, [128, T + 8], F32, at=R0 + HF * T * 2 + i * (T + 8) * 4) for i in range(2)]
    ftmp_off = R0 + HF * T * 2 + 2 * (T + 8) * 4
    ft = [P.sb("ft%d" % i, [128, T], F32, at=ftmp_off + i * T * 4) for i in range(3)]
    sil = [P.sb("sil%d" % i, [128, T], F32, at=ftmp_off + 3 * T * 4 + i * T * 4) for i in range(4)]
    FFN_END = ftmp_off + 7 * T * 4
    assert FFN_END <= (16512 + 212000), FFN_END
    FFN_KEYS = ['actT', ('gp', 0), ('gp', 1), ('ft', 0), ('ft', 1), ('ft', 2)] + [('sil', i) for i in range(4)]

    def ffn(layer, NS, NB, cst, cst_key, out_fc):
        TT = NS * 128
        L = TT // NB
        prenorm(layer * 4 + 2, NS)
        wi = s_f_w_in[layer]
        wd = s_f_wd[layer]
        for half in range(2):
            for cp in range(0, HF, 4):
                ncnk = min(4, HF - cp)
                c_abs = half * HF + cp
                wg, wgk = wpanel(wsrc(wi, 0, 16, c_abs * 128, ncnk * 128), 16, ncnk * 128)
                wu, wuk = wpanel(wsrc(wi, 0, 16, DFF + c_abs * 128, ncnk * 128), 16, ncnk * 128)
                for m in range(ncnk):
                    ca = c_abs + m
                    pg, pgk = next_pf()

                    def f(en, w=wg, pb=pg, m=m):
                        ins = None
                        for k in range(16):
                            ins = en.matmul(pb[:, 0:TT], w[:, k, m * 128:(m + 1) * 128], hT[:, k, 0:TT],
                                            start=(k == 0), stop=(k == 15))
                        return ins
                    P.op('pe', f, reads=[wgk] + [('hT', sb_) for sb_ in range(NS)], writes=[pgk])
                    gi = ca % 2
                    g = gp[gi]
                    gk = ('gp', gi)
                    g3 = g[:, 0:NB * (L + 2)].rearrange("p (b l) -> p b l", l=L + 2)
                    P.op('pool', lambda en, g3=g3, ca=ca: en.tensor_copy(out=g3[:, :, 0:2], in_=cst(ca)),
                         reads=[cst_key], writes=[gk])
                    P.op('act', lambda en, g3=g3, pg=pg: en.activation(
                        out=g3[:, :, 2:L + 2], in_=pg[:, 0:TT].rearrange("p (b l) -> p b l", l=L), func=AF.Identity),
                        reads=[pgk], writes=[gk])
                    P.op('pool', lambda en, g3=g3, ca=ca: en.tensor_copy(out=cst(ca), in_=g3[:, :, L:L + 2]),
                         reads=[gk], writes=[cst_key])
                    t0 = ft[0][:, 0:TT].rearrange("p (b l) -> p b l", l=L)
                    t1 = ft[1][:, 0:TT].rearrange("p (b l) -> p b l", l=L)
                    t2 = sil[m][:, 0:TT]
                    P.op('dve', lambda en, g3=g3, ca=ca, t0=t0: en.tensor_scalar(
                        out=t0, in0=g3[:, :, 0:L], scalar1=cw_sb[:, layer, 0, ca:ca + 1], scalar2=cb_sb[:, layer, ca:ca + 1],
                        op0=ALU.mult, op1=ALU.add), reads=[gk, 'cw'], writes=[('ft', 0)])
                    P.op('dve', lambda en, g3=g3, ca=ca, t0=t0, t1=t1: en.scalar_tensor_tensor(
                        out=t1, in0=g3[:, :, 1:L + 1], scalar=cw_sb[:, layer, 1, ca:ca + 1], in1=t0,
                        op0=ALU.mult, op1=ALU.add), reads=[gk, 'cw', ('ft', 0)], writes=[('ft', 1)])
                    P.op('dve', lambda en, g3=g3, ca=ca, t0=t0, t1=t1: en.scalar_tensor_tensor(
                        out=t0, in0=g3[:, :, 2:L + 2], scalar=cw_sb[:, layer, 2, ca:ca + 1], in1=t1,
                        op0=ALU.mult, op1=ALU.add), reads=[gk, 'cw', ('ft', 1), ('ft', 0)], writes=[('ft', 0)])
                    P.op('act', lambda en, t2=t2: en.activation(out=t2, in_=ft[0][:, 0:TT], func=AF.Silu),
                         reads=[('ft', 0)], writes=[('sil', m)])
                for m in range(ncnk):
                    ci = cp + m
                    pu, puk = next_pf()
                    t2 = sil[m][:, 0:TT]

                    def f2(en, w=wu, pb=pu, m=m):
                        ins = None
                        for k in range(16):
                            ins = en.matmul(pb[:, 0:TT], w[:, k, m * 128:(m + 1) * 128], hT[:, k, 0:TT],
                                            start=(k == 0), stop=(k == 15))
                        return ins
                    P.op('pe', f2, reads=[wuk] + [('hT', sb_) for sb_ in range(NS)], writes=[puk])
                    P.op('dve', lambda en, ci=ci, pu=pu, t2=t2: en.tensor_tensor(
                        out=actT[:, ci, 0:TT], in0=pu[:, 0:TT], in1=t2, op=ALU.mult),
                        reads=[puk, ('sil', m)], writes=['actT'])
            panels = []
            for n in range(4):
                pl = []
                k0 = 0
                while k0 < HF:
                    kc = min(16, HF - k0)
                    pl.append((wsrc(wd, (half * HF + k0) * 128, kc, n * 512, 512), kc, k0))
                    k0 += kc
                panels.append(pl)
            down_proj(panels, lambda k, sub: actT[:, k, sub * 128:(sub + 1) * 128], ['actT'], NS, acc_first=(half == 0))
        if out_fc is not None:
            out_fc()
        postnorm_residual(layer * 4 + 3, NS)

    o = R0
    def ralloc(name, shape, dt):
        nonlocal o
        nbytes = (int(np.prod(shape[1:])) * (4 if dt == F32 else 2) + 63) // 64 * 64
        t_ = P.sb(name, shape, dt, at=o)
        o += nbytes
        assert o <= (16512 + 212000), (name, o)
        return t_
    a_sb = ralloc("a_sb", [128, 512], F32)
    a_bf = ralloc("a_bf", [128, 512], BF16)
    kpe_f = ralloc("kpe_f", [128, 64], F32)
    kpe_t = ralloc("kpe_t", [128, 64], F32)
    kpe_bf = ralloc("kpe_bf", [128, 64], BF16)
    cqT = ralloc("cqT", [128, 4, T], BF16)
    ckvT = ralloc("ckvT", [128, 4, T], BF16)
    kpeT = ralloc("kpeT", [64, T], BF16)
    qnT = ralloc("qnT", [128, T], BF16)
    qpeT = ralloc("qpeT", [64, T], BF16)
    rt0 = ralloc("rt0", [64, T], F32)
    rt1 = ralloc("rt1", [64, T], F32)
    oT = ralloc("oT", [128, AH, T], BF16)
    NKV = 3
    kvK = [ralloc("kvK%d" % i, [128, 512], BF16) for i in range(NKV)]
    kvP = [ralloc("kvP%d" % i, [64, 512], BF16) for i in range(NKV)]
    kvV = [ralloc("kvV%d" % i, [128, 4, 128], BF16) for i in range(NKV)]
    PT = [ralloc("PT%d" % i, [128, T], BF16) for i in range(3)]
    recip = ralloc("recip", [128, T], F32)
    kn_bf = [ralloc("kn_bf%d" % i, [128, T], BF16) for i in range(2)]
    MLA_END = o
    MLA_KEYS = ['a_sb', 'a_bf', 'kpe_f', 'kpe_t', 'kpe_bf', 'cqT', 'ckvT', 'kpeT', 'qnT', 'qpeT', 'rt0', 'rt1', 'oT',
                'recip', ('kn', 0), ('kn', 1)] + [('kv', i) for i in range(NKV)] + [('PT', i) for i in range(3)]
    print("MLA region end", MLA_END, "FFN end", FFN_END)

    kvstate = {'n': 0}
    ptstate = {'n': 0}

    def make_kv(la, ckv_src_keys, nk, kt_dst, v_dst, col0):
        for hp in range(0, AH, 4):
            w, wk = wpanel(wsrc(s_w_uk[la], 0, 4, hp * 128, 512), 4, 512)
            for m in range(4):
                h = hp + m
                pb, pk = next_pf()

                def f(en, w=w, pb=pb, m=m):
                    ins = None
                    for k in range(4):
                        ins = en.matmul(pb[:, 0:nk], w[:, k, m * 128:(m + 1) * 128], ckvT[:, k, 0:nk],
                                        start=(k == 0), stop=(k == 3))
                    return ins
                P.op('pe', f, reads=[wk] + list(ckv_src_keys), writes=[pk])
                i = h % 2
                P.op('act', lambda en, pb=pb, i=i: en.activation(out=kn_bf[i][:, 0:nk], in_=pb[:, 0:nk], func=AF.Identity),
                     reads=[pk], writes=[('kn', i)])
                P.dma('pool', kt_dst[h, :, col0:col0 + nk], kn_bf[i][:, 0:nk], 'kn%d' % i, reads=[('kn', i)], writes=['kvscr'])
        nsub = (nk + 127) // 128
        for hp in range(0, AH, 4):
            w, wk = wpanel(wsrc(s_w_uv[la], 0, 4, hp * 128, 512), 4, 512)
            for sub in range(nsub):
                rows = min(128, nk - sub * 128)
                pb, pk = next_pf()

                def f(en, w=w, pb=pb, sub=sub, rows=rows):
                    ins = None
                    for k in range(4):
                        ins = en.matmul(pb[0:rows, :], ckvT[:, k, sub * 128:sub * 128 + rows], w[:, k, :],
                                        start=(k == 0), stop=(k == 3))
                    return ins
                P.op('pe', f, reads=[wk] + list(ckv_src_keys), writes=[pk])
                i = (sub + hp // 4) % 2
                P.op('dve', lambda en, pb=pb, i=i, rows=rows: en.tensor_copy(out=kn_bf[i][0:rows, :], in_=pb[0:rows, :]),
                     reads=[pk], writes=[('kn', i)])
                P.dma('pool', v_dst[hp:hp + 4, col0 + sub * 128:col0 + sub * 128 + rows, :].rearrange("h t v -> t h v"),
                      kn_bf[i][0:rows, :].rearrange("t (h v) -> t h v", v=128), 'kn%d' % i,
                      reads=[('kn', i)], writes=['kvscr'])

    def attend(ncols, qn_ap, qpe_ap, qkeys, kblocks, out_fn, outkeys, scale):
        o_ps, o_k = pf[4], ('pf', 4)
        s_ps, s_k = pf[5], ('pf', 5)
        first = True
        nb = len(kblocks)
        for bi, (kt_src, kpe_src, v_src, nk, diag) in enumerate(kblocks):
            i = kvstate['n'] % NKV
            kvstate['n'] += 1
            nch = (nk + 127) // 128
            P.dma('sp', kvK[i][:, 0:nk], kt_src, 'kvK%d' % i, reads=['kvscr'], writes=[('kv', i)])
            P.dma('sp', kvP[i][:, 0:nk], kpe_src, 'kvP%d' % i, reads=['kvscr'], writes=[('kv', i)])
            if nk % 128 == 0:
                P.dma('sp', kvV[i][:, 0:nch, :], v_src.rearrange("(c p) v -> p c v", p=128), 'kvV%d' % i,
                      reads=['kvscr'], writes=[('kv', i)])
            else:
                P.dma('sp', kvV[i][0:nk, 0, :], v_src, 'kvV%d' % i, reads=['kvscr'], writes=[('kv', i)])
            for c in range(nch):
                rows = min(128, nk - c * 128)
                q0 = c * 128 if diag else 0
                sp_, sk = next_pf(0, 4)

                def f(en, i=i, c=c, rows=rows, q0=q0, sp_=sp_):
                    en.matmul(sp_[0:rows, q0:ncols], kvK[i][:, c * 128:c * 128 + rows], qn_ap[:, q0:ncols], start=True, stop=False)
                    return en.matmul(sp_[0:rows, q0:ncols], kvP[i][:, c * 128:c * 128 + rows], qpe_ap[:, q0:ncols], start=False, stop=True)
                P.op('pe', f, reads=[('kv', i)] + list(qkeys), writes=[sk])
                pi = ptstate['n'] % 3
                ptstate['n'] += 1
                pt = PT[pi]
                P.op('act', lambda en, pt=pt, sp_=sp_, rows=rows, q0=q0: en.activation(
                    out=pt[0:rows, q0:ncols], in_=sp_[0:rows, q0:ncols], func=AF.Exp, scale=scale),
                    reads=[sk], writes=[('PT', pi)])
                if diag:
                    P.op('pool', lambda en, pt=pt, q0=q0: en.memset(pt[64:128, q0:q0 + 64], 0.0),
                         reads=[], writes=[('PT', pi)])

                lastmm = (bi == nb - 1 and c == nch - 1)

                def f2(en, i=i, c=c, rows=rows, q0=q0, pt=pt, first=first, lastmm=lastmm):
                    en.matmul(o_ps[:, q0:ncols], kvV[i][0:rows, c, :], pt[0:rows, q0:ncols], start=first, stop=lastmm,
                              skip_group_check=True)
                    return en.matmul(s_ps[:, q0:ncols], ones[0:rows, :], pt[0:rows, q0:ncols], start=first, stop=lastmm,
                                     skip_group_check=True)
                P.op('pe', f2, reads=[('kv', i), ('PT', pi), 'ones'], writes=[o_k, s_k])
                first = False
        P.op('dve', lambda en: en.reciprocal(out=recip[:, 0:ncols], in_=s_ps[:, 0:ncols]), reads=[s_k], writes=['recip'])
        out_fn(o_ps, o_k)

    def mla(layer, la, NS, NB, tile_idx, sample):
        TT = NS * 128
        tcol = tile_idx
        prenorm(layer * 4 + 0, NS)
        tsub = tile_idx * 4 if not sample else NT * 4
        nts = NS
        P.dma('pool', tabAc[:, 0:nts, :], tA_c[:, tsub:tsub + nts, :], 'tab', writes=['tab'])
        P.dma('pool', tabAs[:, 0:nts, :], tA_s[:, tsub:tsub + nts, :], 'tab', writes=['tab'])
        fc0 = tile_idx * T if not sample else SEQ
        P.dma('pool', tabFc[:, 0:TT], tF_c[:, fc0:fc0 + TT], 'tab', writes=['tab'])
        P.dma('pool', tabFs[:, 0:TT], tF_s[:, fc0:fc0 + TT], 'tab', writes=['tab'])
        ckpt(20)
        win = s_w_in_a[la]
        wq_, wqk = None, None
        for blk, (c0, ncol) in enumerate(((0, 512), (512, 512), (1024, 64))):
            w, wk = wpanel(wsrc(win, 0, 16, c0, ncol), 16, ncol)
            for sub in range(NS):
                pb, pk = next_pf(0, 4)

                def f(en, w=w, pb=pb, sub=sub, ncol=ncol):
                    ins = None
                    for k in range(16):
                        ins = en.matmul(pb[:, 0:ncol], hT[:, k, sub * 128:(sub + 1) * 128], w[:, k, :],
                                        start=(k == 0), stop=(k == 15))
                    return ins
                P.op('pe', f, reads=[wk, ('hT', sub)], writes=[pk])
                if blk < 2:
                    col = 4 + sub
                    sumsq(col, pb[:, 0:512], [pk], a_bf[:], 'a_bf', n=512)
                    rstd_from_ss(col, 512)
                    gsrc = gq_bc if blk == 0 else gkv_bc
                    if blk == 0:
                        P.op('dve', lambda en, pb=pb, col=col, gsrc=gsrc: en.scalar_tensor_tensor(
                            out=a_bf[:], in0=pb[:, 0:512], scalar=stat[:, col:col + 1], in1=gsrc[:, la, :],
                            op0=ALU.mult, op1=ALU.mult), reads=[pk, ('stat', col), 'gq'], writes=['a_bf'])
                        transpose_to(lambda cc, n, sub=sub: cqT[:, cc:cc + n, sub * 128:(sub + 1) * 128], a_bf[:], 4,
                                     ['a_bf'], ['cqT'])
                    else:
                        P.op('dve', lambda en, pb=pb, col=col, gsrc=gsrc: en.scalar_tensor_tensor(
                            out=a_sb[:], in0=pb[:, 0:512], scalar=stat[:, col:col + 1], in1=gsrc[:, la, :],
                            op0=ALU.mult, op1=ALU.mult), reads=[pk, ('stat', col), 'gq'], writes=['a_sb'])
                        if not sample:
                            r0 = tile_idx * T + sub * 128
                            P.dma('pool', o_ckv_p[la, r0:r0 + 128, :], a_sb[:], 'o_a', reads=['a_sb'], writes=['o_ckv'])
                        else:
                            P.dma('pool', o_ckv_s[la, :, :], a_sb[:], 'o_a', reads=['a_sb'], writes=['o_ckv'])
                        P.op('pool', lambda en: en.tensor_copy(out=a_bf[:], in_=a_sb[:]), reads=['a_sb'], writes=['a_bf'])
                        transpose_to(lambda cc, n, sub=sub: ckvT[:, cc:cc + n, sub * 128:(sub + 1) * 128], a_bf[:], 4,
                                     ['a_bf'], ['ckvT'])
                else:
                    P.op('dve', lambda en, pb=pb, sub=sub: en.tensor_tensor(out=kpe_f[:, 0:32], in0=pb[:, 0:32], in1=tabAc[:, sub, :], op=ALU.mult),
                         reads=[pk, 'tab'], writes=['kpe_f'])
                    P.op('dve', lambda en, pb=pb, sub=sub: en.tensor_tensor(out=kpe_t[:, 0:32], in0=pb[:, 32:64], in1=tabAs[:, sub, :], op=ALU.mult),
                         reads=[pk, 'tab'], writes=['kpe_t'])
                    P.op('dve', lambda en, pb=pb, sub=sub: en.tensor_tensor(out=kpe_f[:, 32:64], in0=pb[:, 0:32], in1=tabAs[:, sub, :], op=ALU.mult),
                         reads=[pk, 'tab'], writes=['kpe_f'])
                    P.op('dve', lambda en, pb=pb, sub=sub: en.tensor_tensor(out=kpe_t[:, 32:64], in0=pb[:, 32:64], in1=tabAc[:, sub, :], op=ALU.mult),
                         reads=[pk, 'tab'], writes=['kpe_t'])
                    P.op('dve', lambda en: en.tensor_tensor(out=kpe_f[:, 0:32], in0=kpe_f[:, 0:32], in1=kpe_t[:, 0:32], op=ALU.subtract),
                         reads=['kpe_f', 'kpe_t'], writes=['kpe_f'])
                    P.op('dve', lambda en: en.tensor_tensor(out=kpe_f[:, 32:64], in0=kpe_f[:, 32:64], in1=kpe_t[:, 32:64], op=ALU.add),
                         reads=['kpe_f', 'kpe_t'], writes=['kpe_f'])
                    if not sample:
                        r0 = tile_idx * T + sub * 128
                        P.dma('pool', o_kpe_p[la, r0:r0 + 128, :], kpe_f[:], 'o_k', reads=['kpe_f'], writes=['o_kpe'])
                    else:
                        P.dma('pool', o_kpe_s[la, :, :], kpe_f[:], 'o_k', reads=['kpe_f'], writes=['o_kpe'])
                    P.op('pool', lambda en: en.tensor_copy(out=kpe_bf[:], in_=kpe_f[:]), reads=['kpe_f'], writes=['kpe_bf'])
                    transpose_to(lambda cc, n, sub=sub: kpeT[:, sub * 128:(sub + 1) * 128].rearrange("p (c t) -> p c t", c=1),
                                 kpe_bf[:], 1, ['kpe_bf'], ['kpeT'], width=64)
        ckpt(21)
        if not sample:
            c0 = tile_idx * T
            make_kv(la, ['ckvT'], TT, kt_p[la], v_p[la], c0)
            P.dma('pool', kpe_p[la][:, c0:c0 + TT], kpeT[:, 0:TT], 'kpw', reads=['kpeT'], writes=['kvscr'])
        else:
            for b in range(4):
                pass
            make_kv_sample_new(la)
        ckpt(22)
        for h in range(AH):
            w, wk = wpanel(s_w_qb[la][:, h, :].rearrange("(k p) n -> p k n", p=128), 4, 256)
            pn, pnk = next_pf(0, 4)
            pa, pak = next_pf(0, 4)
            pb2, pbk = next_pf(0, 4)

            def f(en, w=w, pn=pn, pa=pa, pb2=pb2):
                ins = None
                for (pp, cc0, mm) in ((pn, 0, 128), (pa, 128, 64), (pb2, 192, 64)):
                    for k in range(4):
                        ins = en.matmul(pp[0:mm, 0:TT], w[:, k, cc0:cc0 + mm], cqT[:, k, 0:TT], start=(k == 0), stop=(k == 3))
                return ins
            P.op('pe', f, reads=[wk, 'cqT'], writes=[pnk, pak, pbk])
            P.op('act', lambda en, pn=pn: en.activation(out=qnT[:, 0:TT], in_=pn[:, 0:TT], func=AF.Identity), reads=[pnk], writes=['qnT'])
            P.op('dve', lambda en, pa=pa: en.tensor_tensor(out=rt0[:, 0:TT], in0=pa[0:64, 0:TT], in1=tabFc[:, 0:TT], op=ALU.mult),
                 reads=[pak, 'tab'], writes=['rt0'])
            P.op('dve', lambda en, pb2=pb2: en.tensor_tensor(out=rt1[:, 0:TT], in0=pb2[0:64, 0:TT], in1=tabFs[:, 0:TT], op=ALU.mult),
                 reads=[pbk, 'tab'], writes=['rt1'])
            P.op('pool', lambda en: en.tensor_tensor(out=qpeT[:, 0:TT], in0=rt0[:, 0:TT], in1=rt1[:, 0:TT], op=ALU.add),
                 reads=['rt0', 'rt1'], writes=['qpeT'])
            ckpt(23)
            if not sample:
                kb = []
                for j in range(tile_idx + 1):
                    kb.append((kt_p[la, h, :, j * T:(j + 1) * T], kpe_p[la, :, j * T:(j + 1) * T],
                               v_p[la, h, j * T:(j + 1) * T, :], T, j == tile_idx))

                def outf(o_ps, o_k, h=h):
                    P.op('dve', lambda en: en.tensor_tensor(out=oT[:, h, 0:TT], in0=o_ps[:, 0:TT], in1=recip[:, 0:TT], op=ALU.mult),
                         reads=[o_k, 'recip'], writes=['oT'])
                attend(TT, qnT, qpeT, ['qnT', 'qpeT'], kb, outf, ['oT'], A_SCALE)
                ckpt(24)
            else:
                for b in range(4):
                    kb = []
                    for j in range(0, KLS, T):
                        nk = min(T, KLS - j)
                        kb.append((kt_s[la, b, h, :, j:j + nk], kpe_s[la, b, :, j:j + nk], v_s[la, b, h, j:j + nk, :], nk, False))

                    def outf(o_ps, o_k, h=h, b=b):
                        P.op('dve', lambda en: en.tensor_tensor(out=oT[:, h, b * 32:(b + 1) * 32], in0=o_ps[:, 0:32], in1=recip[:, 0:32], op=ALU.mult),
                             reads=[o_k, 'recip'], writes=['oT'])
                    attend(32, qnT[:, b * 32:(b + 1) * 32], qpeT[:, b * 32:(b + 1) * 32], ['qnT', 'qpeT'], kb, outf, ['oT'], A_SCALE)
        ckpt(25)
        wo = s_w_o_a[la]
        panels = [[(wsrc(wo, 0, 16, n * 512, 512), 16, 0)] for n in range(4)]
        down_proj(panels, lambda k, sub: oT[:, k, sub * 128:(sub + 1) * 128], ['oT'], NS)
        postnorm_residual(layer * 4 + 1, NS)

    def make_kv_sample_new(la):
        for hp in range(0, AH, 4):
            w, wk = wpanel(wsrc(s_w_uk[la], 0, 4, hp * 128, 512), 4, 512)
            for m in range(4):
                h = hp + m
                pb, pk = next_pf(0, 4)

                def f(en, w=w, pb=pb, m=m):
                    ins = None
                    for k in range(4):
                        ins = en.matmul(pb[:, 0:128], w[:, k, m * 128:(m + 1) * 128], ckvT[:, k, 0:128], start=(k == 0), stop=(k == 3))
                    return ins
                P.op('pe', f, reads=[wk, 'ckvT'], writes=[pk])
                i = h % 2
                P.op('act', lambda en, pb=pb, i=i: en.activation(out=kn_bf[i][:, 0:128], in_=pb[:, 0:128], func=AF.Identity),
                     reads=[pk], writes=[('kn', i)])
                P.dma('pool', kt_s[la, :, h, :, PAST:PAST + 32].rearrange("b p t -> p b t"),
                      kn_bf[i][:, 0:128].rearrange("p (b t) -> p b t", t=32), 'kn%d' % i, reads=[('kn', i)], writes=['kvscr'])
        for hp in range(0, AH, 4):
            w, wk = wpanel(wsrc(s_w_uv[la], 0, 4, hp * 128, 512), 4, 512)
            pb, pk = next_pf(0, 4)

            def f(en, w=w, pb=pb):
                ins = None
                for k in range(4):
                    ins = en.matmul(pb[:, :], ckvT[:, k, 0:128], w[:, k, :], start=(k == 0), stop=(k == 3))
                return ins
            P.op('pe', f, reads=[wk, 'ckvT'], writes=[pk])
            i = (hp // 4) % 2
            P.op('dve', lambda en, pb=pb, i=i: en.tensor_copy(out=kn_bf[i][:, :], in_=pb[:, :]), reads=[pk], writes=[('kn', i)])
            for b in range(4):
                P.dma('pool', v_s[la, b, hp:hp + 4, PAST:PAST + 32, :].rearrange("h t v -> t h v"),
                      kn_bf[i][b * 32:(b + 1) * 32, :].rearrange("t (h v) -> t h v", v=128), 'kn%d' % i,
                      reads=[('kn', i)], writes=['kvscr'])
        P.dma('pool', kpe_s[la, :, :, PAST:PAST + 32].rearrange("b p t -> p b t"),
              kpeT[:, 0:128].rearrange("p (b t) -> p b t", t=32), 'kpw', reads=['kpeT'], writes=['kvscr'])

    def sample_past_kv(la):
        for b in range(4):
            for j in range(0, PAST, T):
                for sub in range(4):
                    r0 = j + sub * 128
                    P.dma('pool', a_sb[:], cckv[la, b, r0:r0 + 128, :], 'ld_a', writes=['a_sb'])
                    P.op('pool', lambda en: en.tensor_copy(out=a_bf[:], in_=a_sb[:]), reads=['a_sb'], writes=['a_bf'])
                    transpose_to(lambda cc, n, sub=sub: ckvT[:, cc:cc + n, sub * 128:(sub + 1) * 128], a_bf[:], 4,
                                 ['a_bf'], ['ckvT'])
                    P.dma('pool', kpe_f[:], ckpe[la, b, r0:r0 + 128, :], 'ld_k', writes=['kpe_f'])
                    P.op('pool', lambda en: en.tensor_copy(out=kpe_bf[:], in_=kpe_f[:]), reads=['kpe_f'], writes=['kpe_bf'])
                    transpose_to(lambda cc, n, sub=sub: kpeT[:, sub * 128:(sub + 1) * 128].rearrange("p (c t) -> p c t", c=1),
                                 kpe_bf[:], 1, ['kpe_bf'], ['kpeT'], width=64)
                make_kv(la, ['ckvT'], T, kt_s[la, b], v_s[la, b], j)
                P.dma('pool', kpe_s[la, b][:, j:j + T], kpeT[:, 0:T], 'kpw', reads=['kpeT'], writes=['kvscr'])

    o = R0
    class _QB:
        def __getitem__(self, idx):
            p, sub, cols = idx
            return y_sb[:, sub, :].bitcast(BF16)[:, cols]
    q_bf = _QB()
    kv_f = ralloc("kv_f", [128, 1024], F32)
    kv_bf = ralloc("kv_bf", [128, 1024], BF16)
    rr = [ralloc("rr%d" % i, [128, 8, 8], F32) for i in range(4)]
    qT = ralloc("qT", [64, 8, 512], BF16)
    kTc = ralloc("kTc", [64, 8, 128], BF16)
    sPT = [ralloc("sPT%d" % i, [128, 512], BF16) for i in range(2)]
    sden = ralloc("sden", [64, 512], F32)
    oTs = ralloc("oTs", [64, BH, T], BF16)
    SWA_END = o
    SWA_KEYS = ['kv_f', 'kv_bf', 'rr', 'qT', 'kTc', ('sPT', 0), ('sPT', 1), 'sden', 'oTs']
    print("SWA region end", SWA_END)

    def rope_tm(pb, pk, nh, sub, dst, dstkey, dcol0):
        src = pb[:, 0:nh * 64].rearrange("p (h d) -> p h d", d=64)
        d3 = dst[:, dcol0:dcol0 + nh * 64].rearrange("p (h d) -> p h d", d=64)
        cb = tabBc[:, sub, :].rearrange("p (o d) -> p o d", o=1).to_broadcast([128, nh, 8])
        sbb = tabBs[:, sub, :].rearrange("p (o d) -> p o d", o=1).to_broadcast([128, nh, 8])
        r = [rr[i][:, 0:nh, :] for i in range(4)]
        P.op('dve', lambda en: en.tensor_tensor(out=r[0], in0=src[:, :, 0:8], in1=cb, op=ALU.mult), reads=[pk, 'tab', dstkey], writes=['rr'])
        P.op('dve', lambda en: en.tensor_tensor(out=r[1], in0=src[:, :, 8:16], in1=sbb, op=ALU.mult), reads=[pk, 'tab', dstkey], writes=['rr'])
        P.op('dve', lambda en: en.tensor_tensor(out=r[2], in0=src[:, :, 0:8], in1=sbb, op=ALU.mult), reads=[pk, 'tab', dstkey], writes=['rr'])
        P.op('dve', lambda en: en.tensor_tensor(out=r[3], in0=src[:, :, 8:16], in1=cb, op=ALU.mult), reads=[pk, 'tab', dstkey], writes=['rr'])
        P.op('dve', lambda en: en.tensor_tensor(out=d3[:, :, 0:8], in0=r[0], in1=r[1], op=ALU.subtract), reads=['rr'], writes=[dstkey])
        P.op('dve', lambda en: en.tensor_tensor(out=d3[:, :, 8:16], in0=r[2], in1=r[3], op=ALU.add), reads=['rr'], writes=[dstkey])

    def swa(layer, lb, NS, tile_idx, sample):
        ckpt(30)
        prenorm(layer * 4 + 0, NS)
        tsub = tile_idx * 4 if not sample else NT * 4
        P.dma('pool', tabBc[:, 0:NS, :], tB_c[:, tsub:tsub + NS, :], 'tab', writes=['tab'])
        P.dma('pool', tabBs[:, 0:NS, :], tB_s[:, tsub:tsub + NS, :], 'tab', writes=['tab'])
        wq = s_w_qkv[lb]
        for n in range(4):
            w, wk = wpanel(wsrc(wq, 0, 16, n * 512, 512), 16, 512)
            for sub in range(NS):
                pb, pk = next_pf()

                def f(en, w=w, pb=pb, sub=sub):
                    ins = None
                    for k in range(16):
                        ins = en.matmul(pb[:, :], hT[:, k, sub * 128:(sub + 1) * 128], w[:, k, :], start=(k == 0), stop=(k == 15))
                    return ins
                P.op('pe', f, reads=[wk, ('hT', sub)], writes=[pk])
                P.op('act', lambda en, pb=pb, sub=sub, n=n: en.activation(out=q_bf[:, sub, n * 512:(n + 1) * 512], in_=pb[:, :], func=AF.Identity),
                     reads=[pk], writes=[('y', sub)])
                rope_tm(pb, pk, 8, sub, q_bf[:, sub, :], ('y', sub), n * 512)
        ckpt(31)
        for sub in range(NS):
            wkp, wkk = wpanel(wsrc(wq, 0, 16, 2048, 512), 16, 512)
            pbk_, pkk = next_pf()

            def f(en, w=wkp, pb=pbk_, sub=sub):
                ins = None
                for k in range(16):
                    ins = en.matmul(pb[:, :], hT[:, k, sub * 128:(sub + 1) * 128], w[:, k, :], start=(k == 0), stop=(k == 15))
                return ins
            P.op('pe', f, reads=[wkk, ('hT', sub)], writes=[pkk])
            wvp, wvk = wpanel(wsrc(wq, 0, 16, 2560, 512), 16, 512)
            pbv, pkv = next_pf()

            def f2(en, w=wvp, pb=pbv, sub=sub):
                ins = None
                for k in range(16):
                    ins = en.matmul(pb[:, :], hT[:, k, sub * 128:(sub + 1) * 128], w[:, k, :], start=(k == 0), stop=(k == 15))
                return ins
            P.op('pe', f2, reads=[wvk, ('hT', sub)], writes=[pkv])
            P.op('act', lambda en, pb=pbk_: en.activation(out=kv_f[:, 0:512], in_=pb[:, :], func=AF.Identity), reads=[pkk], writes=['kv_f'])
            rope_tm(pbk_, pkk, 8, sub, kv_f[:, :], 'kv_f', 0)
            P.op('act', lambda en, pb=pbv: en.activation(out=kv_f[:, 512:1024], in_=pb[:, :], func=AF.Identity), reads=[pkv], writes=['kv_f'])
            P.op('pool', lambda en: en.tensor_copy(out=kv_bf[:], in_=kv_f[:]), reads=['kv_f'], writes=['kv_bf'])
            transpose_to(lambda cc, n: kTc[:, cc:cc + n, :], kv_bf[:, 0:512], 8, ['kv_bf'], ['kTc'], width=64)
            transpose_to(lambda cc, n: qT[:, cc // 4:(cc + n) // 4, :].rearrange("p g (h t) -> p (g h) t", t=128),
                         q_bf[:, sub, :], 32, [('y', sub)], ['qT'], width=64)
            ckpt(32)
            if not sample:
                last = (tile_idx == NT - 1 and sub == NS - 1)
                if last:
                    P.dma('pool', o_wk_p[lb], kv_f[:, 0:512], 'o_w', reads=['kv_f'], writes=['o_wk'])
                    P.dma('pool', o_wv_p[lb], kv_f[:, 512:1024], 'o_w', reads=['kv_f'], writes=['o_wk'])
                has_prev = not (tile_idx == 0 and sub == 0)
                for g in range(BKV):
                    o_ps, o_k = next_pf()
                    d_ps, d_k = next_pf()
                    blocks = []
                    if has_prev:
                        blocks.append((kTprev[:, lb, g, :], vprev[:, lb, g * 64:(g + 1) * 64], ['kprev'], True))
                    blocks.append((kTc[:, g, :], kv_bf[:, 512 + g * 64:512 + (g + 1) * 64], ['kTc', 'kv_bf'], False))
                    for bi, (kt_ap, v_ap, kkeys, isprev) in enumerate(blocks):
                        sp_, sk = next_pf()
                        P.op('pe', lambda en, sp_=sp_, kt_ap=kt_ap, g=g: en.matmul(sp_[:, :], kt_ap, qT[:, g, :], start=True, stop=True),
                             reads=kkeys + ['qT'], writes=[sk])
                        pi = ptstate['n'] % 2
                        ptstate['n'] += 1
                        pt = sPT[pi]
                        P.op('act', lambda en, pt=pt, sp_=sp_: en.activation(out=pt[:, :], in_=sp_[:, :], func=AF.Exp, scale=B_SCALE),
                             reads=[sk], writes=[('sPT', pi)])
                        pt3 = pt[:, :].rearrange("p (h t) -> p h t", t=128)
                        if isprev:
                            P.op('pool', lambda en, pt3=pt3: en.memset(pt3[0:64, :, 64:128], 0.0), writes=[('sPT', pi)])
                        else:
                            P.op('pool', lambda en, pt3=pt3: en.memset(pt3[64:128, :, 0:64], 0.0), writes=[('sPT', pi)])

                        def f3(en, pt=pt, v_ap=v_ap, bi=bi, nb=len(blocks), o_ps=o_ps, d_ps=d_ps):
                            en.matmul(o_ps[0:64, :], v_ap, pt[:, :], start=(bi == 0), stop=(bi == nb - 1), skip_group_check=True)
                            return en.matmul(d_ps[0:64, :], ones[:, 0:64], pt[:, :], start=(bi == 0), stop=(bi == nb - 1), skip_group_check=True)
                        P.op('pe', f3, reads=kkeys + [('sPT', pi), 'ones'], writes=[o_k, d_k])
                    swa_finish(lb, g, o_ps, o_k, d_ps, d_k, 128, lambda hh, sub=sub: oTs[:, hh, sub * 128:(sub + 1) * 128])
                P.op('pool', lambda en: en.tensor_copy(out=kTprev[:, lb], in_=kTc[:]), reads=['kTc'], writes=['kprev'])
                P.op('pool', lambda en: en.tensor_copy(out=vprev[:, lb, :], in_=kv_bf[:, 512:1024]), reads=['kv_bf'], writes=['kprev'])
            else:
                for b in range(4):
                    P.dma('pool', cw_f[:], cwk[lb, b], 'ld_c', writes=['cw_f'])
                    P.dma('pool', o_wk_s[lb, b, 0:96, :], cwk[lb, b, 32:128, :], 'o_w', writes=['o_wk'])
                    P.dma('pool', o_wv_s[lb, b, 0:96, :], cwv[lb, b, 32:128, :], 'o_w', writes=['o_wk'])
                    P.dma('pool', o_wk_s[lb, b, 96:128, :], kv_f[b * 32:(b + 1) * 32, 0:512], 'o_w', reads=['kv_f'], writes=['o_wk'])
                    P.dma('pool', o_wv_s[lb, b, 96:128, :], kv_f[b * 32:(b + 1) * 32, 512:1024], 'o_w', reads=['kv_f'], writes=['o_wk'])
                    P.op('pool', lambda en: en.tensor_copy(out=vpast[:], in_=cw_f[:]), reads=['cw_f'], writes=['vpast'])
                    transpose_to(lambda cc, n: kTpast[:, cc:cc + n, :], vpast[:], 8, ['vpast'], ['kTpast'], width=64)
                    P.dma('pool', cw_f[:], cwv[lb, b], 'ld_c', reads=['vpast'], writes=['cw_f'])
                    P.op('pool', lambda en: en.tensor_copy(out=vpast[:], in_=cw_f[:]), reads=['cw_f', 'kTpast'], writes=['vpast'])
                    for g in range(BKV):
                        o_ps, o_k = next_pf()
                        d_ps, d_k = next_pf()
                        qv = qT[:, g, :].rearrange("p (h t) -> p h t", t=128)[:, :, b * 32:(b + 1) * 32]
                        blocks = [(kTpast[:, g, :], vpast[:, g * 64:(g + 1) * 64], 128, ['kTpast', 'vpast']),
                                  (kTc[:, g, :], kv_bf[:, 512 + g * 64:512 + (g + 1) * 64], 128, ['kTc', 'kv_bf'])]
                        for bi, (kt_ap, v_ap, nk, kkeys) in enumerate(blocks):
                            sp_, sk = next_pf()
                            s3 = sp_[0:nk, 0:128].rearrange("p (h t) -> p h t", t=32)
                            P.op('pe', lambda en, s3=s3, kt_ap=kt_ap, qv=qv: en.matmul(s3, kt_ap, qv, start=True, stop=True),
                                 reads=kkeys + ['qT'], writes=[sk])
                            pi = ptstate['n'] % 2
                            ptstate['n'] += 1
                            pt = sPT[pi]
                            P.op('act', lambda en, pt=pt, sp_=sp_, nk=nk: en.activation(out=pt[0:nk, 0:128], in_=sp_[0:nk, 0:128], func=AF.Exp, scale=B_SCALE),
                                 reads=[sk], writes=[('sPT', pi)])
                            if bi == 1:
                                for ob in range(4):
                                    if ob != b:
                                        P.op('pool', lambda en, pt=pt, ob=ob: en.memset(pt[ob * 32:(ob + 1) * 32, 0:128], 0.0), writes=[('sPT', pi)])

                            def f3(en, pt=pt, v_ap=v_ap, bi=bi, nk=nk, o_ps=o_ps, d_ps=d_ps):
                                en.matmul(o_ps[0:64, 0:128], v_ap, pt[0:nk, 0:128], start=(bi == 0), stop=(bi == 1), skip_group_check=True)
                                return en.matmul(d_ps[0:64, 0:128], ones[0:nk, 0:64], pt[0:nk, 0:128], start=(bi == 0), stop=(bi == 1), skip_group_check=True)
                            P.op('pe', f3, reads=kkeys + [('sPT', pi), 'ones'], writes=[o_k, d_k])
                        swa_finish(lb, g, o_ps, o_k, d_ps, d_k, 32, lambda hh, b=b: oTs[:, hh, b * 32:(b + 1) * 32])
        ckpt(33)
        wo = s_w_o_b[lb]
        for n in range(4):
            banks = [next_pf() for _ in range(NS)]
            for hq in range(0, BH, 8):
                src3 = wo[hq * 64:(hq + 8) * 64, n * 512:(n + 1) * 512].rearrange("(k p) n -> p k n", p=64)
                i = wstate['n'] % NWB
                wstate['n'] += 1
                view = wring[i][0:64, 0:8 * 512].rearrange("p (k n) -> p k n", n=512)
                P.dma('sp', view, src3, 'w%d' % i, writes=[('w', i)])
                for sub in range(NS):
                    pb, pk = banks[sub]

                    def f(en, view=view, hq=hq, sub=sub, pb=pb):
                        ins = None
                        for k in range(8):
                            ins = en.matmul(pb[:, :], oTs[:, hq + k, sub * 128:(sub + 1) * 128], view[:, k, :],
                                            start=(hq + k == 0), stop=(hq + k == BH - 1))
                        return ins
                    P.op('pe', f, reads=[('w', i), 'oTs'], writes=[pk])
            for sub in range(NS):
                pb, pk = banks[sub]
                P.op('act', lambda en, pb=pb, sub=sub, n=n: en.activation(out=y_sb[:, sub, n * 512:(n + 1) * 512], in_=pb[:, :], func=AF.Identity),
                     reads=[pk], writes=[('y', sub)])
        postnorm_residual(layer * 4 + 1, NS)

    def swa_finish(lb, g, o_ps, o_k, d_ps, d_k, nq, dst_fn):
        ncol = 4 * nq
        d3 = sden[:, 0:ncol].rearrange("p (h t) -> p h t", t=nq)
        es = esink[:, lb, g * 4:(g + 1) * 4].rearrange("p (h o) -> p h o", o=1).to_broadcast([64, 4, nq])
        P.op('dve', lambda en: en.tensor_tensor(out=d3, in0=d_ps[0:64, 0:ncol].rearrange("p (h t) -> p h t", t=nq), in1=es, op=ALU.add),
             reads=[d_k, 'esink'], writes=['sden'])
        P.op('dve', lambda en: en.reciprocal(out=sden[:, 0:ncol], in_=sden[:, 0:ncol]), reads=['sden'], writes=['sden'])
        for hh in range(4):
            P.op('dve', lambda en, hh=hh: en.tensor_tensor(out=dst_fn(g * 4 + hh), in0=o_ps[0:64, hh * nq:(hh + 1) * nq],
                                                            in1=sden[:, hh * nq:(hh + 1) * nq], op=ALU.mult),
                 reads=[o_k, 'sden'], writes=['oTs'])

    def ckpt(n):
        if STAGE == n:
            raise StopBuild()
    try:
        cur = {'keys': []}

        def phase(newkeys):
            P.fence(cur['keys'], newkeys)
            cur['keys'] = newkeys

        def fc_out(layer, sample):
            def f():
                if not sample:
                    for j in range(2):
                        P.dma('pool', o_fc_p[layer, j].rearrange("(c p) -> p c", p=128), cst_p[:, layer, :, j], 'o_fc',
                              reads=['cst_p'], writes=['o_fc'], slow=True)
                else:
                    for b in range(4):
                        for j in range(2):
                            P.dma('pool', o_fc_s[layer, b, j].rearrange("(c p) -> p c", p=128), cst_s[:, layer, b, :, j], 'o_fc',
                                  reads=['cst_s'], writes=['o_fc'], slow=True)
            return f

        def wscr_guard():
            pass
        P.fence(['wscr'], [('w', i) for i in range(NWB)])

        if STAGE == 0:
            P.emit(); return nc
        phase(MLA_KEYS)
        for la in range(2):
            sample_past_kv(la)
        if STAGE == 1:
            P.emit(); return nc

        for t in range(NT):
            for sub in range(4):
                P.dma('pool', x_sb[:, sub, :], xp[t * T + sub * 128:t * T + (sub + 1) * 128, :], 'ldx', writes=[('x', sub)])
            for layer in range(DEPTH):
                if layer % 2 == 0:
                    phase(MLA_KEYS)
                    mla(layer, layer // 2, 4, 1, t, False)
                else:
                    phase(SWA_KEYS)
                    swa(layer, layer // 2, 4, t, False)
                if STAGE == 2 + layer * 2 and t == 0:
                    P.emit(); return nc
                phase(FFN_KEYS)
                ffn(layer, 4, 1, lambda ca, layer=layer: cst_p[:, layer, ca, :].rearrange("p (b j) -> p b j", b=1), 'cst_p',
                    fc_out(layer, False) if t == NT - 1 else None)
            for sub in range(4):
                P.dma('pool', yp[t * T + sub * 128:t * T + (sub + 1) * 128, :], x_sb[:, sub, :], 'stx', reads=[('x', sub)], writes=['yp'])
        P.fence([('y', 1), ('y', 2), ('y', 3)], ['kTpast', 'vpast', 'cw_f'])
        P.dma('pool', x_sb[:, 0, :], xs[:, :], 'ldx', writes=[('x', 0)])
        for layer in range(DEPTH):
            if layer % 2 == 0:
                phase(MLA_KEYS)
                mla(layer, layer // 2, 1, 4, 0, True)
            else:
                phase(SWA_KEYS)
                swa(layer, layer // 2, 1, 0, True)
            phase(FFN_KEYS)
            ffn(layer, 1, 4, lambda ca, layer=layer: cst_s[:, layer, :, ca, :], 'cst_s', fc_out(layer, True))
        P.dma('pool', ys[:, :], x_sb[:, 0, :], 'stx', reads=[('x', 0)], writes=['ys'])
    except StopBuild:
        pass
    P.emit()
    return nc


def rope_tables(SEQ, PAST):
    NT = SEQ // T
    NTS = NT * 4 + 1
    pos_tm = np.zeros((128, NTS), np.float32)
    p = np.arange(128)
    for j in range(NT * 4):
        pos_tm[:, j] = j * 128 + p
    pos_tm[:, NT * 4] = PAST + (p % 32)

    def tabs(half):
        inv = (np.float32(THETA) ** (-np.arange(half, dtype=np.float32) / np.float32(half))).astype(np.float32)
        ang = pos_tm[:, :, None].astype(np.float32) * inv[None, None, :]
        return np.cos(ang).astype(np.float32), np.sin(ang).astype(np.float32)
    tA_c, tA_s = tabs(32)
    tB_c, tB_s = tabs(8)
    pos_f = np.concatenate([np.arange(SEQ), PAST + (np.arange(128) % 32)]).astype(np.float32)
    inv = (np.float32(THETA) ** (-np.arange(32, dtype=np.float32) / np.float32(32))).astype(np.float32)
    ang = pos_f[None, :] * inv[:, None]
    c = np.cos(ang).astype(np.float32)
    s_ = np.sin(ang).astype(np.float32)
    tF_c = np.concatenate([c, c], axis=0)
    tF_s = np.concatenate([-s_, s_], axis=0)
    return dict(tA_c=tA_c, tA_s=tA_s, tB_c=tB_c, tB_s=tB_s, tF_c=np.ascontiguousarray(tF_c), tF_s=np.ascontiguousarray(tF_s))


def run(inputs, SEQ, PAST, STAGE=99, ncores=NCORES):
    f = lambda a: np.ascontiguousarray(np.asarray(a, dtype=np.float32))
    I = {k: np.asarray(v) for k, v in inputs.items()}
    nc = build(SEQ, PAST, STAGE)
    tabs = rope_tables(SEQ, PAST)
    shared = dict(
        norm_g=f(I['norm_g'].reshape(16, D)), w_in_a=f(I['mla_w_in']), g_q=f(I['mla_g_q']), w_qb=f(I['mla_w_qb']),
        g_kv=f(I['mla_g_kv']), w_uk=f(I['mla_w_uk'].reshape(2, KVL, 2048)), w_uv=f(I['mla_w_uv'].reshape(2, KVL, 2048)),
        w_o_a=f(I['mla_w_o']), w_qkv=f(I['swa_w_qkv']), sinks=f(I['swa_sinks']), w_o_b=f(I['swa_w_o']),
        f_w_in=f(I['ffn_w_in']), f_cw=f(I['ffn_conv_w']), f_cb=f(I['ffn_conv_b']), f_wd=f(I['ffn_w_down']),
        ident=np.eye(128, dtype=np.float32), **tabs)
    in_maps = []
    for c in range(ncores):
        b = c % 4
        sb_ = slice(4 * b, 4 * b + 4)
        m = dict(shared)
        m.update(
            xp=f(I['x_prompt'][b]), xs=f(I['x_sample'][sb_].reshape(128, D)),
            cckv=f(I['cache_ckv'][:, sb_]), ckpe=f(I['cache_kpe'][:, sb_]),
            cwk=f(I['cache_win_k'][:, sb_].reshape(2, 4, 128, 512)), cwv=f(I['cache_win_v'][:, sb_].reshape(2, 4, 128, 512)),
            sfc=f(I['state_ffn_conv'][:, sb_]))
        in_maps.append(m)
    res = run_bass_kernel_spmd(nc, in_maps, core_ids=list(range(ncores)))
    R = res.results
    if ncores < 4:
        R = [R[0]] * 4
    st = lambda name, ax=0: np.stack([R[c][name] for c in range(4)], axis=ax)
    y_prompt = st('yp')
    y_sample = st('ys').reshape(16, 32, D)
    ckv_p = st('o_ckv_p', 1)
    kpe_p = st('o_kpe_p', 1)
    wk_p = st('o_wk_p', 1).reshape(2, 4, 128, BKV, BHD)
    wv_p = st('o_wv_p', 1).reshape(2, 4, 128, BKV, BHD)
    fc_p = st('o_fc_p', 1)
    ckv_s = st('o_ckv_s', 1).reshape(2, 16, 32, KVL)
    kpe_s = st('o_kpe_s', 1).reshape(2, 16, 32, ROPE)
    wk_s = np.concatenate([R[c]['o_wk_s'] for c in range(4)], axis=1).reshape(2, 16, 128, BKV, BHD)
    wv_s = np.concatenate([R[c]['o_wv_s'] for c in range(4)], axis=1).reshape(2, 16, 128, BKV, BHD)
    fc_s = np.concatenate([R[c]['o_fc_s'] for c in range(4)], axis=1)
    return (y_prompt, y_sample, ckv_p, kpe_p, wk_p, wv_p, fc_p, ckv_s, kpe_s, wk_s, wv_s, fc_s)


def kernel(**inputs):
    return run(inputs, 8192, 4096)
```
